# Optimizing a Trainium2 kernel written in Bass

```python
import math
import jax, jax.numpy as jnp
from jax import lax
import numpy as np


D_MODEL = 1024
BATCH = 16
SEQ = 4096
DEPTH = 2
DEC_BATCH = 16
DEC_SEQ = 16
PAST_LEN = 1024

CHUNK = 64
EPS = 1e-6
D_FF = 2816
GLA_HEADS = 4
GLA_DK = 64
GLA_DV = 128
GLA_RANK = 16
GLA_TAU = 16.0
SSD_HEADS = 8
SSD_HEAD_DIM = 64
SSD_GROUPS = 2
SSD_STATE = 64
SSD_CONV = 4
SSD_INNER = SSD_HEADS * SSD_HEAD_DIM
SSD_CONV_DIM = SSD_INNER + 2 * SSD_GROUPS * SSD_STATE
RET_HEADS = 4
RET_DK = 64
RET_DV = 128
ROPE_BASE = 10000.0
SPLIT_SIZES = (GLA_HEADS * GLA_DK, GLA_HEADS * GLA_DK, GLA_HEADS * GLA_DV, GLA_HEADS * GLA_DV, GLA_RANK,
               SSD_INNER, SSD_CONV_DIM, SSD_HEADS,
               RET_HEADS * RET_DK, RET_HEADS * RET_DK, RET_HEADS * RET_DV, RET_HEADS * RET_DV,
               D_MODEL, D_MODEL, D_MODEL)
IN_COLS = sum(SPLIT_SIZES)

kernel_name = 'hybrid_gla_ssd_retnet_stream_step'


def split_points():
    pts, acc = [], 0
    for s in SPLIT_SIZES[:-1]:
        acc += s
        pts.append(acc)
    return pts


def rmsnorm(x, g):
    xf = x.astype(jnp.float32)
    y = xf * lax.rsqrt(jnp.mean(xf * xf, axis=-1, keepdims=True) + EPS)
    return (y * g.astype(jnp.float32)).astype(x.dtype)


def group_norm(o, g):
    mu = jnp.mean(o, axis=-1, keepdims=True)
    var = jnp.mean(jnp.square(o - mu), axis=-1, keepdims=True)
    y = (o - mu) * lax.rsqrt(var + EPS)
    B, T = o.shape[:2]
    return y.reshape(B, T, -1) * g.astype(jnp.float32)


def swiglu(x, w_in, w_out):
    gate, up = jnp.split(x @ w_in, 2, axis=-1)
    return (jax.nn.silu(gate) * up) @ w_out


def rope(x, pos):
    half = x.shape[-1] // 2
    freqs = ROPE_BASE ** (-jnp.arange(half, dtype=jnp.float32) / half)
    ang = pos.astype(jnp.float32)[:, None] * freqs[None, :]
    cos = jnp.cos(ang)[None, :, None, :]
    sin = jnp.sin(ang)[None, :, None, :]
    xf = x.astype(jnp.float32)
    x1, x2 = xf[..., :half], xf[..., half:]
    return jnp.concatenate([x1 * cos - x2 * sin, x1 * sin + x2 * cos], axis=-1)


def chunk_len(T):
    return CHUNK if T % CHUNK == 0 else T


def to_chunks(a, L):
    B, T = a.shape[:2]
    return jnp.moveaxis(a.reshape((B, T // L, L) + a.shape[2:]), 1, 0)


def from_chunks(a):
    a = jnp.moveaxis(a, 0, 1)
    return a.reshape((a.shape[0], -1) + a.shape[3:])


def scan_vector_decay(q, k, v, log_a, s0):
    L = chunk_len(q.shape[1])
    mask = jnp.tril(jnp.ones((L, L), dtype=bool))

    def step(S, inp):
        qc, kc, vc, ac = inp
        b = jnp.cumsum(ac, axis=1)
        bL = b[:, -1]
        qd = qc * jnp.exp(b)
        kd = kc * jnp.exp(-b)
        scores = jnp.where(mask, jnp.einsum('blhn,bshn->bhls', qd, kd), 0.0)
        o = jnp.einsum('bhls,bshp->blhp', scores, vc) + jnp.einsum('blhn,bhnp->blhp', qd, S)
        ks = kc * jnp.exp(bL[:, None] - b)
        S = jnp.exp(bL)[..., None] * S + jnp.einsum('bshn,bshp->bhnp', ks, vc)
        return S, o

    xs = tuple(to_chunks(t.astype(jnp.float32), L) for t in (q, k, v, log_a))
    S, o = lax.scan(step, s0.astype(jnp.float32), xs)
    return from_chunks(o), S


def scan_scalar_decay(q, k, v, log_a, s0):
    L = chunk_len(q.shape[1])
    mask = jnp.tril(jnp.ones((L, L), dtype=bool))[None, :, :, None]

    def step(S, inp):
        qc, kc, vc, ac = inp
        b = jnp.cumsum(ac, axis=1)
        bL = b[:, -1]
        seg = b[:, :, None, :] - b[:, None, :, :]
        decay = jnp.exp(jnp.where(mask, seg, -jnp.inf))
        scores = jnp.einsum('blhn,bshn->blsh', qc, kc) * decay
        o = (jnp.einsum('blsh,bshp->blhp', scores, vc)
             + jnp.einsum('blhn,bhnp->blhp', qc * jnp.exp(b)[..., None], S))
        ks = kc * jnp.exp(bL[:, None] - b)[..., None]
        S = jnp.exp(bL)[:, :, None, None] * S + jnp.einsum('bshn,bshp->bhnp', ks, vc)
        return S, o

    xs = tuple(to_chunks(t.astype(jnp.float32), L) for t in (q, k, v, log_a))
    S, o = lax.scan(step, s0.astype(jnp.float32), xs)
    return from_chunks(o), S


def token_mixers(h, pos, s_gla, s_ssd, conv_buf, s_ret, w_in, gla_w_gate2, gla_b_gate, gla_norm,
                 ssd_conv_w, ssd_conv_b, ssd_dt_bias, ssd_a_log, ssd_d, ssd_norm, ret_norm,
                 w_branch_gla, w_branch_ssd, w_branch_ret, w_out):
    B, T, _ = h.shape
    dt_ = h.dtype
    proj = h @ w_in
    (g_q, g_k, g_v, g_r, g_lr, m_z, m_xbc, m_dt,
     r_q, r_k, r_v, r_g, gate_a, gate_b, gate_c) = jnp.split(proj, split_points(), axis=-1)

    q = g_q.reshape(B, T, GLA_HEADS, GLA_DK) * (GLA_DK ** -0.5)
    k = g_k.reshape(B, T, GLA_HEADS, GLA_DK)
    v = g_v.reshape(B, T, GLA_HEADS, GLA_DV)
    log_a = jax.nn.log_sigmoid((g_lr @ gla_w_gate2 + gla_b_gate).astype(jnp.float32)) / GLA_TAU
    log_a = log_a.reshape(B, T, GLA_HEADS, GLA_DK)
    o_gla, s_gla_new = scan_vector_decay(q, k, v, log_a, s_gla)
    y_gla = (group_norm(o_gla, gla_norm) * jax.nn.silu(g_r.astype(jnp.float32))).astype(dt_)

    xpad = jnp.concatenate([conv_buf.astype(dt_), m_xbc], axis=1)
    conv = ssd_conv_b
    for j in range(SSD_CONV):
        conv = conv + xpad[:, j:j + T] * ssd_conv_w[j]
    conv_new = xpad[:, -(SSD_CONV - 1):]
    xbc = jax.nn.silu(conv)
    xs, Bm, Cm = jnp.split(xbc, [SSD_INNER, SSD_INNER + SSD_GROUPS * SSD_STATE], axis=-1)
    xs = xs.reshape(B, T, SSD_HEADS, SSD_HEAD_DIM)
    rep = SSD_HEADS // SSD_GROUPS
    Bm = jnp.repeat(Bm.reshape(B, T, SSD_GROUPS, SSD_STATE), rep, axis=2)
    Cm = jnp.repeat(Cm.reshape(B, T, SSD_GROUPS, SSD_STATE), rep, axis=2)
    dt = jax.nn.softplus(m_dt.astype(jnp.float32) + ssd_dt_bias.astype(jnp.float32))
    A = -jnp.exp(ssd_a_log.astype(jnp.float32))
    y, s_ssd_new = scan_scalar_decay(Cm, Bm * dt[..., None], xs, dt * A, s_ssd)
    y = y + ssd_d.astype(jnp.float32)[:, None] * xs.astype(jnp.float32)
    y = y.reshape(B, T, SSD_INNER) * jax.nn.silu(m_z.astype(jnp.float32))
    y_ssd = rmsnorm(y, ssd_norm).astype(dt_)

    rq = rope(r_q.reshape(B, T, RET_HEADS, RET_DK), pos)
    rk = rope(r_k.reshape(B, T, RET_HEADS, RET_DK), pos) * (RET_DK ** -0.5)
    rv = r_v.reshape(B, T, RET_HEADS, RET_DV)
    log_gamma = jnp.log1p(-jnp.exp2(-5.0 - jnp.arange(RET_HEADS, dtype=jnp.float32)))
    o_ret, s_ret_new = scan_scalar_decay(rq, rk, rv, jnp.broadcast_to(log_gamma, (B, T, RET_HEADS)), s_ret)
    y_ret = (group_norm(o_ret, ret_norm) * jax.nn.silu(r_g.astype(jnp.float32))).astype(dt_)

    m = (jax.nn.sigmoid(gate_a) * (y_gla @ w_branch_gla)
         + jax.nn.sigmoid(gate_b) * (y_ssd @ w_branch_ssd)
         + jax.nn.sigmoid(gate_c) * (y_ret @ w_branch_ret))
    out = m @ w_out
    return out, (s_gla_new.astype(dt_), s_ssd_new.astype(dt_), conv_new, s_ret_new.astype(dt_))


def setup_inputs(seed: int = 0) -> dict:
    key = jax.random.key(seed)
    ks = jax.random.split(key, 32)
    f32 = jnp.float32

    def nrm(k, shape, scale):
        return jax.random.normal(k, shape, f32) * scale

    def gain(k, shape):
        return 1.0 + 0.05 * jax.random.normal(k, shape, f32)

    dt0 = jnp.exp(jax.random.uniform(ks[14], (DEPTH, SSD_HEADS), f32, math.log(1e-3), math.log(1e-1)))
    return {
        'x_prompt': nrm(ks[0], (BATCH, SEQ, D_MODEL), 1.0),
        'x_sample': nrm(ks[1], (DEC_BATCH, DEC_SEQ, D_MODEL), 1.0),
        'state_gla': nrm(ks[2], (DEPTH, DEC_BATCH, GLA_HEADS, GLA_DK, GLA_DV), 0.1),
        'state_ssd': nrm(ks[3], (DEPTH, DEC_BATCH, SSD_HEADS, SSD_STATE, SSD_HEAD_DIM), 0.1),
        'cache_conv': nrm(ks[4], (DEPTH, DEC_BATCH, SSD_CONV - 1, SSD_CONV_DIM), 1.0),
        'state_ret': nrm(ks[5], (DEPTH, DEC_BATCH, RET_HEADS, RET_DK, RET_DV), 0.1),
        'norm_ffn1': gain(ks[6], (DEPTH, D_MODEL)),
        'ffn1_w_in': nrm(ks[7], (DEPTH, D_MODEL, 2 * D_FF), D_MODEL ** -0.5),
        'ffn1_w_out': nrm(ks[8], (DEPTH, D_FF, D_MODEL), D_FF ** -0.5),
        'norm_mix': gain(ks[9], (DEPTH, D_MODEL)),
        'w_in': nrm(ks[10], (DEPTH, D_MODEL, IN_COLS), D_MODEL ** -0.5),
        'gla_w_gate2': nrm(ks[11], (DEPTH, GLA_RANK, GLA_HEADS * GLA_DK), GLA_RANK ** -0.5),
        'gla_b_gate': nrm(ks[12], (DEPTH, GLA_HEADS * GLA_DK), 0.1),
        'gla_norm': gain(ks[13], (DEPTH, GLA_HEADS * GLA_DV)),
        'ssd_conv_w': nrm(ks[15], (DEPTH, SSD_CONV, SSD_CONV_DIM), SSD_CONV ** -0.5),
        'ssd_conv_b': nrm(ks[16], (DEPTH, SSD_CONV_DIM), 0.02),
        'ssd_dt_bias': dt0 + jnp.log(-jnp.expm1(-dt0)),
        'ssd_a_log': jnp.log(jax.random.uniform(ks[17], (DEPTH, SSD_HEADS), f32, 1.0, 16.0)),
        'ssd_d': gain(ks[18], (DEPTH, SSD_HEADS)),
        'ssd_norm': gain(ks[19], (DEPTH, SSD_INNER)),
        'ret_norm': gain(ks[20], (DEPTH, RET_HEADS * RET_DV)),
        'w_branch_gla': nrm(ks[21], (DEPTH, GLA_HEADS * GLA_DV, D_MODEL), (GLA_HEADS * GLA_DV) ** -0.5),
        'w_branch_ssd': nrm(ks[22], (DEPTH, SSD_INNER, D_MODEL), SSD_INNER ** -0.5),
        'w_branch_ret': nrm(ks[23], (DEPTH, RET_HEADS * RET_DV, D_MODEL), (RET_HEADS * RET_DV) ** -0.5),
        'w_out': nrm(ks[24], (DEPTH, D_MODEL, D_MODEL), D_MODEL ** -0.5),
        'norm_ffn2': gain(ks[25], (DEPTH, D_MODEL)),
        'ffn2_w_in': nrm(ks[26], (DEPTH, D_MODEL, 2 * D_FF), D_MODEL ** -0.5),
        'ffn2_w_out': nrm(ks[27], (DEPTH, D_FF, D_MODEL), D_FF ** -0.5),
        'norm_final': gain(ks[28], (D_MODEL,)),
    }


def reference(x_prompt, x_sample, state_gla, state_ssd, cache_conv, state_ret,
              norm_ffn1, ffn1_w_in, ffn1_w_out, norm_mix, w_in, gla_w_gate2, gla_b_gate, gla_norm,
              ssd_conv_w, ssd_conv_b, ssd_dt_bias, ssd_a_log, ssd_d, ssd_norm, ret_norm,
              w_branch_gla, w_branch_ssd, w_branch_ret, w_out, norm_ffn2, ffn2_w_in, ffn2_w_out,
              norm_final):

    def run_layer(l, x, pos, s_gla, s_ssd, buf, s_ret):
        x = x + 0.5 * swiglu(rmsnorm(x, norm_ffn1[l]), ffn1_w_in[l], ffn1_w_out[l])
        mix, st = token_mixers(rmsnorm(x, norm_mix[l]), pos, s_gla, s_ssd, buf, s_ret, w_in[l],
                               gla_w_gate2[l], gla_b_gate[l], gla_norm[l], ssd_conv_w[l], ssd_conv_b[l],
                               ssd_dt_bias[l], ssd_a_log[l], ssd_d[l], ssd_norm[l], ret_norm[l],
                               w_branch_gla[l], w_branch_ssd[l], w_branch_ret[l], w_out[l])
        x = x + mix
        x = x + 0.5 * swiglu(rmsnorm(x, norm_ffn2[l]), ffn2_w_in[l], ffn2_w_out[l])
        return x, st

    Bp, Tp = x_prompt.shape[:2]
    Ts = x_sample.shape[1]
    dt_ = x_prompt.dtype
    pos_p = jnp.arange(Tp, dtype=jnp.int32)
    pos_s = PAST_LEN + jnp.arange(Ts, dtype=jnp.int32)
    xp, xs = x_prompt, x_sample
    new_p, new_s = [], []
    for l in range(DEPTH):
        xp, st_p = run_layer(l, xp, pos_p,
                             jnp.zeros((Bp, GLA_HEADS, GLA_DK, GLA_DV), dt_),
                             jnp.zeros((Bp, SSD_HEADS, SSD_STATE, SSD_HEAD_DIM), dt_),
                             jnp.zeros((Bp, SSD_CONV - 1, SSD_CONV_DIM), dt_),
                             jnp.zeros((Bp, RET_HEADS, RET_DK, RET_DV), dt_))
        xs, st_s = run_layer(l, xs, pos_s, state_gla[l], state_ssd[l], cache_conv[l], state_ret[l])
        new_p.append(st_p)
        new_s.append(st_s)
    y_prompt = rmsnorm(xp, norm_final)
    y_sample = rmsnorm(xs, norm_final)
    gla_p = jnp.stack([s[0] for s in new_p])
    ssd_p = jnp.stack([s[1] for s in new_p])
    conv_p = jnp.stack([s[2] for s in new_p])
    ret_p = jnp.stack([s[3] for s in new_p])
    gla_s = jnp.stack([s[0] for s in new_s])
    ssd_s = jnp.stack([s[1] for s in new_s])
    conv_s = jnp.stack([s[2] for s in new_s])
    ret_s = jnp.stack([s[3] for s in new_s])
    return (y_prompt, y_sample, gla_p, ssd_p, conv_p, ret_p, gla_s, ssd_s, conv_s, ret_s)
```

```python
import numpy as np
from contextlib import ExitStack
import concourse.bass as bass
import concourse.mybir as mybir
from concourse.bass_utils import run_bass_kernel_spmd

F32 = mybir.dt.float32
BF16 = mybir.dt.bfloat16
AF = mybir.ActivationFunctionType
ALU = mybir.AluOpType
AX = mybir.AxisListType

D = 1024
KC = 8
DFF = 2816
FC = 22
INC = 7448
EPS = 1e-6
(O_GQ, O_GK, O_GV, O_GR, O_GLR, O_MZ, O_XBC, O_DT, O_RQ, O_RK, O_RV, O_RG, O_GA, O_GB, O_GC) = (
    0, 256, 512, 1024, 1536, 1552, 2064, 2832, 2840, 3096, 3352, 3864, 4376, 5400, 6424)
NSLOT = 5
NSCR = 5
NSM = 8
ENGS = ('pe', 'act', 'dve', 'pool', 'sp')


class Sched:
    def __init__(self):
        self.ops = {e: [] for e in ENGS}
        self.last_w = {}
        self.readers = {}
        self.dcount = {}
        self.tag = ''

    def op(self, eng, fn, r=(), w=(), dsem=None, ndma=1):
        deps = {}

        def add(k2, v):
            if deps.get(k2, 0) < v:
                deps[k2] = v

        for k in r:
            t = self.last_w.get(k)
            if t is not None:
                add(t[:2], t[2])
        for k in w:
            t = self.last_w.get(k)
            if t is not None:
                add(t[:2], t[2])
            for k2, v in self.readers.get(k, {}).items():
                add(k2, v)
        idx = len(self.ops[eng]) + 1
        if dsem is None:
            tok = ('E', eng, idx)
        else:
            self.dcount[dsem] = self.dcount.get(dsem, 0) + ndma
            tok = ('D', dsem, 16 * self.dcount[dsem])
        for k in w:
            self.last_w[k] = tok
            self.readers[k] = {}
        for k in r:
            d = self.readers.setdefault(k, {})
            if d.get(tok[:2], 0) < tok[2]:
                d[tok[:2]] = tok[2]
        self.ops[eng].append(dict(fn=fn, deps=deps, dsem=dsem, flag=False, waits=[], tag=self.tag))

    def finalize(self):
        for eng in ENGS:
            waited = {}
            for op in self.ops[eng]:
                for (kind, name), val in op['deps'].items():
                    if kind == 'E' and name == 'pe' and eng == 'pe':
                        continue
                    if waited.get((kind, name), 0) >= val:
                        continue
                    waited[(kind, name)] = val
                    op['waits'].append((kind, name, val))
                    if kind == 'E':
                        self.ops[name][val - 1]['flag'] = True
        self.semval = {}
        for eng in ENGS:
            c = 0
            for i, op in enumerate(self.ops[eng]):
                if op['flag'] and op['dsem'] is None:
                    c += 1
                    self.semval[(eng, i + 1)] = c

    def emit(self, eng, e, sems, dsems):
        for op in self.ops[eng]:
            for (kind, name, val) in op['waits']:
                if kind == 'E':
                    e.wait_ge(sems[name], self.semval[(name, val)])
                else:
                    e.wait_ge(dsems[name], val)
            ins = op['fn'](e)
            if op['dsem'] is not None:
                for x in (ins if isinstance(ins, (list, tuple)) else [ins]):
                    x.then_inc(dsems[op['dsem']], 16)
            elif op['flag']:
                ins.then_inc(sems[eng], 1)


class ST:
    pass


class NS_:
    pass


class _Dual:
    def __init__(self, n):
        self.n = n

    def __get__(self, obj, cls):
        if obj is None:
            return self
        return getattr(obj.F if obj.f32 else obj.B, self.n)


DUAL = ('hb', 'hT', 'hid', 'glrT', 'qdT', 'kdT', 'v', 'P', 'yb', 'BCT', 'xdt', 'Mh', 'rqk', 'rqkT', 'rv', 'yT', 'mT', 'w2t', 'idb')


class Prog:
    DBG = 99
    def __init__(self, NPS, T, NSS, TS=16):
        self.NPS, self.T, self.NSS, self.TS = NPS, T, NSS, TS
        self.nc = bass.Bass("TRN2", target_bir_lowering=False)
        self.S = Sched()
        self.es = ExitStack()
        self.din = {}
        self.dout = {}
        self.ps_live = [False] * 8
        self.ps_next = 0
        self.scr_next = 0
        self.sm_next = 0
        self.ring_cnt = 0
        self.gt_cnt = 0
        self.rt_cnt = 0
        self.f32 = False
        self.slot_f32 = {}
        self.deferred = []

    def input_specs(self):
        NPS, T, NSS, TS = self.NPS, self.T, self.NSS, self.TS
        sp = {
            'xp': ([NPS, T, D], F32), 'xs': ([NSS, TS, D], F32),
            'sgla': ([2, NSS, 256, 128], F32), 'sssd': ([2, NSS, 512, 64], F32),
            'cconv': ([2, NSS, 3, 768], F32), 'sret': ([2, NSS, 256, 128], F32),
            'ffn1_w_in': ([2, D, 2 * DFF], F32), 'ffn1_w_out': ([2, DFF, D], F32),
            'w_in': ([2, D, INC], F32), 'w2': ([2, 16, 256], F32),
            'wbg': ([2, 512, D], F32), 'wbs': ([2, 512, D], F32), 'wbr': ([2, 512, D], F32),
            'w_out': ([2, D, D], F32),
            'ffn2_w_in': ([2, D, 2 * DFF], F32), 'ffn2_w_out': ([2, DFF, D], F32),
            'gains': ([7, 128, D], F32), 'glab': ([128, 2, 256], F32),
            'gnorm': ([128, 2, 512], F32), 'snorm': ([128, 2, 512], F32), 'rnorm': ([128, 2, 512], F32),
            'dtb': ([128, 2, 8], F32), 'alog': ([128, 2, 8], F32), 'dD': ([128, 2, 8], F32),
            'cw': ([128, 2, 6, 4], F32), 'cb': ([128, 2, 6], F32),
            'idf': ([128, 128], F32), 'tri': ([128, 128], F32), 'upper': ([128, 128], F32),
            'ones': ([128, 128], F32), 'neg': ([128, 128], F32), 'dret': ([128, 4, 128], F32),
            'rtab': ([128, 24], F32),
            'rope': ([4096, 512], F32),
        }
        return sp

    def output_specs(self):
        NPS, T, NSS, TS = self.NPS, self.T, self.NSS, self.TS
        return {
            'yp': ([NPS, T, D], F32), 'ys': ([NSS, TS, D], F32),
            'gla_p': ([2, NPS, 256, 128], F32), 'ssd_p': ([2, NPS, 512, 64], F32),
            'conv_p': ([2, NPS, 3, 768], F32), 'ret_p': ([2, NPS, 256, 128], F32),
            'gla_s': ([2, NSS, 256, 128], F32), 'ssd_s': ([2, NSS, 512, 64], F32),
            'conv_s': ([2, NSS, 3, 768], F32), 'ret_s': ([2, NSS, 256, 128], F32),
        }

    def sb(self, name, shape, dt):
        return self.es.enter_context(self.nc.sbuf_tensor('t_' + name, shape, dt))

    def declare(self):
        nc = self.nc
        for k, (shp, dt) in self.input_specs().items():
            self.din[k] = nc.dram_tensor(k, shp, dt, kind="ExternalInput").ap()
        for k, (shp, dt) in self.output_specs().items():
            self.dout[k] = nc.dram_tensor(k, shp, dt, kind="ExternalOutput").ap()
        self.wb = {}
        for k in ('ffn1_w_in', 'ffn1_w_out', 'w_in', 'w2', 'wbg', 'wbs', 'wbr', 'w_out', 'ffn2_w_in', 'ffn2_w_out'):
            shp = self.input_specs()[k][0]
            self.wb[k] = nc.dram_tensor('b_' + k, shp, BF16, kind="Internal").ap()
        self.es.enter_context(nc.allow_low_precision("bf16 matmul operands, fp32 accumulate"))
        self.es.enter_context(nc.allow_non_contiguous_dma("small strided state/conv transfers"))
        sb = self.sb
        self.x = sb('x', [128, 2, D], F32)
        self.hb = sb('hb', [128, 2, D], BF16)
        self.hT = sb('hT', [128, KC, 256], BF16)
        self.gt = sb('gt', [128, 2, D], F32)
        self.hid = sb('hid', [128, FC, 256], BF16)
        self.wr = sb('wr', [128, NSLOT, 4096], BF16)
        self.scr = sb('scr', [128, NSCR, 512], F32)
        self.sm = sb('sm', [128, NSM, 16], F32)
        self.glrT = sb('glrT', [16, 256], BF16)
        self.la = sb('la', [128, 2, 256], F32)
        self.ebT = sb('ebT', [128, 2, 256], F32)
        self.enbT = sb('enbT', [128, 2, 256], F32)
        self.ebLb = sb('ebLb', [128, 2, 256], F32)
        self.qdT = sb('qdT', [128, 2, 256], BF16)
        self.kdT = sb('kdT', [128, 2, 256], BF16)
        self.ks = sb('ks', [128, 2, 256], BF16)
        self.v = sb('v', [128, 2, 512], BF16)
        self.grs = sb('grs', [128, 2, 512], BF16)
        self.P = sb('P', [128, 4, 128], BF16)
        self.yb = sb('yb', [128, 512], BF16)
        self.zs = sb('zs', [128, 2, 512], BF16)
        self.xbcT = sb('xbcT', [128, 6, 260], F32)
        self.acc = sb('acc', [128, 6, 256], F32)
        self.cst = sb('cst', [128, 2, 6, 3], F32)
        self.BCT = sb('BCT', [128, 2, 256], BF16)
        self.xdt = sb('xdt', [128, 512], BF16)
        self.xdd = sb('xdd', [128, 512], BF16)
        self.xsD = sb('xsD', [128, 512], F32)
        self.Btm = sb('Btm', [128, 128], BF16)
        self.dtt = sb('dtt', [128, 2, 8], F32)
        self.at = sb('at', [128, 2, 8], F32)
        self.sdec = sb('sdec', [128, 40], F32)
        self.Dh = sb('Dh', [128, 8, 128], F32)
        self.GT = sb('GT', [128, 2, 128], F32)
        self.Mh = sb('Mh', [128, 8, 128], BF16)
        self.rt = sb('rt', [128, 1, 512], F32)
        self.rqk = sb('rqk', [128, 2, 512], BF16)
        self.rqkT = sb('rqkT', [128, 4, 256], BF16)
        self.rkd = sb('rkd', [128, 2, 256], BF16)
        self.rv = sb('rv', [128, 2, 512], BF16)
        self.rgs = sb('rgs', [128, 2, 512], BF16)
        self.yT = [sb('yT%d' % i, [128, 4, 256], BF16) for i in range(3)]
        self.mT = sb('mT', [128, KC, 256], BF16)
        self.Sf = [[sb('Sf%d_%d' % (l, k), [128, 256], F32) for k in range(3)] for l in range(2)]
        self.Sb = [[sb('Sb%d_%d' % (l, k), [128, 512], BF16) for k in range(3)] for l in range(2)]
        self.idf = sb('idf', [128, 128], F32)
        self.idb = sb('idb', [128, 128], BF16)
        self.tri = sb('tri', [128, 128], F32)
        self.upper = sb('upper', [128, 128], F32)
        self.ones = sb('ones', [128, 128], F32)
        self.neg = sb('neg', [128, 128], F32)
        self.dret = sb('dret', [128, 4, 128], F32)
        self.rtab = sb('rtab', [128, 24], F32)
        self.glab = sb('glab', [128, 2, 256], F32)
        self.gnorm = sb('gnorm', [128, 2, 512], F32)
        self.snorm = sb('snorm', [128, 2, 512], F32)
        self.rnorm = sb('rnorm', [128, 2, 512], F32)
        self.dtb = sb('dtb', [128, 2, 8], F32)
        self.At = sb('At', [128, 2, 8], F32)
        self.dD = sb('dD', [128, 2, 8], F32)
        self.cw = sb('cw', [128, 2, 6, 4], F32)
        self.cb = sb('cb', [128, 2, 6], F32)
        self.w2t = sb('w2t', [16, 2, 256], BF16)
        self.ps = [self.es.enter_context(nc.psum_tensor('ps%d' % i, [128, 512], F32)) for i in range(8)]
        names = ('hb', 'hT', 'hid', 'glrT', 'qdT', 'kdT', 'v', 'P', 'yb', 'BCT', 'xdt', 'Mh', 'rqk', 'rqkT', 'rv', 'yT', 'mT',
                 'w2t', 'idb')
        self.B = NS_()
        for n in names:
            setattr(self.B, n, self.__dict__.pop(n))
        Fs = NS_()
        Fs.hb = sb('f_hb', [128, 1, D], F32)
        Fs.hT = sb('f_hT', [128, KC, 16], F32)
        Fs.hid = sb('f_hid', [128, FC, 16], F32)
        Fs.glrT = sb('f_glrT', [16, 16], F32)
        Fs.qdT = sb('f_qdT', [128, 2, 16], F32)
        Fs.kdT = sb('f_kdT', [128, 2, 16], F32)
        Fs.v = sb('f_v', [128, 1, 512], F32)
        Fs.P = sb('f_P', [128, 4, 16], F32)
        Fs.yb = sb('f_yb', [128, 512], F32)
        Fs.BCT = sb('f_BCT', [128, 2, 16], F32)
        Fs.xdt = sb('f_xdt', [128, 512], F32)
        Fs.Mh = sb('f_Mh', [128, 8, 16], F32)
        Fs.rqk = sb('f_rqk', [128, 1, 512], F32)
        Fs.rqkT = sb('f_rqkT', [128, 4, 16], F32)
        Fs.rv = sb('f_rv', [128, 1, 512], F32)
        Fs.yT = [sb('f_yT%d' % i, [128, 4, 16], F32) for i in range(3)]
        Fs.mT = sb('f_mT', [128, KC, 16], F32)
        Fs.w2t = sb('f_w2t', [16, 2, 256], F32)
        Fs.idb = self.idf
        self.F = Fs

    def pa(self):
        for _ in range(8):
            b = self.ps_next
            self.ps_next = (self.ps_next + 1) % 8
            if not self.ps_live[b]:
                self.ps_live[b] = True
                return b
        raise RuntimeError("PSUM exhausted")

    def pf(self, b):
        self.ps_live[b] = False

    def psb(self, b):
        return self.ps[b][:, :] if self.f32 else self.ps[b][:].bitcast(BF16)

    @property
    def toff(self):
        return 64 if self.f32 else 128

    def sc(self):
        i = self.scr_next
        self.scr_next = (i + 1) % NSCR
        return i

    def smi(self):
        i = self.sm_next
        self.sm_next = (i + 1) % NSM
        return i

    def op(self, eng, fn, **k):
        f = self.f32

        def fn2(e, fn=fn, f=f):
            old = self.f32
            self.f32 = f
            try:
                return fn(e)
            finally:
                self.f32 = old

        self.S.op(eng, fn2, **k)

    def mm(self, out, lhsT, rhs, start, stop, r, w):
        self.S.op('pe', lambda e: e.matmul(out, lhsT, rhs, start=start, stop=stop), r=r, w=w)

    def tp(self, out, in_, ident, r, w):
        self.S.op('pe', lambda e: e.transpose(out=out, in_=in_, identity=ident), r=r, w=w)

    def ring(self, dmas, wkeys, nslots=2):
        s = self.ring_cnt % NSLOT
        if self.f32 and nslots == 1:
            self.ring_cnt += 1
            self.slot_f32[s] = 1
            keys = [('wr', s)]
        elif self.f32:
            if s == NSLOT - 1:
                self.ring_cnt += 1
                s = 0
            self.ring_cnt += 2
            self.slot_f32[s] = 2
            keys = [('wr', s), ('wr', s + 1)]
        else:
            self.ring_cnt += 1
            self.slot_f32[s] = False
            keys = [('wr', s)]
        flat = self.sv(s)

        def fn(e, dmas=dmas, flat=flat):
            return [e.dma_start(out=d(flat), in_=src) for d, src in dmas]

        self.S.op('sp', fn, r=list(wkeys), w=keys, dsem=('wr', s), ndma=len(dmas))
        return s

    def sv(self, s):
        if self.slot_f32.get(s) == 2:
            return self.wr[:, s:s + 2, :].rearrange("p a b -> p (a b)").bitcast(F32)
        if self.slot_f32.get(s) == 1:
            return self.wr[:, s, :].bitcast(F32)
        return self.wr[:, s, :]

    def sk(self, s):
        return [('wr', s), ('wr', s + 1)] if self.slot_f32.get(s) == 2 else [('wr', s)]

    def wsrc(self, name):
        return self.din[name] if self.f32 else self.wb[name]

    def wview(self, name, l):
        return self.wsrc(name)[l].rearrange("(kc p) n -> p kc n", p=128)

    def setup(self):
        S, din = self.S, self.din
        order = []
        for l in range(2):
            for k in ('ffn1_w_in', 'ffn1_w_out', 'w_in', 'w2', 'wbg', 'wbs', 'wbr', 'w_out', 'ffn2_w_in', 'ffn2_w_out'):
                order.append((k, l))
        for (k, l) in order:
            src, dst = din[k][l], self.wb[k][l]
            rows = src.shape[0]
            step = 128 if rows > 128 else rows
            pieces = [(r0, min(r0 + step, rows)) for r0 in range(0, rows, step)]

            def fn(e, src=src, dst=dst, pieces=pieces):
                return [e.dma_start(out=dst[a:b, :], in_=src[a:b, :]) for a, b in pieces]

            S.op('pool', fn, r=[], w=[('W', k, l)], dsem=('cast', k, l), ndma=len(pieces))
        cl = [(self.idf, 'idf'), (self.tri, 'tri'), (self.upper, 'upper'), (self.ones, 'ones'), (self.neg, 'neg'),
              (self.dret, 'dret'), (self.rtab, 'rtab'), (self.glab, 'glab'), (self.gnorm, 'gnorm'),
              (self.snorm, 'snorm'), (self.rnorm, 'rnorm'), (self.dtb, 'dtb'), (self.At, 'alog'), (self.dD, 'dD'),
              (self.cw, 'cw'), (self.cb, 'cb')]

        def fnc(e):
            return [e.dma_start(out=t[:], in_=din[n]) for t, n in cl]

        S.op('sp', fnc, r=[], w=[('const',)], dsem=('const',), ndma=len(cl))
        S.op('sp', lambda e: e.dma_start(out=self.w2t[:], in_=self.wb['w2'].rearrange("l k n -> k l n")),
             r=[('W', 'w2', 0), ('W', 'w2', 1)], w=[('w2t',)], dsem=('w2t',))
        S.op('dve', lambda e: e.tensor_copy(out=self.idb[:], in_=self.idf[:]), r=[('const',)], w=[('idb',)])
        S.op('sp', lambda e: e.dma_start(out=self.F.w2t[:], in_=din['w2'].rearrange("l k n -> k l n")),
             r=[], w=[('w2t',)], dsem=('w2f',))
        for l in range(2):
            for k in range(3):
                S.op('pool', lambda e, t=self.Sb[l][k]: e.memset(t[:, :], 0.0), r=[], w=[('Sb', l, k)])
        S.op('pool', lambda e: e.memset(self.acc[:, :, :], 0.0), r=[], w=[('acc', c) for c in range(6)])
        S.op('act', lambda e: e.activation(out=self.At[:], in_=self.At[:], func=AF.Exp), r=[('const',)], w=[('At',)])
        S.op('dve', lambda e: e.tensor_scalar(out=self.At[:], in0=self.At[:], scalar1=-1.0, scalar2=0.0, op0=ALU.mult, op1=ALU.add),
             r=[('At',)], w=[('At',)])

    def norm_T(self, st, gidx):
        self.S.tag = 'norm'
        L, NS = st.L, st.NS
        g = self.gt_cnt % 2
        self.gt_cnt += 1
        self.op('sp', lambda e: e.dma_start(out=self.gt[:, g, :], in_=self.din['gains'][gidx]),
                r=[], w=[('gt', g)], dsem=('gt', g))
        for j in range(NS):
            m = self.smi()
            sm = self.sm
            self.op('act', lambda e, j=j, m=m: e.activation(out=self.hb[0:L, j, :], in_=self.x[0:L, j, :], func=AF.Square,
                                                           accum_out=sm[0:L, m, 0:1]),
                    r=[('x', j)], w=[('hb', j), ('sm', m)])
            self.op('act', lambda e, m=m: e.activation(out=sm[0:L, m, 1:2], in_=sm[0:L, m, 0:1], func=AF.Ln, bias=EPS,
                                                      scale=1.0 / D), r=[('sm', m)], w=[('sm', m)])
            self.op('act', lambda e, m=m: e.activation(out=sm[0:L, m, 2:3], in_=sm[0:L, m, 1:2], func=AF.Exp, scale=-0.5),
                    r=[('sm', m)], w=[('sm', m)])
            self.op('dve', lambda e, j=j, m=m: e.scalar_tensor_tensor(
                out=self.hb[0:L, j, :], in0=self.x[0:L, j, :], scalar=sm[0:L, m, 2:3], in1=self.gt[0:L, g, :],
                op0=ALU.mult, op1=ALU.mult), r=[('x', j), ('sm', m), ('gt', g)], w=[('hb', j)])
            b = self.pa()
            pb = self.psb(b)
            to = self.toff
            for kc in range(KC):
                self.tp(pb[:, kc * to:kc * to + L], self.hb[0:L, j, kc * 128:(kc + 1) * 128], self.idb[0:L, 0:L],
                        r=[('hb', j), ('idb',)], w=[('ps', b)])
            self.op('dve', lambda e, j=j, pb=pb, to=to: e.tensor_copy(
                out=self.hT[:, :, j * L:(j + 1) * L], in_=pb[:, 0:KC * to].rearrange("p (k l) -> p k l", k=KC)[:, :, 0:L]),
                r=[('ps', b)], w=[('hT',)])
            self.pf(b)

    def ffn(self, st, l, which, gidx):
        L, NS, NT = st.L, st.NS, st.NT
        self.norm_T(st, gidx)
        self.S.tag = 'ffn_in'
        wi = self.wview('ffn%d_w_in' % which, l)
        wo = self.wsrc('ffn%d_w_out' % which)[l].rearrange("(c p) n -> p c n", p=128)
        kin = ('W', 'ffn%d_w_in' % which, l)
        kout = ('W', 'ffn%d_w_out' % which, l)
        for i in range(FC // 2):
            s = self.ring([
                (lambda f: f[:, 0:4096].rearrange("p (k n) -> p k n", k=KC)[:, :, 0:256], wi[:, :, 256 * i:256 * i + 256]),
                (lambda f: f[:, 0:4096].rearrange("p (k n) -> p k n", k=KC)[:, :, 256:512],
                 wi[:, :, DFF + 256 * i:DFF + 256 * i + 256])], [kin])
            pv = self.sv(s).rearrange("p (k n) -> p k n", k=KC)
            for mm_ in range(2):
                m = 2 * i + mm_
                b = self.pa()
                for half in range(2):
                    for kc in range(KC):
                        self.mm(self.ps[b][:, half * 256:half * 256 + NT],
                                pv[:, kc, half * 256 + mm_ * 128:half * 256 + mm_ * 128 + 128], self.hT[:, kc, 0:NT],
                                kc == 0, kc == KC - 1, r=[*self.sk(s), ('hT',)], w=[('ps', b)])
                t = self.sc()
                self.op('act', lambda e, b=b, t=t: e.activation(out=self.scr[:, t, 0:NT], in_=self.ps[b][:, 0:NT],
                                                               func=AF.Silu), r=[('ps', b)], w=[('scr', t)])
                self.op('dve', lambda e, b=b, t=t, m=m: e.tensor_tensor(
                    out=self.hid[:, m, 0:NT], in0=self.scr[:, t, 0:NT], in1=self.ps[b][:, 256:256 + NT], op=ALU.mult),
                    r=[('scr', t), ('ps', b)], w=[('hid', m)])
                self.pf(b)
        self.S.tag = 'ffn_out'
        banks = [[self.pa() for _ in range(2)] for _ in range(NS)]
        for i in range((FC + 3) // 4):
            c0 = 4 * i
            n = min(4, FC - c0)
            s = self.ring([(lambda f, n=n: f[:, 0:n * 1024].rearrange("p (c n) -> p c n", c=n), wo[:, c0:c0 + n, :])], [kout])
            pv = self.sv(s).rearrange("p (c n) -> p c n", c=4)
            for ci in range(n):
                c = c0 + ci
                for j in range(NS):
                    for half in range(2):
                        b = banks[j][half]
                        self.mm(self.ps[b][0:L, :], self.hid[:, c, j * L:(j + 1) * L], pv[:, ci, half * 512:(half + 1) * 512],
                                c == 0, c == FC - 1, r=[*self.sk(s), ('hid', c)], w=[('ps', b)])
        for j in range(NS):
            for half in range(2):
                b = banks[j][half]
                self.op('dve', lambda e, b=b, j=j, half=half: e.scalar_tensor_tensor(
                    out=self.x[0:L, j, half * 512:(half + 1) * 512], in0=self.ps[b][0:L, :], scalar=0.5,
                    in1=self.x[0:L, j, half * 512:(half + 1) * 512], op0=ALU.mult, op1=ALU.add),
                    r=[('ps', b), ('x', j)], w=[('x', j)])
                self.pf(b)

    def proj_tm(self, st, s, pv, c0, n, evac):
        L, NS = st.L, st.NS
        for j in range(NS):
            b = self.pa()
            for kc in range(KC):
                self.mm(self.ps[b][0:L, 0:n], self.hT[:, kc, j * L:(j + 1) * L], pv[:, kc, c0:c0 + n],
                        kc == 0, kc == KC - 1, r=[*self.sk(s), ('hT',)], w=[('ps', b)])
            evac(j, b)
            self.pf(b)

    def panel(self, l, c0, n):
        wi = self.wview('w_in', l)
        s = self.ring([(lambda f, n=n: f[:, 0:KC * n].rearrange("p (k n) -> p k n", k=KC), wi[:, :, c0:c0 + n])],
                      [('W', 'w_in', l)])
        return s, self.sv(s)[:, 0:KC * n].rearrange("p (k n) -> p k n", k=KC)

    def rstd_ops(self, L, m, col_in, col_out, n, scale, ncols=1):
        sm = self.sm
        self.op('act', lambda e: e.activation(out=sm[0:L, m, col_out:col_out + ncols], in_=sm[0:L, m, col_in:col_in + ncols],
                                              func=AF.Ln, bias=EPS, scale=scale), r=[('sm', m)], w=[('sm', m)])
        self.op('act', lambda e: e.activation(out=sm[0:L, m, col_out:col_out + ncols], in_=sm[0:L, m, col_out:col_out + ncols],
                                              func=AF.Exp, scale=-0.5), r=[('sm', m)], w=[('sm', m)])

    def group_norm_out(self, st, j, u, gate, ydst):
        L = st.L
        self.flush_y()
        sm, scr = self.sm, self.scr
        m = self.smi()
        t = self.sc()
        u3 = scr[0:L, u, :].rearrange("p (h d) -> p h d", h=4)
        t3 = scr[0:L, t, :].rearrange("p (h d) -> p h d", h=4)
        self.op('dve', lambda e: e.tensor_reduce(out=sm[0:L, m, 0:4], in_=u3, axis=AX.X, op=ALU.add),
                r=[('scr', u)], w=[('sm', m)])
        self.op('dve', lambda e: e.tensor_tensor(out=scr[0:L, t, :], in0=scr[0:L, u, :], in1=scr[0:L, u, :], op=ALU.mult),
                r=[('scr', u)], w=[('scr', t)])
        self.op('dve', lambda e: e.tensor_reduce(out=sm[0:L, m, 4:8], in_=t3, axis=AX.X, op=ALU.add),
                r=[('scr', t), ('sm', m)], w=[('sm', m)])
        self.op('dve', lambda e: e.tensor_scalar(out=sm[0:L, m, 0:4], in0=sm[0:L, m, 0:4], scalar1=1.0 / 128, scalar2=0.0,
                                                 op0=ALU.mult, op1=ALU.add), r=[('sm', m)], w=[('sm', m)])
        self.op('dve', lambda e: e.tensor_tensor(out=sm[0:L, m, 8:12], in0=sm[0:L, m, 0:4], in1=sm[0:L, m, 0:4], op=ALU.mult),
                r=[('sm', m)], w=[('sm', m)])
        self.op('dve', lambda e: e.scalar_tensor_tensor(out=sm[0:L, m, 4:8], in0=sm[0:L, m, 4:8], scalar=1.0 / 128,
                                                        in1=sm[0:L, m, 8:12], op0=ALU.mult, op1=ALU.subtract),
                r=[('sm', m)], w=[('sm', m)])
        self.rstd_ops(L, m, 4, 12, 4, 1.0, ncols=4)
        self.op('dve', lambda e: e.tensor_tensor(out=u3, in0=u3, in1=sm[0:L, m, 0:4].unsqueeze(2).to_broadcast([L, 4, 128]),
                                                 op=ALU.subtract), r=[('scr', u), ('sm', m)], w=[('scr', u)])
        self.op('dve', lambda e: e.tensor_tensor(out=u3, in0=u3, in1=sm[0:L, m, 12:16].unsqueeze(2).to_broadcast([L, 4, 128]),
                                                 op=ALU.mult), r=[('scr', u), ('sm', m)], w=[('scr', u)])
        gk, gap = gate
        self.op('dve', lambda e: e.tensor_tensor(out=self.yb[0:L, :], in0=scr[0:L, u, :], in1=gap, op=ALU.mult),
                r=[('scr', u), gk], w=[('yb',)])
        self.y_transpose(st, j, ydst)

    def y_transpose(self, st, j, ydst):
        self.deferred.append((st, j, ydst))

    def flush_y(self):
        for (st, j, ydst) in self.deferred:
            self.y_transpose_now(st, j, ydst)
        self.deferred = []

    def y_transpose_now(self, st, j, ydst):
        L = st.L
        b = self.pa()
        pb = self.psb(b)
        to = self.toff
        for c4 in range(4):
            self.tp(pb[:, c4 * to:c4 * to + L], self.yb[0:L, c4 * 128:(c4 + 1) * 128], self.idb[0:L, 0:L],
                    r=[('yb',), ('idb',)], w=[('ps', b)])
        yt = self.yT[ydst]
        self.op('dve', lambda e: e.tensor_copy(out=yt[:, :, j * L:(j + 1) * L],
                                               in_=pb[:, 0:4 * to].rearrange("p (k l) -> p k l", k=4)[:, :, 0:L]),
                r=[('ps', b)], w=[('yT', ydst)])
        self.pf(b)

    def mixer(self, st, l):
        L, NS, NT = st.L, st.NS, st.NT
        S = self.S
        scr, sm = self.scr, self.sm
        self.norm_T(st, 3 * l + 1)
        wi = self.wview('w_in', l)
        kW = ('W', 'w_in', l)
        Sg, Ss, Sr = self.Sf[l]
        Sgb, Ssb, Srb = self.Sb[l]
        self.S.tag = 'gla_proj'
        s0 = self.ring([
            (lambda f: f[:, 0:KC * 32].rearrange("p (k n) -> p k n", k=KC)[:, :, 0:16], wi[:, :, O_GLR:O_GLR + 16]),
            (lambda f: f[:, 0:KC * 32].rearrange("p (k n) -> p k n", k=KC)[:, :, 16:24], wi[:, :, O_DT:O_DT + 8])], [kW])
        pv0 = self.sv(s0)[:, 0:KC * 32].rearrange("p (k n) -> p k n", k=KC)
        b = self.pa()
        for kc in range(KC):
            self.mm(self.ps[b][0:16, 0:NT], pv0[:, kc, 0:16], self.hT[:, kc, 0:NT], kc == 0, kc == KC - 1,
                    r=[*self.sk(s0), ('hT',)], w=[('ps', b)])
        self.op('dve', lambda e, b=b: e.tensor_copy(out=self.glrT[0:16, 0:NT], in_=self.ps[b][0:16, 0:NT]),
                r=[('ps', b)], w=[('glrT',)])
        self.pf(b)

        def ev_dt(j, b):
            self.op('dve', lambda e: e.tensor_tensor(out=self.dtt[0:L, j, :], in0=self.ps[b][0:L, 0:8],
                                                     in1=self.dtb[0:L, l, :], op=ALU.add),
                    r=[('ps', b), ('const',)], w=[('dtt', j)])
            self.op('act', lambda e: e.activation(out=self.dtt[0:L, j, :], in_=self.dtt[0:L, j, :], func=AF.Exp),
                    r=[('dtt', j)], w=[('dtt', j)])
            self.op('act', lambda e: e.activation(out=self.dtt[0:L, j, :], in_=self.dtt[0:L, j, :], func=AF.Ln, bias=1.0),
                    r=[('dtt', j)], w=[('dtt', j)])
            self.op('dve', lambda e: e.tensor_tensor(out=self.at[0:L, j, :], in0=self.dtt[0:L, j, :], in1=self.At[0:L, l, :],
                                                     op=ALU.mult), r=[('dtt', j), ('At',)], w=[('at', j)])

        self.proj_tm(st, s0, pv0, 16, 8, ev_dt)
        for j in range(NS):
            b = self.pa()
            self.mm(self.ps[b][0:L, 0:256], self.glrT[0:16, j * L:(j + 1) * L], self.w2t[0:16, l, :], True, True,
                    r=[('glrT',), ('w2t',)], w=[('ps', b)])
            self.op('dve', lambda e, j=j, b=b: e.tensor_tensor(out=self.la[0:L, j, :], in0=self.ps[b][0:L, 0:256],
                                                              in1=self.glab[0:L, l, :], op=ALU.add),
                    r=[('ps', b), ('const',)], w=[('la', j)])
            self.pf(b)
            self.op('act', lambda e, j=j: e.activation(out=self.la[0:L, j, :], in_=self.la[0:L, j, :], func=AF.Exp, scale=-1.0),
                    r=[('la', j)], w=[('la', j)])
            self.op('act', lambda e, j=j: e.activation(out=self.la[0:L, j, :], in_=self.la[0:L, j, :], func=AF.Ln, bias=1.0),
                    r=[('la', j)], w=[('la', j)])
            self.op('dve', lambda e, j=j: e.tensor_scalar(out=self.la[0:L, j, :], in0=self.la[0:L, j, :], scalar1=-1.0 / 16.0,
                                                         scalar2=0.0, op0=ALU.mult, op1=ALU.add), r=[('la', j)], w=[('la', j)])
            b = self.pa()
            for c in range(2):
                self.mm(self.ps[b][:, c * 128:c * 128 + L], self.la[0:L, j, c * 128:(c + 1) * 128], self.tri[0:L, 0:L],
                        True, True, r=[('la', j), ('const',)], w=[('ps', b)])
            b2 = self.pa()
            self.mm(self.ps[b2][0:L, 0:256], self.upper[0:L, 0:L], self.la[0:L, j, :], True, True,
                    r=[('la', j), ('const',)], w=[('ps', b2)])
            pin = self.ps[b][:, 0:256].rearrange("p (c l) -> p c l", c=2)[:, :, 0:L]
            self.op('act', lambda e, j=j, pin=pin: e.activation(out=self.ebT[:, :, j * L:(j + 1) * L], in_=pin, func=AF.Exp),
                    r=[('ps', b)], w=[('ebT',)])
            self.op('act', lambda e, j=j, pin=pin: e.activation(out=self.enbT[:, :, j * L:(j + 1) * L], in_=pin, func=AF.Exp,
                                                               scale=-1.0), r=[('ps', b)], w=[('enbT',)])
            self.op('act', lambda e, j=j, b2=b2: e.activation(out=self.ebLb[0:L, j, :], in_=self.ps[b2][0:L, 0:256], func=AF.Exp),
                    r=[('ps', b2)], w=[('ebLb', j)])
            self.pf(b)
            self.pf(b2)
        s1, pv1 = self.panel(l, 0, 512)
        for c in range(4):
            b = self.pa()
            for kc in range(KC):
                self.mm(self.ps[b][:, 0:NT], pv1[:, kc, c * 128:(c + 1) * 128], self.hT[:, kc, 0:NT], kc == 0, kc == KC - 1,
                        r=[*self.sk(s1), ('hT',)], w=[('ps', b)])
            if c < 2:
                self.op('dve', lambda e, b=b, c=c: e.scalar_tensor_tensor(
                    out=self.qdT[:, c, 0:NT], in0=self.ps[b][:, 0:NT], scalar=0.125, in1=self.ebT[:, c, 0:NT],
                    op0=ALU.mult, op1=ALU.mult), r=[('ps', b), ('ebT',)], w=[('qdT',)])
            else:
                self.op('dve', lambda e, b=b, c=c: e.tensor_tensor(
                    out=self.kdT[:, c - 2, 0:NT], in0=self.ps[b][:, 0:NT], in1=self.enbT[:, c - 2, 0:NT], op=ALU.mult),
                    r=[('ps', b), ('enbT',)], w=[('kdT',)])
            self.pf(b)

        def ev_k(j, b):
            self.op('dve', lambda e: e.tensor_tensor(out=self.ks[0:L, j, :], in0=self.ps[b][0:L, 0:256],
                                                     in1=self.ebLb[0:L, j, :], op=ALU.mult),
                    r=[('ps', b), ('ebLb', j)], w=[('ks', j)])

        self.proj_tm(st, s1, pv1, 256, 256, ev_k)
        s2, pv2 = self.panel(l, O_GV, 512)

        def ev_v(j, b):
            self.op('act', lambda e: e.activation(out=self.v[0:L, j, :], in_=self.ps[b][0:L, :], func=AF.Copy),
                    r=[('ps', b)], w=[('v', j)])
            if st.f32:
                self.op('act', lambda e: e.activation(out=self.B.v[0:L, j, :], in_=self.ps[b][0:L, :], func=AF.Copy),
                        r=[('ps', b)], w=[('vb', j)])

        self.proj_tm(st, s2, pv2, 0, 512, ev_v)

        def ev_gate(dst, dkey, normt):
            def ev(j, b):
                t = self.sc()
                self.op('act', lambda e: e.activation(out=scr[0:L, t, :], in_=self.ps[b][0:L, :], func=AF.Silu),
                        r=[('ps', b)], w=[('scr', t)])
                self.op('dve', lambda e: e.tensor_tensor(out=dst[0:L, j, :], in0=scr[0:L, t, :], in1=normt[0:L, l, :],
                                                          op=ALU.mult), r=[('scr', t), ('const',)], w=[(dkey, j)])
            return ev

        s3, pv3 = self.panel(l, O_GR, 512)
        self.proj_tm(st, s3, pv3, 0, 512, ev_gate(self.grs, 'grs', self.gnorm))
        self.S.tag = 'gla_scan'
        for j in range(NS):
            tok = slice(j * L, (j + 1) * L)
            bSs = [self.pa(), self.pa()]
            for h in range(4):
                c, hh = h // 2, h % 2
                pr = slice(hh * 64, hh * 64 + 64)
                self.mm(self.ps[bSs[hh]][0:L, c * 128:c * 128 + L], self.kdT[pr, c, tok], self.qdT[pr, c, tok], True, True,
                        r=[('kdT',), ('qdT',)], w=[('ps', bSs[hh])])
            for hh in range(2):
                self.op('dve', lambda e, hh=hh, bq=bSs[hh]: e.tensor_tensor(
                    out=self.P[0:L, :, 0:L].rearrange("p (c h) l -> p c h l", h=2)[:, :, hh, :],
                    in0=self.ps[bq][0:L, 0:256].rearrange("p (c l) -> p c l", c=2)[:, :, 0:L],
                    in1=self.tri[0:L, 0:L].unsqueeze(1).to_broadcast([L, 2, L]), op=ALU.mult),
                    r=[('ps', bSs[hh]), ('const',)], w=[('P',)])
                self.pf(bSs[hh])
            bO = self.pa()
            for c in range(2):
                if not st.f32:
                    self.mm(self.ps[bO][0:L, c * 256:(c + 1) * 256], self.qdT[:, c, tok], Sgb[:, c * 256:(c + 1) * 256],
                            True, False, r=[('qdT',), ('Sb', l, 0)], w=[('ps', bO)])
                for hh in range(2):
                    h = 2 * c + hh
                    self.mm(self.ps[bO][0:L, h * 128:(h + 1) * 128], self.P[0:L, h, 0:L], self.v[0:L, j, h * 128:(h + 1) * 128],
                            bool(st.f32), True if st.f32 else hh == 1, r=[('P',), ('v', j)], w=[('ps', bO)])
            u = self.sc()
            self.op('act', lambda e, bO=bO, u=u: e.activation(out=scr[0:L, u, :], in_=self.ps[bO][0:L, :], func=AF.Copy),
                    r=[('ps', bO)], w=[('scr', u)])
            self.pf(bO)
            bU = self.pa()
            for h in range(4):
                c, hh = h // 2, h % 2
                self.mm(self.ps[bU][hh * 64:hh * 64 + 64, c * 128:(c + 1) * 128], self.ks[0:L, j, h * 64:(h + 1) * 64],
                        self.B.v[0:L, j, h * 128:(h + 1) * 128], True, True, r=[('ks', j), ('v', j), ('vb', j)], w=[('ps', bU)])
            for c in range(2):
                self.op('dve', lambda e, c=c, bU=bU, j=j: e.scalar_tensor_tensor(
                    out=Sg[:, c * 128:(c + 1) * 128], in0=Sg[:, c * 128:(c + 1) * 128],
                    scalar=self.ebT[:, c, (j + 1) * L - 1:(j + 1) * L], in1=self.ps[bU][:, c * 128:(c + 1) * 128],
                    op0=ALU.mult, op1=ALU.add), r=[('Sf', l, 0), ('ebT',), ('ps', bU)], w=[('Sf', l, 0)])
            self.pf(bU)
            self.sb_copy(l, 0)
            self.group_norm_out(st, j, u, (('grs', j), self.grs[0:L, j, :]), 0)

        self.S.tag = 'ssd_proj'
        s4, pv4 = self.panel(l, O_MZ, 512)

        def ev_z(j, b):
            self.op('act', lambda e: e.activation(out=self.zs[0:L, j, :], in_=self.ps[b][0:L, :], func=AF.Silu),
                    r=[('ps', b)], w=[('zs', j)])

        self.proj_tm(st, s4, pv4, 0, 512, ev_z)
        s5, pv5 = self.panel(l, O_XBC, 512)
        s6, pv6 = self.panel(l, O_XBC + 512, 256)
        for cc in range(6):
            b = self.pa()
            for kc in range(KC):
                lh = pv5[:, kc, cc * 128:(cc + 1) * 128] if cc < 4 else pv6[:, kc, (cc - 4) * 128:(cc - 3) * 128]
                self.mm(self.ps[b][:, 0:NT], lh, self.hT[:, kc, 0:NT], kc == 0, kc == KC - 1,
                        r=[*self.sk(s5 if cc < 4 else s6), ('hT',)], w=[('ps', b)])
            self.op('act', lambda e, b=b, cc=cc: e.activation(out=self.xbcT[:, cc, 3:3 + NT], in_=self.ps[b][:, 0:NT],
                                                             func=AF.Copy), r=[('ps', b)], w=[('xbcT', cc)])
            self.pf(b)
            self.op('dve', lambda e, cc=cc: e.tensor_copy(out=self.xbcT[:, cc, 0:3], in_=self.cst[:, l, cc, :]),
                    r=[('cst', l)], w=[('xbcT', cc)])
            self.op('dve', lambda e, cc=cc: e.tensor_scalar(
                out=self.acc[:, cc, 0:NT], in0=self.xbcT[:, cc, 0:NT], scalar1=self.cw[:, l, cc, 0:1],
                scalar2=self.cb[:, l, cc:cc + 1], op0=ALU.mult, op1=ALU.add),
                r=[('xbcT', cc), ('const',)], w=[('acc', cc)])
            for jj in range(1, 4):
                self.op('dve', lambda e, cc=cc, jj=jj: e.scalar_tensor_tensor(
                    out=self.acc[:, cc, 0:NT], in0=self.xbcT[:, cc, jj:jj + NT], scalar=self.cw[:, l, cc, jj:jj + 1],
                    in1=self.acc[:, cc, 0:NT], op0=ALU.mult, op1=ALU.add),
                    r=[('xbcT', cc), ('const',), ('acc', cc)], w=[('acc', cc)])
            if cc < 4:
                self.op('act', lambda e, cc=cc: e.activation(out=self.acc[:, cc, 0:NT], in_=self.acc[:, cc, 0:NT], func=AF.Silu),
                        r=[('acc', cc)], w=[('acc', cc)])
            else:
                self.op('act', lambda e, cc=cc: e.activation(out=self.BCT[:, cc - 4, 0:NT], in_=self.acc[:, cc, 0:NT],
                                                            func=AF.Silu), r=[('acc', cc)], w=[('BCT',)])
                if st.f32 and cc == 4:
                    self.op('act', lambda e, cc=cc: e.activation(out=self.B.BCT[:, 0, 0:NT], in_=self.acc[:, cc, 0:NT],
                                                                func=AF.Silu), r=[('acc', cc)], w=[('BCTb',)])
        if st.last:
            dst = self.dout['conv_' + st.kind][l, st.seq]
            self.op('pool', lambda e, dst=dst: [e.dma_start(
                out=dst[:, cc * 128:(cc + 1) * 128].rearrange("t p -> p t"), in_=self.xbcT[:, cc, NT:NT + 3]) for cc in range(6)],
                r=[('xbcT', cc) for cc in range(6)], w=[], dsem=('st', l, 3), ndma=6)
        self.op('dve', lambda e: e.tensor_copy(out=self.cst[:, l, :, :], in_=self.xbcT[:, :, NT:NT + 3]),
                r=[('xbcT', cc) for cc in range(6)], w=[('cst', l)])
        self.S.tag = 'ssd_scan'
        sd = self.sdec
        for j in range(NS):
            tok = slice(j * L, (j + 1) * L)
            bd = self.pa()
            self.mm(self.ps[bd][0:L, 0:8], self.tri[0:L, 0:L], self.at[0:L, j, :], True, True, r=[('at', j), ('const',)],
                    w=[('ps', bd)])
            self.mm(self.ps[bd][0:L, 8:16], self.upper[0:L, 0:L], self.at[0:L, j, :], True, True, r=[('at', j), ('const',)],
                    w=[('ps', bd)])
            self.mm(self.ps[bd][:, 16:24], self.ones[0:L, :], self.at[0:L, j, :], True, True, r=[('at', j), ('const',)],
                    w=[('ps', bd)])
            self.op('dve', lambda e, bd=bd: e.tensor_scalar(out=sd[0:L, 0:8], in0=self.ps[bd][0:L, 0:8], scalar1=-1.0,
                                                           scalar2=0.0, op0=ALU.mult, op1=ALU.add), r=[('ps', bd)], w=[('sd', 0)])
            self.op('act', lambda e, bd=bd: e.activation(out=sd[0:L, 8:24], in_=self.ps[bd][0:L, 0:16], func=AF.Exp),
                    r=[('ps', bd)], w=[('sd', 1)])
            for g in range(2):
                self.op('act', lambda e, bd=bd, g=g: e.activation(
                    out=sd[g * 64:(g + 1) * 64, 24:28], in_=self.ps[bd][g * 64:(g + 1) * 64, 16 + 4 * g:20 + 4 * g], func=AF.Exp),
                    r=[('ps', bd)], w=[('sd', 2)])
            self.pf(bd)
            bX = self.pa()
            Lt = L if L > 64 else 128
            for c4 in range(4):
                self.tp(self.ps[bX][0:Lt, c4 * 128:(c4 + 1) * 128], self.acc[:, c4, j * L:j * L + Lt], self.idf[:, :],
                        r=[('acc', c4), ('const',)], w=[('ps', bX)])
            px3 = self.ps[bX][0:L, :].rearrange("p (h d) -> p h d", h=8)
            self.op('dve', lambda e, j=j, px3=px3: e.tensor_tensor(
                out=self.xdt[0:L, :].rearrange("p (h d) -> p h d", h=8), in0=px3,
                in1=self.dtt[0:L, j, :].unsqueeze(2).to_broadcast([L, 8, 64]), op=ALU.mult),
                r=[('ps', bX), ('dtt', j)], w=[('xdt',)])
            self.op('dve', lambda e, px3=px3: e.tensor_tensor(
                out=self.xsD[0:L, :].rearrange("p (h d) -> p h d", h=8), in0=px3,
                in1=self.dD[0:L, l, :].unsqueeze(2).to_broadcast([L, 8, 64]), op=ALU.mult),
                r=[('ps', bX), ('const',)], w=[('xsD',)])
            self.pf(bX)
            self.op('dve', lambda e: e.tensor_tensor(
                out=self.xdd[0:L, :].rearrange("p (h d) -> p h d", h=8), in0=self.xdt[0:L, :].rearrange("p (h d) -> p h d", h=8),
                in1=sd[0:L, 16:24].unsqueeze(2).to_broadcast([L, 8, 64]), op=ALU.mult),
                r=[('xdt',), ('sd', 1)], w=[('xdd',)])
            bB = self.pa()
            pbB = self.ps[bB][:].bitcast(BF16)
            self.tp(pbB[0:L, 0:128], self.B.BCT[:, 0, tok], self.B.idb[:, :], r=[('BCT',), ('BCTb',), ('idb',)], w=[('ps', bB)])
            self.op('dve', lambda e, pbB=pbB: e.tensor_copy(out=self.Btm[0:L, :], in_=pbB[0:L, 0:128]),
                    r=[('ps', bB)], w=[('Btm',)])
            self.pf(bB)
            for half in range(2):
                bD = self.pa()
                for hq in range(4):
                    h = 4 * half + hq
                    self.mm(self.ps[bD][0:L, hq * 128:hq * 128 + L], self.at[0:L, j, h:h + 1].to_broadcast([L, L]),
                            self.tri[0:L, 0:L], True, False, r=[('at', j), ('const',)], w=[('ps', bD)])
                    self.mm(self.ps[bD][0:L, hq * 128:hq * 128 + L], self.idf[0:L, 0:L], self.neg[0:L, 0:L], False, True,
                            r=[('const',)], w=[('ps', bD)])
                for hq in range(4):
                    h = 4 * half + hq
                    self.op('act', lambda e, bD=bD, hq=hq, h=h: e.activation(
                        out=self.Dh[0:L, h, 0:L], in_=self.ps[bD][0:L, hq * 128:hq * 128 + L], func=AF.Exp,
                        bias=sd[0:L, h:h + 1]), r=[('ps', bD), ('sd', 0)], w=[('Dh', half)])
                self.pf(bD)
            for g in range(2):
                pr = slice(g * 64, g * 64 + 64)
                bG = self.pa()
                self.mm(self.ps[bG][0:L, 0:L], self.BCT[pr, 0, tok], self.BCT[pr, 1, tok], True, True,
                        r=[('BCT',)], w=[('ps', bG)])
                self.op('act', lambda e, bG=bG, g=g: e.activation(out=self.GT[0:L, g, 0:L], in_=self.ps[bG][0:L, 0:L], func=AF.Copy),
                        r=[('ps', bG)], w=[('GT',)])
                self.pf(bG)
            for g in range(2):
                self.op('dve', lambda e, g=g: e.tensor_tensor(
                    out=self.Mh[0:L, 4 * g:4 * g + 4, 0:L], in0=self.Dh[0:L, 4 * g:4 * g + 4, 0:L],
                    in1=self.GT[0:L, g, 0:L].unsqueeze(1).to_broadcast([L, 4, L]), op=ALU.mult),
                    r=[('Dh', g), ('GT',)], w=[('Mh', g)])
            bO = self.pa()
            for h in range(8):
                self.mm(self.ps[bO][0:L, h * 64:(h + 1) * 64], self.Mh[0:L, h, 0:L], self.xdt[0:L, h * 64:(h + 1) * 64],
                        True, True, r=[('Mh', h // 4), ('xdt',)], w=[('ps', bO)])
            self.flush_y()
            u = self.sc()
            if st.f32:
                self.op('act', lambda e, bO=bO, u=u: e.activation(out=scr[0:L, u, :], in_=self.ps[bO][0:L, :], func=AF.Copy),
                        r=[('ps', bO)], w=[('scr', u)])
            else:
                bI = self.pa()
                self.mm(self.ps[bI][0:L, :], self.BCT[:, 1, tok], Ssb[:, :], True, True,
                        r=[('BCT',), ('Sb', l, 1)], w=[('ps', bI)])
                self.op('dve', lambda e, bI=bI, u=u: e.tensor_tensor(
                    out=scr[0:L, u, :].rearrange("p (h d) -> p h d", h=8),
                    in0=self.ps[bI][0:L, :].rearrange("p (h d) -> p h d", h=8),
                    in1=sd[0:L, 8:16].unsqueeze(2).to_broadcast([L, 8, 64]), op=ALU.mult),
                    r=[('ps', bI), ('sd', 1)], w=[('scr', u)])
                self.pf(bI)
                self.op('dve', lambda e, bO=bO, u=u: e.tensor_tensor(out=scr[0:L, u, :], in0=scr[0:L, u, :],
                                                                    in1=self.ps[bO][0:L, :], op=ALU.add),
                        r=[('ps', bO), ('scr', u)], w=[('scr', u)])
            self.pf(bO)
            self.op('dve', lambda e, u=u: e.tensor_tensor(out=scr[0:L, u, :], in0=scr[0:L, u, :], in1=self.xsD[0:L, :],
                                                          op=ALU.add), r=[('scr', u), ('xsD',)], w=[('scr', u)])
            self.op('dve', lambda e, u=u, j=j: e.tensor_tensor(out=scr[0:L, u, :], in0=scr[0:L, u, :], in1=self.zs[0:L, j, :],
                                                               op=ALU.mult), r=[('scr', u), ('zs', j)], w=[('scr', u)])
            m = self.smi()
            t = self.sc()
            self.op('dve', lambda e, u=u, t=t, m=m: e.scalar_tensor_tensor(
                out=scr[0:L, t, :], in0=scr[0:L, u, :], scalar=1.0, in1=scr[0:L, u, :], op0=ALU.mult, op1=ALU.mult,
                accum_out=sm[0:L, m, 0:1]), r=[('scr', u)], w=[('scr', t), ('sm', m)])
            self.rstd_ops(L, m, 0, 1, 1, 1.0 / 512)
            self.op('dve', lambda e, u=u, m=m: e.scalar_tensor_tensor(
                out=self.yb[0:L, :], in0=scr[0:L, u, :], scalar=sm[0:L, m, 1:2], in1=self.snorm[0:L, l, :],
                op0=ALU.mult, op1=ALU.mult), r=[('scr', u), ('sm', m), ('const',)], w=[('yb',)])
            self.y_transpose(st, j, 1)
            bU = self.pa()
            for g in range(2):
                self.mm(self.ps[bU][g * 64:(g + 1) * 64, 0:256], self.Btm[0:L, g * 64:(g + 1) * 64],
                        self.xdd[0:L, g * 256:(g + 1) * 256], True, True, r=[('Btm',), ('xdd',)], w=[('ps', bU)])
            self.op('dve', lambda e: e.tensor_tensor(
                out=Ss[:, :].rearrange("p (h d) -> p h d", h=4), in0=Ss[:, :].rearrange("p (h d) -> p h d", h=4),
                in1=sd[:, 24:28].unsqueeze(2).to_broadcast([128, 4, 64]), op=ALU.mult),
                r=[('Sf', l, 1), ('sd', 2)], w=[('Sf', l, 1)])
            self.op('dve', lambda e, bU=bU: e.tensor_tensor(out=Ss[:, :], in0=Ss[:, :], in1=self.ps[bU][:, 0:256], op=ALU.add),
                    r=[('Sf', l, 1), ('ps', bU)], w=[('Sf', l, 1)])
            self.pf(bU)
            self.sb_copy(l, 1)

        self.S.tag = 'ret_proj'
        s7, pv7 = self.panel(l, O_RQ, 512)
        rt = self.rt
        rk_off = {128: 4, 16: 8, 120: 12}[L]
        gl_off = {128: 16, 16: 18, 120: 20}[L]

        def ev_rope(j, b):
            q = 0
            self.rt_cnt += 1
            p0 = st.pos0 + j * L
            self.op('sp', lambda e: e.dma_start(out=rt[0:L, q, :], in_=self.din['rope'][p0:p0 + L, :]), r=[], w=[('rt', q)],
                    dsem=('rt', q))
            p4 = self.ps[b][0:L, :].rearrange("p (h t d) -> p h t d", h=8, t=2)
            x1, x2 = p4[:, :, 0, :], p4[:, :, 1, :]
            cos = rt[0:L, q, 0:256].rearrange("p (h d) -> p h d", h=8)
            sin = rt[0:L, q, 256:512].rearrange("p (h d) -> p h d", h=8)
            o4 = self.rqk[0:L, j, :].rearrange("p (h t d) -> p h t d", h=8, t=2)
            ta, tb = self.sc(), self.sc()
            va = scr[0:L, ta, 0:256].rearrange("p (h d) -> p h d", h=8)
            vb = scr[0:L, ta, 256:512].rearrange("p (h d) -> p h d", h=8)
            vc = scr[0:L, tb, 0:256].rearrange("p (h d) -> p h d", h=8)
            vd = scr[0:L, tb, 256:512].rearrange("p (h d) -> p h d", h=8)
            for (o, i0, i1) in ((va, x1, cos), (vb, x2, sin), (vc, x1, sin), (vd, x2, cos)):
                tk = ta if (o is va or o is vb) else tb
                self.op('dve', lambda e, o=o, i0=i0, i1=i1: e.tensor_tensor(out=o, in0=i0, in1=i1, op=ALU.mult),
                        r=[('ps', b), ('rt', q)], w=[('scr', tk)])
            self.op('dve', lambda e: e.tensor_tensor(out=o4[:, :, 0, :], in0=va, in1=vb, op=ALU.subtract),
                    r=[('scr', ta)], w=[('rqk', j)])
            self.op('dve', lambda e: e.tensor_tensor(out=o4[:, :, 1, :], in0=vc, in1=vd, op=ALU.add),
                    r=[('scr', tb)], w=[('rqk', j)])

        self.proj_tm(st, s7, pv7, 0, 512, ev_rope)
        for j in range(NS):
            bT = self.pa()
            pb = self.psb(bT)
            to = self.toff
            for c4 in range(4):
                self.tp(pb[:, c4 * to:c4 * to + L], self.rqk[0:L, j, c4 * 128:(c4 + 1) * 128], self.idb[0:L, 0:L],
                        r=[('rqk', j), ('idb',)], w=[('ps', bT)])
            self.op('dve', lambda e, j=j, pb=pb, to=to: e.tensor_copy(
                out=self.rqkT[:, :, j * L:(j + 1) * L], in_=pb[:, 0:4 * to].rearrange("p (k l) -> p k l", k=4)[:, :, 0:L]),
                r=[('ps', bT)], w=[('rqkT',)])
            self.pf(bT)
            self.op('dve', lambda e, j=j: e.tensor_tensor(
                out=self.rkd[0:L, j, :].rearrange("p (h d) -> p h d", h=4),
                in0=self.rqk[0:L, j, 256:512].rearrange("p (h d) -> p h d", h=4),
                in1=self.rtab[0:L, rk_off:rk_off + 4].unsqueeze(2).to_broadcast([L, 4, 64]), op=ALU.mult),
                r=[('rqk', j), ('const',)], w=[('rkd', j)])
        s8, pv8 = self.panel(l, O_RV, 512)

        def ev_rv(j, b):
            self.op('act', lambda e: e.activation(out=self.rv[0:L, j, :], in_=self.ps[b][0:L, :], func=AF.Copy),
                    r=[('ps', b)], w=[('rv', j)])
            if st.f32:
                self.op('act', lambda e: e.activation(out=self.B.rv[0:L, j, :], in_=self.ps[b][0:L, :], func=AF.Copy),
                        r=[('ps', b)], w=[('rvb', j)])

        self.proj_tm(st, s8, pv8, 0, 512, ev_rv)
        s9, pv9 = self.panel(l, O_RG, 512)
        self.S.tag2 = 0
        self.proj_tm(st, s9, pv9, 0, 512, ev_gate(self.rgs, 'rgs', self.rnorm))
        self.S.tag = "ret_scan"
        for j in range(NS):
            tok = slice(j * L, (j + 1) * L)
            bSs = [self.pa(), self.pa()]
            for h in range(4):
                c, hh = h // 2, h % 2
                pr = slice(hh * 64, hh * 64 + 64)
                self.mm(self.ps[bSs[hh]][0:L, c * 128:c * 128 + L], self.rqkT[pr, 2 + c, tok], self.rqkT[pr, c, tok], True, True,
                        r=[('rqkT',)], w=[('ps', bSs[hh])])
            for hh in range(2):
                self.op('dve', lambda e, hh=hh, bq=bSs[hh]: e.tensor_tensor(
                    out=self.P[0:L, :, 0:L].rearrange("p (c h) l -> p c h l", h=2)[:, :, hh, :],
                    in0=self.ps[bq][0:L, 0:256].rearrange("p (c l) -> p c l", c=2)[:, :, 0:L],
                    in1=self.dret[0:L, :, 0:L].rearrange("p (c h) l -> p c h l", h=2)[:, :, hh, :], op=ALU.mult),
                    r=[('ps', bSs[hh]), ('const',)], w=[('P',)])
                self.pf(bSs[hh])
            bO = self.pa()
            for h in range(4):
                self.mm(self.ps[bO][0:L, h * 128:(h + 1) * 128], self.P[0:L, h, 0:L], self.rv[0:L, j, h * 128:(h + 1) * 128],
                        True, True, r=[('P',), ('rv', j)], w=[('ps', bO)])
            u = self.sc()
            if st.f32:
                self.op('act', lambda e, bO=bO, u=u: e.activation(out=scr[0:L, u, :], in_=self.ps[bO][0:L, :], func=AF.Copy),
                        r=[('ps', bO)], w=[('scr', u)])
            else:
                bI = self.pa()
                for c in range(2):
                    self.mm(self.ps[bI][0:L, c * 256:(c + 1) * 256], self.rqkT[:, c, tok], Srb[:, c * 256:(c + 1) * 256],
                            True, True, r=[('rqkT',), ('Sb', l, 2)], w=[('ps', bI)])
                self.op('dve', lambda e, bI=bI, u=u: e.tensor_tensor(
                    out=scr[0:L, u, :].rearrange("p (h d) -> p h d", h=4),
                    in0=self.ps[bI][0:L, :].rearrange("p (h d) -> p h d", h=4),
                    in1=self.rtab[0:L, 0:4].unsqueeze(2).to_broadcast([L, 4, 128]), op=ALU.mult),
                    r=[('ps', bI), ('const',)], w=[('scr', u)])
                self.pf(bI)
                self.op('dve', lambda e, bO=bO, u=u: e.tensor_tensor(out=scr[0:L, u, :], in0=scr[0:L, u, :],
                                                                    in1=self.ps[bO][0:L, :], op=ALU.add),
                        r=[('ps', bO), ('scr', u)], w=[('scr', u)])
            self.pf(bO)
            bU = self.pa()
            for h in range(4):
                c, hh = h // 2, h % 2
                self.mm(self.ps[bU][hh * 64:hh * 64 + 64, c * 128:(c + 1) * 128], self.rkd[0:L, j, h * 64:(h + 1) * 64],
                        self.B.rv[0:L, j, h * 128:(h + 1) * 128], True, True, r=[('rkd', j), ('rv', j), ('rvb', j)], w=[('ps', bU)])
            for c in range(2):
                self.op('dve', lambda e, c=c, bU=bU: e.scalar_tensor_tensor(
                    out=Sr[:, c * 128:(c + 1) * 128], in0=Sr[:, c * 128:(c + 1) * 128],
                    scalar=self.rtab[:, gl_off + c:gl_off + c + 1], in1=self.ps[bU][:, c * 128:(c + 1) * 128],
                    op0=ALU.mult, op1=ALU.add), r=[('Sf', l, 2), ('const',), ('ps', bU)], w=[('Sf', l, 2)])
            self.pf(bU)
            self.sb_copy(l, 2)
            self.group_norm_out(st, j, u, (('rgs', j), self.rgs[0:L, j, :]), 2)

        self.flush_y()
        self.S.tag = 'merge'
        wbr = [self.wview(n, l) for n in ('wbg', 'wbs', 'wbr')]
        W = 128 if st.f32 else 256
        ndc = W // 128
        for pp in range(D // W):
            sA = self.ring([
                (lambda f: f[:, 0:KC * 2 * W].rearrange("p (k n) -> p k n", k=KC)[:, :, 0:W], wi[:, :, O_GA + pp * W:O_GA + (pp + 1) * W]),
                (lambda f: f[:, 0:KC * 2 * W].rearrange("p (k n) -> p k n", k=KC)[:, :, W:2 * W], wi[:, :, O_GB + pp * W:O_GB + (pp + 1) * W])],
                [kW], nslots=1)
            sB = self.ring([(lambda f: f[:, 0:KC * W].rearrange("p (k n) -> p k n", k=KC), wi[:, :, O_GC + pp * W:O_GC + (pp + 1) * W])],
                           [kW], nslots=1)
            sC = self.ring([(lambda f, xi=xi: f[:, xi * 4 * W:(xi + 1) * 4 * W].rearrange("p (k n) -> p k n", k=4),
                             wbr[xi][:, :, pp * W:(pp + 1) * W]) for xi in range(3)],
                           [('W', 'wbg', l), ('W', 'wbs', l), ('W', 'wbr', l)], nslots=1)
            pA = self.sv(sA)[:, 0:KC * 2 * W].rearrange("p (k n) -> p k n", k=KC)
            pB = self.sv(sB)[:, 0:KC * W].rearrange("p (k n) -> p k n", k=KC)
            pC = self.sv(sC)[:, 0:12 * W].rearrange("p (x k n) -> p x k n", x=3, k=4)
            for dc in range(ndc):
                c = ndc * pp + dc
                prs = []
                for xi in range(3):
                    b = self.pa()
                    for kc in range(KC):
                        lh = pA[:, kc, xi * W + dc * 128:xi * W + dc * 128 + 128] if xi < 2 else pB[:, kc, dc * 128:dc * 128 + 128]
                        self.mm(self.ps[b][:, 0:NT], lh, self.hT[:, kc, 0:NT], kc == 0, kc == KC - 1,
                                r=[*self.sk(sA if xi < 2 else sB), ('hT',)], w=[('ps', b)])
                    for kc in range(4):
                        self.mm(self.ps[b][:, 256:256 + NT], pC[:, xi, kc, dc * 128:dc * 128 + 128], self.yT[xi][:, kc, 0:NT],
                                kc == 0, kc == 3, r=[*self.sk(sC), ('yT', xi)], w=[('ps', b)])
                    t = self.sc()
                    self.op('act', lambda e, b=b, t=t: e.activation(out=scr[:, t, 0:NT], in_=self.ps[b][:, 0:NT], func=AF.Tanh,
                                                                   scale=0.5), r=[('ps', b)], w=[('scr', t)])
                    self.op('dve', lambda e, b=b, t=t: e.scalar_tensor_tensor(
                        out=scr[:, t, 256:256 + NT], in0=scr[:, t, 0:NT], scalar=1.0, in1=self.ps[b][:, 256:256 + NT],
                        op0=ALU.add, op1=ALU.mult), r=[('ps', b), ('scr', t)], w=[('scr', t)])
                    self.pf(b)
                    prs.append(t)
                t0, t1, t2 = prs
                self.op('dve', lambda e, t0=t0, t1=t1: e.tensor_tensor(out=scr[:, t0, 256:256 + NT], in0=scr[:, t0, 256:256 + NT],
                                                                       in1=scr[:, t1, 256:256 + NT], op=ALU.add),
                        r=[('scr', t0), ('scr', t1)], w=[('scr', t0)])
                self.op('dve', lambda e, t0=t0, t2=t2, c=c: e.tensor_tensor(out=self.mT[:, c, 0:NT], in0=scr[:, t0, 256:256 + NT],
                                                                            in1=scr[:, t2, 256:256 + NT], op=ALU.add),
                        r=[('scr', t0), ('scr', t2)], w=[('mT', c)])
        self.S.tag = 'wout'
        wo = self.wview('w_out', l)
        for half in range(2):
            sO = self.ring([(lambda f: f[:, 0:4096].rearrange("p (k n) -> p k n", k=KC), wo[:, :, half * 512:(half + 1) * 512])],
                           [('W', 'w_out', l)])
            pO = self.sv(sO).rearrange("p (k n) -> p k n", k=KC)
            for j in range(NS):
                b = self.pa()
                for kc in range(KC):
                    self.mm(self.ps[b][0:L, :], self.mT[:, kc, j * L:(j + 1) * L], pO[:, kc, :], kc == 0, kc == KC - 1,
                            r=[*self.sk(sO), ('mT', kc)], w=[('ps', b)])
                self.op('dve', lambda e, b=b, j=j, half=half: e.scalar_tensor_tensor(
                    out=self.x[0:L, j, half * 512:(half + 1) * 512], in0=self.ps[b][0:L, :], scalar=0.5,
                    in1=self.x[0:L, j, half * 512:(half + 1) * 512], op0=ALU.mult, op1=ALU.add),
                    r=[('ps', b), ('x', j)], w=[('x', j)])
                self.pf(b)

    def sb_copy(self, l, k):
        Sf, Sb = self.Sf[l][k], self.Sb[l][k]
        for q in range(2):
            pr = slice(q * 64, q * 64 + 64)
            if k == 1:
                self.op('dve', lambda e, pr=pr, q=q: e.tensor_copy(out=Sb[pr, q * 256:(q + 1) * 256], in_=Sf[pr, :]),
                        r=[('Sf', l, k)], w=[('Sb', l, k)])
            else:
                self.op('dve', lambda e, pr=pr, q=q: e.tensor_copy(
                    out=Sb[pr, :].rearrange("p (c h d) -> p c h d", c=2, h=2)[:, :, q, :],
                    in_=Sf[pr, :].rearrange("p (c d) -> p c d", c=2)), r=[('Sf', l, k)], w=[('Sb', l, k)])

    def state_init(self, st):
        for l in range(2):
            Sg, Ss, Sr = self.Sf[l]
            if st.kind == 'p':
                for k in range(3):
                    self.op('pool', lambda e, t=self.Sf[l][k]: e.memset(t[:, :], 0.0), r=[], w=[('Sf', l, k)])
                self.op('pool', lambda e, l=l: e.memset(self.cst[:, l, :, :], 0.0), r=[], w=[('cst', l)])
            else:
                b = st.seq
                self.op('pool', lambda e, l=l, b=b: e.dma_start(
                    out=self.Sf[l][0][:, :].rearrange("p (c d) -> p c d", c=2),
                    in_=self.din['sgla'][l, b].rearrange("(c q) d -> q c d", q=128)),
                    r=[], w=[('Sf', l, 0)], dsem=('si', l, 0))
                self.op('pool', lambda e, l=l, b=b: [e.dma_start(
                    out=self.Sf[l][1][g * 64:(g + 1) * 64, :].rearrange("p (h d) -> p h d", h=4),
                    in_=self.din['sssd'][l, b, g * 256:(g + 1) * 256, :].rearrange("(h n) d -> n h d", n=64)) for g in range(2)],
                    r=[], w=[('Sf', l, 1)], dsem=('si', l, 1), ndma=2)
                self.op('pool', lambda e, l=l, b=b: e.dma_start(
                    out=self.Sf[l][2][:, :].rearrange("p (c d) -> p c d", c=2),
                    in_=self.din['sret'][l, b].rearrange("(c q) d -> q c d", q=128)),
                    r=[], w=[('Sf', l, 2)], dsem=('si', l, 2))
                self.op('pool', lambda e, l=l, b=b: [e.dma_start(
                    out=self.cst[:, l, cc, :], in_=self.din['cconv'][l, b][:, cc * 128:(cc + 1) * 128].rearrange("t p -> p t"))
                    for cc in range(6)], r=[], w=[('cst', l)], dsem=('ci', l), ndma=6)
            for k in range(3):
                self.sb_copy(l, k)

    def state_out(self, st):
        sfx = st.kind
        b = st.seq
        for l in range(2):
            self.op('pool', lambda e, l=l: e.dma_start(
                out=self.dout['gla_' + sfx][l, b].rearrange("(c q) d -> q c d", q=128),
                in_=self.Sf[l][0][:, :].rearrange("p (c d) -> p c d", c=2)),
                r=[('Sf', l, 0)], w=[], dsem=('st', l, 0))
            self.op('pool', lambda e, l=l: [e.dma_start(
                out=self.dout['ssd_' + sfx][l, b, g * 256:(g + 1) * 256, :].rearrange("(h n) d -> n h d", n=64),
                in_=self.Sf[l][1][g * 64:(g + 1) * 64, :].rearrange("p (h d) -> p h d", h=4)) for g in range(2)],
                r=[('Sf', l, 1)], w=[], dsem=('st', l, 1), ndma=2)
            self.op('pool', lambda e, l=l: e.dma_start(
                out=self.dout['ret_' + sfx][l, b].rearrange("(c q) d -> q c d", q=128),
                in_=self.Sf[l][2][:, :].rearrange("p (c d) -> p c d", c=2)),
                r=[('Sf', l, 2)], w=[], dsem=('st', l, 2))

    def run_st(self, st):
        L, NS = st.L, st.NS
        self.f32 = st.f32
        src = self.din['xp'] if st.kind == 'p' else self.din['xs']
        dst = self.dout['yp'] if st.kind == 'p' else self.dout['ys']
        t0 = st.tok0
        for j in range(NS):
            self.op('pool', lambda e, j=j: e.dma_start(out=self.x[0:L, j, :], in_=src[st.seq, t0 + j * L:t0 + (j + 1) * L, :]),
                    r=[], w=[('x', j)], dsem=('xin', j))
        if st.first:
            self.state_init(st)
        ph = 0
        for l in range(2):
            for which in (1, 0, 2):
                ph += 1
                if ph > self.DBG:
                    continue
                if which == 0:
                    self.mixer(st, l)
                else:
                    self.ffn(st, l, which, 3 * l + (0 if which == 1 else 2))
        g = self.gt_cnt % 2
        self.gt_cnt += 1
        self.op('sp', lambda e: e.dma_start(out=self.gt[:, g, :], in_=self.din['gains'][6]), r=[], w=[('gt', g)], dsem=('gt', g))
        sm = self.sm
        for j in range(NS):
            m = self.smi()
            self.op('act', lambda e, j=j, m=m: e.activation(out=self.hb[0:L, j, :], in_=self.x[0:L, j, :], func=AF.Square,
                                                           accum_out=sm[0:L, m, 0:1]), r=[('x', j)], w=[('hb', j), ('sm', m)])
            self.rstd_ops(L, m, 0, 1, 1, 1.0 / D)
            self.op('dve', lambda e, j=j, m=m: e.scalar_tensor_tensor(
                out=self.x[0:L, j, :], in0=self.x[0:L, j, :], scalar=sm[0:L, m, 1:2], in1=self.gt[0:L, g, :],
                op0=ALU.mult, op1=ALU.mult), r=[('x', j), ('sm', m), ('gt', g)], w=[('x', j)])
            self.op('pool', lambda e, j=j: e.dma_start(out=dst[st.seq, t0 + j * L:t0 + (j + 1) * L, :], in_=self.x[0:L, j, :]),
                    r=[('x', j)], w=[], dsem=('yout', j))
        if st.last:
            self.state_out(st)
        self.f32 = False

    def build(self):
        self.declare()
        self.setup()
        nst = self.T // 256
        for seq in range(self.NPS):
            tiles = [(0, 16, 1, True), (16, 120, 2, False)] + [(i * 256, 128, 2, False) for i in range(1, nst)]
            for ti, (t0, L, NS, f32) in enumerate(tiles):
                st = ST()
                st.kind, st.seq, st.L, st.NS, st.NT, st.f32 = 'p', seq, L, NS, L * NS, f32
                st.tok0 = st.pos0 = t0
                st.first, st.last = (ti == 0), (ti == len(tiles) - 1)
                self.run_st(st)
        for seq in range(self.NSS):
            st = ST()
            st.kind, st.seq, st.L, st.NS, st.NT, st.f32 = 's', seq, self.TS, 1, self.TS, False
            st.tok0, st.pos0 = 0, 1024
            st.first = st.last = True
            self.run_st(st)
        self.emit()
        return self.nc

    def emit(self):
        nc, S = self.nc, self.S
        S.finalize()
        sems = {e: self.es.enter_context(nc.semaphore('s_' + e)) for e in ENGS}
        dsems = {}
        for i, k in enumerate(S.dcount.keys()):
            dsems[k] = self.es.enter_context(nc.semaphore('d%d' % i))
        block = self.es.enter_context(nc.Block())

        @block.tensor
        def _(e):
            S.emit('pe', e, sems, dsems)

        @block.scalar
        def _(e):
            S.emit('act', e, sems, dsems)

        @block.vector
        def _(e):
            S.emit('dve', e, sems, dsems)

        @block.gpsimd
        def _(e):
            S.emit('pool', e, sems, dsems)
            for k, cnt in S.dcount.items():
                e.wait_ge(dsems[k], 16 * cnt)

        @block.sync
        def _(e):
            S.emit('sp', e, sems, dsems)

        self.es.close()


for _n in DUAL:
    setattr(Prog, _n, _Dual(_n))


def host_consts():
    c = {}
    c['idf'] = np.eye(128, dtype=np.float32)
    i = np.arange(128)
    c['tri'] = (i[:, None] <= i[None, :]).astype(np.float32)
    c['upper'] = (i[:, None] > i[None, :]).astype(np.float32)
    c['ones'] = np.ones((128, 128), np.float32)
    c['neg'] = np.where(i[:, None] > i[None, :], -30000.0, 0.0).astype(np.float32)
    lg = np.log1p(-np.exp2(-5.0 - np.arange(4, dtype=np.float64)))
    dl = (i[None, :] - i[:, None]).astype(np.float64)
    dret = np.zeros((128, 4, 128), np.float64)
    for h in range(4):
        dret[:, h, :] = np.where(dl >= 0, np.exp(lg[h] * np.maximum(dl, 0)), 0.0)
    c['dret'] = dret.astype(np.float32)
    rtab = np.zeros((128, 24), np.float64)
    for h in range(4):
        rtab[:, h] = np.exp(lg[h] * (i + 1))
        rtab[:, 4 + h] = np.exp(lg[h] * (127 - i))
        rtab[:, 8 + h] = np.exp(lg[h] * np.maximum(15 - i, 0))
        rtab[:, 12 + h] = np.exp(lg[h] * np.maximum(119 - i, 0))
    for cc in range(2):
        for hh in range(2):
            rtab[hh * 64:(hh + 1) * 64, 16 + cc] = np.exp(lg[2 * cc + hh] * 128)
            rtab[hh * 64:(hh + 1) * 64, 18 + cc] = np.exp(lg[2 * cc + hh] * 16)
            rtab[hh * 64:(hh + 1) * 64, 20 + cc] = np.exp(lg[2 * cc + hh] * 120)
    c['rtab'] = rtab.astype(np.float32)
    half = 32
    freqs = (10000.0 ** (-np.arange(half, dtype=np.float32) / half)).astype(np.float32)
    ang = np.arange(4096, dtype=np.float32)[:, None] * freqs[None, :]
    cos, sin = np.cos(ang).astype(np.float32), np.sin(ang).astype(np.float32)
    rope = np.zeros((4096, 512), np.float32)
    for h in range(8):
        sc = 1.0 if h < 4 else 0.125
        rope[:, h * 32:(h + 1) * 32] = cos * sc
        rope[:, 256 + h * 32:256 + (h + 1) * 32] = sin * sc
    c['rope'] = rope
    return c


def rep(a, n=128):
    return np.ascontiguousarray(np.broadcast_to(a[None], (n,) + a.shape)).astype(np.float32)


def shared_inputs(inp):
    f = lambda a: np.ascontiguousarray(np.asarray(a, dtype=np.float32))
    sh = host_consts()
    for k in ('ffn1_w_in', 'ffn1_w_out', 'w_in', 'w_out', 'ffn2_w_in', 'ffn2_w_out'):
        sh[k] = f(inp[k])
    sh['w2'] = f(inp['gla_w_gate2'])
    sh['wbg'], sh['wbs'], sh['wbr'] = f(inp['w_branch_gla']), f(inp['w_branch_ssd']), f(inp['w_branch_ret'])
    gains = np.stack([inp['norm_ffn1'][0], inp['norm_mix'][0], inp['norm_ffn2'][0],
                      inp['norm_ffn1'][1], inp['norm_mix'][1], inp['norm_ffn2'][1], inp['norm_final']])
    sh['gains'] = np.ascontiguousarray(np.broadcast_to(f(gains)[:, None, :], (7, 128, D)))
    sh['glab'] = rep(f(inp['gla_b_gate']))
    sh['gnorm'], sh['snorm'], sh['rnorm'] = rep(f(inp['gla_norm'])), rep(f(inp['ssd_norm'])), rep(f(inp['ret_norm']))
    sh['dtb'], sh['alog'], sh['dD'] = rep(f(inp['ssd_dt_bias'])), rep(f(inp['ssd_a_log'])), rep(f(inp['ssd_d']))
    cwt = f(inp['ssd_conv_w'])
    sh['cw'] = np.ascontiguousarray(cwt.reshape(2, 4, 6, 128).transpose(3, 0, 2, 1))
    sh['cb'] = np.ascontiguousarray(f(inp['ssd_conv_b']).reshape(2, 6, 128).transpose(2, 0, 1))
    return sh


_PROG_CACHE = {}


def run(inp, n_cores, NPS, NSS):
    f = lambda a: np.ascontiguousarray(np.asarray(a, dtype=np.float32))
    T = inp['x_prompt'].shape[1]
    key = (NPS, T, NSS)
    prog = Prog(NPS, T, NSS)
    nc = prog.build()
    sh = shared_inputs(inp)
    in_maps = []
    for i in range(n_cores):
        m = dict(sh)
        m['xp'] = f(inp['x_prompt'][i * NPS:(i + 1) * NPS])
        m['xs'] = f(inp['x_sample'][i * NSS:(i + 1) * NSS])
        m['sgla'] = f(inp['state_gla'][:, i * NSS:(i + 1) * NSS]).reshape(2, NSS, 256, 128)
        m['sssd'] = f(inp['state_ssd'][:, i * NSS:(i + 1) * NSS]).reshape(2, NSS, 512, 64)
        m['cconv'] = f(inp['cache_conv'][:, i * NSS:(i + 1) * NSS])
        m['sret'] = f(inp['state_ret'][:, i * NSS:(i + 1) * NSS]).reshape(2, NSS, 256, 128)
        in_maps.append(m)
    res = run_bass_kernel_spmd(nc, in_maps, core_ids=list(range(n_cores)))
    R = res.results
    cat = lambda k, ax: np.concatenate([np.asarray(r[k]) for r in R], axis=ax)
    Bp, Bs = NPS * n_cores, NSS * n_cores
    out = (
        cat('yp', 0), cat('ys', 0),
        cat('gla_p', 1).reshape(2, Bp, 4, 64, 128), cat('ssd_p', 1).reshape(2, Bp, 8, 64, 64),
        cat('conv_p', 1), cat('ret_p', 1).reshape(2, Bp, 4, 64, 128),
        cat('gla_s', 1).reshape(2, Bs, 4, 64, 128), cat('ssd_s', 1).reshape(2, Bs, 8, 64, 64),
        cat('conv_s', 1), cat('ret_s', 1).reshape(2, Bs, 4, 64, 128),
    )
    return tuple(np.ascontiguousarray(o, dtype=np.float32) for o in out)


def kernel(**inputs):
    return run(inputs, 8, 2, 2)
```

```python
import numpy as np
from contextlib import ExitStack
import concourse.bass as bass
import concourse.mybir as mybir
from concourse.bass_utils import run_bass_kernel_spmd

F32 = mybir.dt.float32
BF16 = mybir.dt.bfloat16
AF = mybir.ActivationFunctionType
ALU = mybir.AluOpType
AX = mybir.AxisListType

D = 1024
KC = 8
DFF = 2816
FC = 22
INC = 7448
EPS = 1e-6
(O_GQ, O_GK, O_GV, O_GR, O_GLR, O_MZ, O_XBC, O_DT, O_RQ, O_RK, O_RV, O_RG, O_GA, O_GB, O_GC) = (
    0, 256, 512, 1024, 1536, 1552, 2064, 2832, 2840, 3096, 3352, 3864, 4376, 5400, 6424)
NSLOT = 5
NSCR = 5
NSM = 8
ENGS = ('pe', 'act', 'dve', 'pool', 'sp')


class Sched:
    def __init__(self):
        self.ops = {e: [] for e in ENGS}
        self.last_w = {}
        self.readers = {}
        self.dcount = {}
        self.tag = ''

    def op(self, eng, fn, r=(), w=(), dsem=None, ndma=1):
        deps = {}

        def add(k2, v):
            if deps.get(k2, 0) < v:
                deps[k2] = v

        for k in r:
            t = self.last_w.get(k)
            if t is not None:
                add(t[:2], t[2])
        for k in w:
            t = self.last_w.get(k)
            if t is not None:
                add(t[:2], t[2])
            for k2, v in self.readers.get(k, {}).items():
                add(k2, v)
        idx = len(self.ops[eng]) + 1
        if dsem is None:
            tok = ('E', eng, idx)
        else:
            self.dcount[dsem] = self.dcount.get(dsem, 0) + ndma
            tok = ('D', dsem, 16 * self.dcount[dsem])
        for k in w:
            self.last_w[k] = tok
            self.readers[k] = {}
        for k in r:
            d = self.readers.setdefault(k, {})
            if d.get(tok[:2], 0) < tok[2]:
                d[tok[:2]] = tok[2]
        self.ops[eng].append(dict(fn=fn, deps=deps, dsem=dsem, flag=False, waits=[], tag=self.tag))

    def finalize(self):
        for eng in ENGS:
            waited = {}
            for op in self.ops[eng]:
                for (kind, name), val in op['deps'].items():
                    if kind == 'E' and name == 'pe' and eng == 'pe':
                        continue
                    if waited.get((kind, name), 0) >= val:
                        continue
                    waited[(kind, name)] = val
                    op['waits'].append((kind, name, val))
                    if kind == 'E':
                        self.ops[name][val - 1]['flag'] = True
        self.semval = {}
        for eng in ENGS:
            c = 0
            for i, op in enumerate(self.ops[eng]):
                if op['flag'] and op['dsem'] is None:
                    c += 1
                    self.semval[(eng, i + 1)] = c

    def emit(self, eng, e, sems, dsems):
        for op in self.ops[eng]:
            for (kind, name, val) in op['waits']:
                if kind == 'E':
                    e.wait_ge(sems[name], self.semval[(name, val)])
                else:
                    e.wait_ge(dsems[name], val)
            ins = op['fn'](e)
            if op['dsem'] is not None:
                for x in (ins if isinstance(ins, (list, tuple)) else [ins]):
                    x.then_inc(dsems[op['dsem']], 16)
            elif op['flag']:
                ins.then_inc(sems[eng], 1)


class ST:
    pass


class NS_:
    pass


class _Dual:
    def __init__(self, n):
        self.n = n

    def __get__(self, obj, cls):
        if obj is None:
            return self
        return getattr(obj.F if obj.f32 else obj.B, self.n)


DUAL = ('hb', 'hT', 'hid', 'glrT', 'qdT', 'kdT', 'v', 'P', 'yb', 'BCT', 'xdt', 'Mh', 'rqk', 'rqkT', 'rv', 'yT', 'mT', 'w2t', 'idb')


class Prog:
    DBG = 99
    def __init__(self, NPS, T, NSS, TS=16):
        self.NPS, self.T, self.NSS, self.TS = NPS, T, NSS, TS
        self.nc = bass.Bass("TRN2", target_bir_lowering=False)
        self.S = Sched()
        self.es = ExitStack()
        self.din = {}
        self.dout = {}
        self.ps_live = [False] * 8
        self.ps_next = 0
        self.scr_next = 0
        self.sm_next = 0
        self.ring_cnt = 0
        self.gt_cnt = 0
        self.rt_cnt = 0
        self.f32 = False
        self.slot_f32 = {}
        self.deferred = []

    def input_specs(self):
        NPS, T, NSS, TS = self.NPS, self.T, self.NSS, self.TS
        sp = {
            'xp': ([NPS, T, D], F32), 'xs': ([NSS, TS, D], F32),
            'sgla': ([2, NSS, 256, 128], F32), 'sssd': ([2, NSS, 512, 64], F32),
            'cconv': ([2, NSS, 3, 768], F32), 'sret': ([2, NSS, 256, 128], F32),
            'ffn1_w_in': ([2, D, 2 * DFF], F32), 'ffn1_w_out': ([2, DFF, D], F32),
            'w_in': ([2, D, INC], F32), 'w2': ([2, 16, 256], F32),
            'wbg': ([2, 512, D], F32), 'wbs': ([2, 512, D], F32), 'wbr': ([2, 512, D], F32),
            'w_out': ([2, D, D], F32),
            'ffn2_w_in': ([2, D, 2 * DFF], F32), 'ffn2_w_out': ([2, DFF, D], F32),
            'gains': ([7, 128, D], F32), 'glab': ([128, 2, 256], F32),
            'gnorm': ([128, 2, 512], F32), 'snorm': ([128, 2, 512], F32), 'rnorm': ([128, 2, 512], F32),
            'dtb': ([128, 2, 8], F32), 'alog': ([128, 2, 8], F32), 'dD': ([128, 2, 8], F32),
            'cw': ([128, 2, 6, 4], F32), 'cb': ([128, 2, 6], F32),
            'idf': ([128, 128], F32), 'tri': ([128, 128], F32), 'upper': ([128, 128], F32),
            'ones': ([128, 128], F32), 'neg': ([128, 128], F32), 'dret': ([128, 4, 128], F32),
            'rtab': ([128, 24], F32),
            'rope': ([4096, 512], F32),
        }
        return sp

    def output_specs(self):
        NPS, T, NSS, TS = self.NPS, self.T, self.NSS, self.TS
        return {
            'yp': ([NPS, T, D], F32), 'ys': ([NSS, TS, D], F32),
            'gla_p': ([2, NPS, 256, 128], F32), 'ssd_p': ([2, NPS, 512, 64], F32),
            'conv_p': ([2, NPS, 3, 768], F32), 'ret_p': ([2, NPS, 256, 128], F32),
            'gla_s': ([2, NSS, 256, 128], F32), 'ssd_s': ([2, NSS, 512, 64], F32),
            'conv_s': ([2, NSS, 3, 768], F32), 'ret_s': ([2, NSS, 256, 128], F32),
        }

    def sb(self, name, shape, dt):
        return self.es.enter_context(self.nc.sbuf_tensor('t_' + name, shape, dt))

    def declare(self):
        nc = self.nc
        for k, (shp, dt) in self.input_specs().items():
            self.din[k] = nc.dram_tensor(k, shp, dt, kind="ExternalInput").ap()
        for k, (shp, dt) in self.output_specs().items():
            self.dout[k] = nc.dram_tensor(k, shp, dt, kind="ExternalOutput").ap()
        self.wb = {}
        for k in ('ffn1_w_in', 'ffn1_w_out', 'w_in', 'w2', 'wbg', 'wbs', 'wbr', 'w_out', 'ffn2_w_in', 'ffn2_w_out'):
            shp = self.input_specs()[k][0]
            self.wb[k] = nc.dram_tensor('b_' + k, shp, BF16, kind="Internal").ap()
        self.es.enter_context(nc.allow_low_precision("bf16 matmul operands, fp32 accumulate"))
        self.es.enter_context(nc.allow_non_contiguous_dma("small strided state/conv transfers"))
        sb = self.sb
        self.x = sb('x', [128, 2, D], F32)
        self.hb = sb('hb', [128, 2, D], BF16)
        self.hT = sb('hT', [128, KC, 256], BF16)
        self.gt = sb('gt', [128, 2, D], F32)
        self.hid = sb('hid', [128, FC, 256], BF16)
        self.wr = sb('wr', [128, NSLOT, 4096], BF16)
        self.scr = sb('scr', [128, NSCR, 512], F32)
        self.sm = sb('sm', [128, NSM, 16], F32)
        self.glrT = sb('glrT', [16, 256], BF16)
        self.la = sb('la', [128, 2, 256], F32)
        self.ebT = sb('ebT', [128, 2, 256], F32)
        self.enbT = sb('enbT', [128, 2, 256], F32)
        self.ebLb = sb('ebLb', [128, 2, 256], F32)
        self.qdT = sb('qdT', [128, 2, 256], BF16)
        self.kdT = sb('kdT', [128, 2, 256], BF16)
        self.ks = sb('ks', [128, 2, 256], BF16)
        self.v = sb('v', [128, 2, 512], BF16)
        self.grs = sb('grs', [128, 2, 512], BF16)
        self.P = sb('P', [128, 4, 128], BF16)
        self.yb = sb('yb', [128, 512], BF16)
        self.zs = sb('zs', [128, 2, 512], BF16)
        self.xbcT = sb('xbcT', [128, 6, 260], F32)
        self.acc = sb('acc', [128, 6, 256], F32)
        self.cst = sb('cst', [128, 2, 6, 3], F32)
        self.BCT = sb('BCT', [128, 2, 256], BF16)
        self.xdt = sb('xdt', [128, 512], BF16)
        self.xdd = sb('xdd', [128, 512], BF16)
        self.xsD = sb('xsD', [128, 512], F32)
        self.Btm = sb('Btm', [128, 128], BF16)
        self.dtt = sb('dtt', [128, 2, 8], F32)
        self.at = sb('at', [128, 2, 8], F32)
        self.sdec = sb('sdec', [128, 40], F32)
        self.Dh = sb('Dh', [128, 8, 128], F32)
        self.GT = sb('GT', [128, 2, 128], F32)
        self.Mh = sb('Mh', [128, 8, 128], BF16)
        self.rt = sb('rt', [128, 2, 512], F32)
        self.rqk = sb('rqk', [128, 2, 512], BF16)
        self.rqkT = sb('rqkT', [128, 4, 256], BF16)
        self.rkd = sb('rkd', [128, 2, 256], BF16)
        self.rv = sb('rv', [128, 2, 512], BF16)
        self.rgs = sb('rgs', [128, 2, 512], BF16)
        self.yT = [sb('yT%d' % i, [128, 4, 256], BF16) for i in range(3)]
        self.mT = sb('mT', [128, KC, 256], BF16)
        self.Sf = [[sb('Sf%d_%d' % (l, k), [128, 256], F32) for k in range(3)] for l in range(2)]
        self.Sb = [[sb('Sb%d_%d' % (l, k), [128, 512], BF16) for k in range(3)] for l in range(2)]
        self.idf = sb('idf', [128, 128], F32)
        self.idb = sb('idb', [128, 128], BF16)
        self.tri = sb('tri', [128, 128], F32)
        self.upper = sb('upper', [128, 128], F32)
        self.ones = sb('ones', [128, 128], F32)
        self.neg = sb('neg', [128, 128], F32)
        self.dret = sb('dret', [128, 4, 128], F32)
        self.rtab = sb('rtab', [128, 24], F32)
        self.glab = sb('glab', [128, 2, 256], F32)
        self.gnorm = sb('gnorm', [128, 2, 512], F32)
        self.snorm = sb('snorm', [128, 2, 512], F32)
        self.rnorm = sb('rnorm', [128, 2, 512], F32)
        self.dtb = sb('dtb', [128, 2, 8], F32)
        self.At = sb('At', [128, 2, 8], F32)
        self.dD = sb('dD', [128, 2, 8], F32)
        self.cw = sb('cw', [128, 2, 6, 4], F32)
        self.cb = sb('cb', [128, 2, 6], F32)
        self.w2t = sb('w2t', [16, 2, 256], BF16)
        self.ps = [self.es.enter_context(nc.psum_tensor('ps%d' % i, [128, 512], F32)) for i in range(8)]
        names = ('hb', 'hT', 'hid', 'glrT', 'qdT', 'kdT', 'v', 'P', 'yb', 'BCT', 'xdt', 'Mh', 'rqk', 'rqkT', 'rv', 'yT', 'mT',
                 'w2t', 'idb')
        self.B = NS_()
        for n in names:
            setattr(self.B, n, self.__dict__.pop(n))
        Fs = NS_()
        Fs.hb = sb('f_hb', [128, 1, D], F32)
        Fs.hT = sb('f_hT', [128, KC, 16], F32)
        Fs.hid = sb('f_hid', [128, FC, 16], F32)
        Fs.glrT = sb('f_glrT', [16, 16], F32)
        Fs.qdT = sb('f_qdT', [128, 2, 16], F32)
        Fs.kdT = sb('f_kdT', [128, 2, 16], F32)
        Fs.v = sb('f_v', [128, 1, 512], F32)
        Fs.P = sb('f_P', [128, 4, 16], F32)
        Fs.yb = sb('f_yb', [128, 512], F32)
        Fs.BCT = sb('f_BCT', [128, 2, 16], F32)
        Fs.xdt = sb('f_xdt', [128, 512], F32)
        Fs.Mh = sb('f_Mh', [128, 8, 16], F32)
        Fs.rqk = sb('f_rqk', [128, 1, 512], F32)
        Fs.rqkT = sb('f_rqkT', [128, 4, 16], F32)
        Fs.rv = sb('f_rv', [128, 1, 512], F32)
        Fs.yT = [sb('f_yT%d' % i, [128, 4, 16], F32) for i in range(3)]
        Fs.mT = sb('f_mT', [128, KC, 16], F32)
        Fs.w2t = sb('f_w2t', [16, 2, 256], F32)
        Fs.idb = self.idf
        self.F = Fs

    def pa(self):
        for _ in range(8):
            b = self.ps_next
            self.ps_next = (self.ps_next + 1) % 8
            if not self.ps_live[b]:
                self.ps_live[b] = True
                return b
        raise RuntimeError("PSUM exhausted")

    def pf(self, b):
        self.ps_live[b] = False

    def psb(self, b):
        return self.ps[b][:, :] if self.f32 else self.ps[b][:].bitcast(BF16)

    @property
    def toff(self):
        return 64 if self.f32 else 128

    def sc(self):
        i = self.scr_next
        self.scr_next = (i + 1) % NSCR
        return i

    def smi(self):
        i = self.sm_next
        self.sm_next = (i + 1) % NSM
        return i

    def op(self, eng, fn, **k):
        f = self.f32

        def fn2(e, fn=fn, f=f):
            old = self.f32
            self.f32 = f
            try:
                return fn(e)
            finally:
                self.f32 = old

        self.S.op(eng, fn2, **k)

    def mm(self, out, lhsT, rhs, start, stop, r, w):
        self.S.op('pe', lambda e: e.matmul(out, lhsT, rhs, start=start, stop=stop), r=r, w=w)

    def tp(self, out, in_, ident, r, w):
        self.S.op('pe', lambda e: e.transpose(out=out, in_=in_, identity=ident), r=r, w=w)

    def ring(self, dmas, wkeys, nslots=2):
        s = self.ring_cnt % NSLOT
        if self.f32 and nslots == 1:
            self.ring_cnt += 1
            self.slot_f32[s] = 1
            keys = [('wr', s)]
        elif self.f32:
            if s == NSLOT - 1:
                self.ring_cnt += 1
                s = 0
            self.ring_cnt += 2
            self.slot_f32[s] = 2
            keys = [('wr', s), ('wr', s + 1)]
        else:
            self.ring_cnt += 1
            self.slot_f32[s] = False
            keys = [('wr', s)]
        flat = self.sv(s)

        def fn(e, dmas=dmas, flat=flat):
            return [e.dma_start(out=d(flat), in_=src) for d, src in dmas]

        self.S.op('sp', fn, r=list(wkeys), w=keys, dsem=('wr', s), ndma=len(dmas))
        return s

    def sv(self, s):
        if self.slot_f32.get(s) == 2:
            return self.wr[:, s:s + 2, :].rearrange("p a b -> p (a b)").bitcast(F32)
        if self.slot_f32.get(s) == 1:
            return self.wr[:, s, :].bitcast(F32)
        return self.wr[:, s, :]

    def sk(self, s):
        return [('wr', s), ('wr', s + 1)] if self.slot_f32.get(s) == 2 else [('wr', s)]

    def wsrc(self, name):
        return self.din[name] if self.f32 else self.wb[name]

    def wview(self, name, l):
        return self.wsrc(name)[l].rearrange("(kc p) n -> p kc n", p=128)

    def setup(self):
        S, din = self.S, self.din
        order = []
        for l in range(2):
            for k in ('ffn1_w_in', 'ffn1_w_out', 'w_in', 'w2', 'wbg', 'wbs', 'wbr', 'w_out', 'ffn2_w_in', 'ffn2_w_out'):
                order.append((k, l))
        for (k, l) in order:
            src, dst = din[k][l], self.wb[k][l]
            rows = src.shape[0]
            step = 128 if rows > 128 else rows
            pieces = [(r0, min(r0 + step, rows)) for r0 in range(0, rows, step)]

            def fn(e, src=src, dst=dst, pieces=pieces):
                return [e.dma_start(out=dst[a:b, :], in_=src[a:b, :]) for a, b in pieces]

            S.op('pool', fn, r=[], w=[('W', k, l)], dsem=('cast', k, l), ndma=len(pieces))
        cl = [(self.idf, 'idf'), (self.tri, 'tri'), (self.upper, 'upper'), (self.ones, 'ones'), (self.neg, 'neg'),
              (self.dret, 'dret'), (self.rtab, 'rtab'), (self.glab, 'glab'), (self.gnorm, 'gnorm'),
              (self.snorm, 'snorm'), (self.rnorm, 'rnorm'), (self.dtb, 'dtb'), (self.At, 'alog'), (self.dD, 'dD'),
              (self.cw, 'cw'), (self.cb, 'cb')]

        def fnc(e):
            return [e.dma_start(out=t[:], in_=din[n]) for t, n in cl]

        S.op('sp', fnc, r=[], w=[('const',)], dsem=('const',), ndma=len(cl))
        S.op('sp', lambda e: e.dma_start(out=self.w2t[:], in_=self.wb['w2'].rearrange("l k n -> k l n")),
             r=[('W', 'w2', 0), ('W', 'w2', 1)], w=[('w2t',)], dsem=('w2t',))
        S.op('dve', lambda e: e.tensor_copy(out=self.idb[:], in_=self.idf[:]), r=[('const',)], w=[('idb',)])
        S.op('sp', lambda e: e.dma_start(out=self.F.w2t[:], in_=din['w2'].rearrange("l k n -> k l n")),
             r=[], w=[('w2t',)], dsem=('w2f',))
        for l in range(2):
            for k in range(3):
                S.op('pool', lambda e, t=self.Sb[l][k]: e.memset(t[:, :], 0.0), r=[], w=[('Sb', l, k)])
        S.op('pool', lambda e: e.memset(self.acc[:, :, :], 0.0), r=[], w=[('acc', c) for c in range(6)])
        S.op('act', lambda e: e.activation(out=self.At[:], in_=self.At[:], func=AF.Exp), r=[('const',)], w=[('At',)])
        S.op('dve', lambda e: e.tensor_scalar(out=self.At[:], in0=self.At[:], scalar1=-1.0, scalar2=0.0, op0=ALU.mult, op1=ALU.add),
             r=[('At',)], w=[('At',)])

    def norm_T(self, st, gidx):
        self.S.tag = 'norm'
        L, NS = st.L, st.NS
        g = self.gt_cnt % 2
        self.gt_cnt += 1
        self.op('sp', lambda e: e.dma_start(out=self.gt[:, g, :], in_=self.din['gains'][gidx]),
                r=[], w=[('gt', g)], dsem=('gt', g))
        for j in range(NS):
            m = self.smi()
            sm = self.sm
            self.op('act', lambda e, j=j, m=m: e.activation(out=self.hb[0:L, j, :], in_=self.x[0:L, j, :], func=AF.Square,
                                                           accum_out=sm[0:L, m, 0:1]),
                    r=[('x', j)], w=[('hb', j), ('sm', m)])
            self.op('act', lambda e, m=m: e.activation(out=sm[0:L, m, 1:2], in_=sm[0:L, m, 0:1], func=AF.Ln, bias=EPS,
                                                      scale=1.0 / D), r=[('sm', m)], w=[('sm', m)])
            self.op('act', lambda e, m=m: e.activation(out=sm[0:L, m, 2:3], in_=sm[0:L, m, 1:2], func=AF.Exp, scale=-0.5),
                    r=[('sm', m)], w=[('sm', m)])
            self.op('dve', lambda e, j=j, m=m: e.scalar_tensor_tensor(
                out=self.hb[0:L, j, :], in0=self.x[0:L, j, :], scalar=sm[0:L, m, 2:3], in1=self.gt[0:L, g, :],
                op0=ALU.mult, op1=ALU.mult), r=[('x', j), ('sm', m), ('gt', g)], w=[('hb', j)])
            b = self.pa()
            pb = self.psb(b)
            to = self.toff
            for kc in range(KC):
                self.tp(pb[:, kc * to:kc * to + L], self.hb[0:L, j, kc * 128:(kc + 1) * 128], self.idb[0:L, 0:L],
                        r=[('hb', j), ('idb',)], w=[('ps', b)])
            self.op('dve', lambda e, j=j, pb=pb, to=to: e.tensor_copy(
                out=self.hT[:, :, j * L:(j + 1) * L], in_=pb[:, 0:KC * to].rearrange("p (k l) -> p k l", k=KC)[:, :, 0:L]),
                r=[('ps', b)], w=[('hT',)])
            self.pf(b)

    def ffn(self, st, l, which, gidx):
        L, NS, NT = st.L, st.NS, st.NT
        self.norm_T(st, gidx)
        self.S.tag = 'ffn_in'
        wi = self.wview('ffn%d_w_in' % which, l)
        wo = self.wsrc('ffn%d_w_out' % which)[l].rearrange("(c p) n -> p c n", p=128)
        kin = ('W', 'ffn%d_w_in' % which, l)
        kout = ('W', 'ffn%d_w_out' % which, l)
        for i in range(FC // 2):
            s = self.ring([
                (lambda f: f[:, 0:4096].rearrange("p (k n) -> p k n", k=KC)[:, :, 0:256], wi[:, :, 256 * i:256 * i + 256]),
                (lambda f: f[:, 0:4096].rearrange("p (k n) -> p k n", k=KC)[:, :, 256:512],
                 wi[:, :, DFF + 256 * i:DFF + 256 * i + 256])], [kin])
            pv = self.sv(s).rearrange("p (k n) -> p k n", k=KC)
            for mm_ in range(2):
                m = 2 * i + mm_
                b = self.pa()
                for half in range(2):
                    for kc in range(KC):
                        self.mm(self.ps[b][:, half * 256:half * 256 + NT],
                                pv[:, kc, half * 256 + mm_ * 128:half * 256 + mm_ * 128 + 128], self.hT[:, kc, 0:NT],
                                kc == 0, kc == KC - 1, r=[*self.sk(s), ('hT',)], w=[('ps', b)])
                t = self.sc()
                self.op('act', lambda e, b=b, t=t: e.activation(out=self.scr[:, t, 0:NT], in_=self.ps[b][:, 0:NT],
                                                               func=AF.Silu), r=[('ps', b)], w=[('scr', t)])
                self.op('dve', lambda e, b=b, t=t, m=m: e.tensor_tensor(
                    out=self.hid[:, m, 0:NT], in0=self.scr[:, t, 0:NT], in1=self.ps[b][:, 256:256 + NT], op=ALU.mult),
                    r=[('scr', t), ('ps', b)], w=[('hid', m)])
                self.pf(b)
        self.S.tag = 'ffn_out'
        banks = [[self.pa() for _ in range(2)] for _ in range(NS)]
        for i in range((FC + 3) // 4):
            c0 = 4 * i
            n = min(4, FC - c0)
            s = self.ring([(lambda f, n=n: f[:, 0:n * 1024].rearrange("p (c n) -> p c n", c=n), wo[:, c0:c0 + n, :])], [kout])
            pv = self.sv(s).rearrange("p (c n) -> p c n", c=4)
            for ci in range(n):
                c = c0 + ci
                for j in range(NS):
                    for half in range(2):
                        b = banks[j][half]
                        self.mm(self.ps[b][0:L, :], self.hid[:, c, j * L:(j + 1) * L], pv[:, ci, half * 512:(half + 1) * 512],
                                c == 0, c == FC - 1, r=[*self.sk(s), ('hid', c)], w=[('ps', b)])
        for j in range(NS):
            for half in range(2):
                b = banks[j][half]
                self.op('dve', lambda e, b=b, j=j, half=half: e.scalar_tensor_tensor(
                    out=self.x[0:L, j, half * 512:(half + 1) * 512], in0=self.ps[b][0:L, :], scalar=0.5,
                    in1=self.x[0:L, j, half * 512:(half + 1) * 512], op0=ALU.mult, op1=ALU.add),
                    r=[('ps', b), ('x', j)], w=[('x', j)])
                self.pf(b)

    def proj_tm(self, st, s, pv, c0, n, evac):
        L, NS = st.L, st.NS
        for j in range(NS):
            b = self.pa()
            for kc in range(KC):
                self.mm(self.ps[b][0:L, 0:n], self.hT[:, kc, j * L:(j + 1) * L], pv[:, kc, c0:c0 + n],
                        kc == 0, kc == KC - 1, r=[*self.sk(s), ('hT',)], w=[('ps', b)])
            evac(j, b)
            self.pf(b)

    def panel(self, l, c0, n):
        wi = self.wview('w_in', l)
        s = self.ring([(lambda f, n=n: f[:, 0:KC * n].rearrange("p (k n) -> p k n", k=KC), wi[:, :, c0:c0 + n])],
                      [('W', 'w_in', l)])
        return s, self.sv(s)[:, 0:KC * n].rearrange("p (k n) -> p k n", k=KC)

    def rstd_ops(self, L, m, col_in, col_out, n, scale, ncols=1):
        sm = self.sm
        self.op('act', lambda e: e.activation(out=sm[0:L, m, col_out:col_out + ncols], in_=sm[0:L, m, col_in:col_in + ncols],
                                              func=AF.Ln, bias=EPS, scale=scale), r=[('sm', m)], w=[('sm', m)])
        self.op('act', lambda e: e.activation(out=sm[0:L, m, col_out:col_out + ncols], in_=sm[0:L, m, col_out:col_out + ncols],
                                              func=AF.Exp, scale=-0.5), r=[('sm', m)], w=[('sm', m)])

    def group_norm_out(self, st, j, u, gate, ydst):
        L = st.L
        self.flush_y()
        sm, scr = self.sm, self.scr
        m = self.smi()
        t = self.sc()
        u3 = scr[0:L, u, :].rearrange("p (h d) -> p h d", h=4)
        t3 = scr[0:L, t, :].rearrange("p (h d) -> p h d", h=4)
        self.op('dve', lambda e: e.tensor_reduce(out=sm[0:L, m, 0:4], in_=u3, axis=AX.X, op=ALU.add),
                r=[('scr', u)], w=[('sm', m)])
        self.op('dve', lambda e: e.tensor_tensor(out=scr[0:L, t, :], in0=scr[0:L, u, :], in1=scr[0:L, u, :], op=ALU.mult),
                r=[('scr', u)], w=[('scr', t)])
        self.op('dve', lambda e: e.tensor_reduce(out=sm[0:L, m, 4:8], in_=t3, axis=AX.X, op=ALU.add),
                r=[('scr', t), ('sm', m)], w=[('sm', m)])
        self.op('dve', lambda e: e.tensor_scalar(out=sm[0:L, m, 0:4], in0=sm[0:L, m, 0:4], scalar1=1.0 / 128, scalar2=0.0,
                                                 op0=ALU.mult, op1=ALU.add), r=[('sm', m)], w=[('sm', m)])
        self.op('dve', lambda e: e.tensor_tensor(out=sm[0:L, m, 8:12], in0=sm[0:L, m, 0:4], in1=sm[0:L, m, 0:4], op=ALU.mult),
                r=[('sm', m)], w=[('sm', m)])
        self.op('dve', lambda e: e.scalar_tensor_tensor(out=sm[0:L, m, 4:8], in0=sm[0:L, m, 4:8], scalar=1.0 / 128,
                                                        in1=sm[0:L, m, 8:12], op0=ALU.mult, op1=ALU.subtract),
                r=[('sm', m)], w=[('sm', m)])
        self.rstd_ops(L, m, 4, 12, 4, 1.0, ncols=4)
        self.op('dve', lambda e: e.tensor_tensor(out=u3, in0=u3, in1=sm[0:L, m, 0:4].unsqueeze(2).to_broadcast([L, 4, 128]),
                                                 op=ALU.subtract), r=[('scr', u), ('sm', m)], w=[('scr', u)])
        self.op('dve', lambda e: e.tensor_tensor(out=u3, in0=u3, in1=sm[0:L, m, 12:16].unsqueeze(2).to_broadcast([L, 4, 128]),
                                                 op=ALU.mult), r=[('scr', u), ('sm', m)], w=[('scr', u)])
        gk, gap = gate
        self.op('dve', lambda e: e.tensor_tensor(out=self.yb[0:L, :], in0=scr[0:L, u, :], in1=gap, op=ALU.mult),
                r=[('scr', u), gk], w=[('yb',)])
        self.y_transpose(st, j, ydst)

    def y_transpose(self, st, j, ydst):
        self.deferred.append((st, j, ydst))

    def flush_y(self):
        for (st, j, ydst) in self.deferred:
            self.y_transpose_now(st, j, ydst)
        self.deferred = []

    def y_transpose_now(self, st, j, ydst):
        L = st.L
        b = self.pa()
        pb = self.psb(b)
        to = self.toff
        for c4 in range(4):
            self.tp(pb[:, c4 * to:c4 * to + L], self.yb[0:L, c4 * 128:(c4 + 1) * 128], self.idb[0:L, 0:L],
                    r=[('yb',), ('idb',)], w=[('ps', b)])
        yt = self.yT[ydst]
        self.op('dve', lambda e: e.tensor_copy(out=yt[:, :, j * L:(j + 1) * L],
                                               in_=pb[:, 0:4 * to].rearrange("p (k l) -> p k l", k=4)[:, :, 0:L]),
                r=[('ps', b)], w=[('yT', ydst)])
        self.pf(b)

    def mixer(self, st, l):
        L, NS, NT = st.L, st.NS, st.NT
        S = self.S
        scr, sm = self.scr, self.sm
        self.norm_T(st, 3 * l + 1)
        wi = self.wview('w_in', l)
        kW = ('W', 'w_in', l)
        Sg, Ss, Sr = self.Sf[l]
        Sgb, Ssb, Srb = self.Sb[l]
        self.S.tag = 'gla_proj'
        s0 = self.ring([
            (lambda f: f[:, 0:KC * 32].rearrange("p (k n) -> p k n", k=KC)[:, :, 0:16], wi[:, :, O_GLR:O_GLR + 16]),
            (lambda f: f[:, 0:KC * 32].rearrange("p (k n) -> p k n", k=KC)[:, :, 16:24], wi[:, :, O_DT:O_DT + 8])], [kW])
        pv0 = self.sv(s0)[:, 0:KC * 32].rearrange("p (k n) -> p k n", k=KC)
        b = self.pa()
        for kc in range(KC):
            self.mm(self.ps[b][0:16, 0:NT], pv0[:, kc, 0:16], self.hT[:, kc, 0:NT], kc == 0, kc == KC - 1,
                    r=[*self.sk(s0), ('hT',)], w=[('ps', b)])
        self.op('dve', lambda e, b=b: e.tensor_copy(out=self.glrT[0:16, 0:NT], in_=self.ps[b][0:16, 0:NT]),
                r=[('ps', b)], w=[('glrT',)])
        self.pf(b)

        def ev_dt(j, b):
            self.op('dve', lambda e: e.tensor_tensor(out=self.dtt[0:L, j, :], in0=self.ps[b][0:L, 0:8],
                                                     in1=self.dtb[0:L, l, :], op=ALU.add),
                    r=[('ps', b), ('const',)], w=[('dtt', j)])
            self.op('act', lambda e: e.activation(out=self.dtt[0:L, j, :], in_=self.dtt[0:L, j, :], func=AF.Exp),
                    r=[('dtt', j)], w=[('dtt', j)])
            self.op('act', lambda e: e.activation(out=self.dtt[0:L, j, :], in_=self.dtt[0:L, j, :], func=AF.Ln, bias=1.0),
                    r=[('dtt', j)], w=[('dtt', j)])
            self.op('dve', lambda e: e.tensor_tensor(out=self.at[0:L, j, :], in0=self.dtt[0:L, j, :], in1=self.At[0:L, l, :],
                                                     op=ALU.mult), r=[('dtt', j), ('At',)], w=[('at', j)])

        self.proj_tm(st, s0, pv0, 16, 8, ev_dt)
        for j in range(NS):
            b = self.pa()
            self.mm(self.ps[b][0:L, 0:256], self.glrT[0:16, j * L:(j + 1) * L], self.w2t[0:16, l, :], True, True,
                    r=[('glrT',), ('w2t',)], w=[('ps', b)])
            self.op('dve', lambda e, j=j, b=b: e.tensor_tensor(out=self.la[0:L, j, :], in0=self.ps[b][0:L, 0:256],
                                                              in1=self.glab[0:L, l, :], op=ALU.add),
                    r=[('ps', b), ('const',)], w=[('la', j)])
            self.pf(b)
            self.op('act', lambda e, j=j: e.activation(out=self.la[0:L, j, :], in_=self.la[0:L, j, :], func=AF.Exp, scale=-1.0),
                    r=[('la', j)], w=[('la', j)])
            self.op('act', lambda e, j=j: e.activation(out=self.la[0:L, j, :], in_=self.la[0:L, j, :], func=AF.Ln, bias=1.0),
                    r=[('la', j)], w=[('la', j)])
            self.op('dve', lambda e, j=j: e.tensor_scalar(out=self.la[0:L, j, :], in0=self.la[0:L, j, :], scalar1=-1.0 / 16.0,
                                                         scalar2=0.0, op0=ALU.mult, op1=ALU.add), r=[('la', j)], w=[('la', j)])
            b = self.pa()
            for c in range(2):
                self.mm(self.ps[b][:, c * 128:c * 128 + L], self.la[0:L, j, c * 128:(c + 1) * 128], self.tri[0:L, 0:L],
                        True, True, r=[('la', j), ('const',)], w=[('ps', b)])
            b2 = self.pa()
            self.mm(self.ps[b2][0:L, 0:256], self.upper[0:L, 0:L], self.la[0:L, j, :], True, True,
                    r=[('la', j), ('const',)], w=[('ps', b2)])
            pin = self.ps[b][:, 0:256].rearrange("p (c l) -> p c l", c=2)[:, :, 0:L]
            self.op('act', lambda e, j=j, pin=pin: e.activation(out=self.ebT[:, :, j * L:(j + 1) * L], in_=pin, func=AF.Exp),
                    r=[('ps', b)], w=[('ebT',)])
            self.op('act', lambda e, j=j, pin=pin: e.activation(out=self.enbT[:, :, j * L:(j + 1) * L], in_=pin, func=AF.Exp,
                                                               scale=-1.0), r=[('ps', b)], w=[('enbT',)])
            self.op('act', lambda e, j=j, b2=b2: e.activation(out=self.ebLb[0:L, j, :], in_=self.ps[b2][0:L, 0:256], func=AF.Exp),
                    r=[('ps', b2)], w=[('ebLb', j)])
            self.pf(b)
            self.pf(b2)
        s1, pv1 = self.panel(l, 0, 512)
        for c in range(4):
            b = self.pa()
            for kc in range(KC):
                self.mm(self.ps[b][:, 0:NT], pv1[:, kc, c * 128:(c + 1) * 128], self.hT[:, kc, 0:NT], kc == 0, kc == KC - 1,
                        r=[*self.sk(s1), ('hT',)], w=[('ps', b)])
            if c < 2:
                self.op('dve', lambda e, b=b, c=c: e.scalar_tensor_tensor(
                    out=self.qdT[:, c, 0:NT], in0=self.ps[b][:, 0:NT], scalar=0.125, in1=self.ebT[:, c, 0:NT],
                    op0=ALU.mult, op1=ALU.mult), r=[('ps', b), ('ebT',)], w=[('qdT',)])
            else:
                self.op('dve', lambda e, b=b, c=c: e.tensor_tensor(
                    out=self.kdT[:, c - 2, 0:NT], in0=self.ps[b][:, 0:NT], in1=self.enbT[:, c - 2, 0:NT], op=ALU.mult),
                    r=[('ps', b), ('enbT',)], w=[('kdT',)])
            self.pf(b)

        def ev_k(j, b):
            self.op('dve', lambda e: e.tensor_tensor(out=self.ks[0:L, j, :], in0=self.ps[b][0:L, 0:256],
                                                     in1=self.ebLb[0:L, j, :], op=ALU.mult),
                    r=[('ps', b), ('ebLb', j)], w=[('ks', j)])

        self.proj_tm(st, s1, pv1, 256, 256, ev_k)
        s2, pv2 = self.panel(l, O_GV, 512)

        def ev_v(j, b):
            self.op('act', lambda e: e.activation(out=self.v[0:L, j, :], in_=self.ps[b][0:L, :], func=AF.Copy),
                    r=[('ps', b)], w=[('v', j)])
            if st.f32:
                self.op('act', lambda e: e.activation(out=self.B.v[0:L, j, :], in_=self.ps[b][0:L, :], func=AF.Copy),
                        r=[('ps', b)], w=[('vb', j)])

        self.proj_tm(st, s2, pv2, 0, 512, ev_v)

        def ev_gate(dst, dkey, normt):
            def ev(j, b):
                t = self.sc()
                self.op('act', lambda e: e.activation(out=scr[0:L, t, :], in_=self.ps[b][0:L, :], func=AF.Silu),
                        r=[('ps', b)], w=[('scr', t)])
                self.op('dve', lambda e: e.tensor_tensor(out=dst[0:L, j, :], in0=scr[0:L, t, :], in1=normt[0:L, l, :],
                                                          op=ALU.mult), r=[('scr', t), ('const',)], w=[(dkey, j)])
            return ev

        s3, pv3 = self.panel(l, O_GR, 512)
        self.proj_tm(st, s3, pv3, 0, 512, ev_gate(self.grs, 'grs', self.gnorm))
        self.S.tag = 'gla_scan'
        for j in range(NS):
            tok = slice(j * L, (j + 1) * L)
            bSs = [self.pa(), self.pa()]
            for h in range(4):
                c, hh = h // 2, h % 2
                pr = slice(hh * 64, hh * 64 + 64)
                self.mm(self.ps[bSs[hh]][0:L, c * 128:c * 128 + L], self.kdT[pr, c, tok], self.qdT[pr, c, tok], True, True,
                        r=[('kdT',), ('qdT',)], w=[('ps', bSs[hh])])
            for hh in range(2):
                self.op('dve', lambda e, hh=hh, bq=bSs[hh]: e.tensor_tensor(
                    out=self.P[0:L, :, 0:L].rearrange("p (c h) l -> p c h l", h=2)[:, :, hh, :],
                    in0=self.ps[bq][0:L, 0:256].rearrange("p (c l) -> p c l", c=2)[:, :, 0:L],
                    in1=self.tri[0:L, 0:L].unsqueeze(1).to_broadcast([L, 2, L]), op=ALU.mult),
                    r=[('ps', bSs[hh]), ('const',)], w=[('P',)])
                self.pf(bSs[hh])
            bO = self.pa()
            for c in range(2):
                if not st.f32:
                    self.mm(self.ps[bO][0:L, c * 256:(c + 1) * 256], self.qdT[:, c, tok], Sgb[:, c * 256:(c + 1) * 256],
                            True, False, r=[('qdT',), ('Sb', l, 0)], w=[('ps', bO)])
                for hh in range(2):
                    h = 2 * c + hh
                    self.mm(self.ps[bO][0:L, h * 128:(h + 1) * 128], self.P[0:L, h, 0:L], self.v[0:L, j, h * 128:(h + 1) * 128],
                            bool(st.f32), True if st.f32 else hh == 1, r=[('P',), ('v', j)], w=[('ps', bO)])
            u = self.sc()
            self.op('act', lambda e, bO=bO, u=u: e.activation(out=scr[0:L, u, :], in_=self.ps[bO][0:L, :], func=AF.Copy),
                    r=[('ps', bO)], w=[('scr', u)])
            self.pf(bO)
            bU = self.pa()
            for h in range(4):
                c, hh = h // 2, h % 2
                self.mm(self.ps[bU][hh * 64:hh * 64 + 64, c * 128:(c + 1) * 128], self.ks[0:L, j, h * 64:(h + 1) * 64],
                        self.B.v[0:L, j, h * 128:(h + 1) * 128], True, True, r=[('ks', j), ('v', j), ('vb', j)], w=[('ps', bU)])
            for c in range(2):
                self.op('dve', lambda e, c=c, bU=bU, j=j: e.scalar_tensor_tensor(
                    out=Sg[:, c * 128:(c + 1) * 128], in0=Sg[:, c * 128:(c + 1) * 128],
                    scalar=self.ebT[:, c, (j + 1) * L - 1:(j + 1) * L], in1=self.ps[bU][:, c * 128:(c + 1) * 128],
                    op0=ALU.mult, op1=ALU.add), r=[('Sf', l, 0), ('ebT',), ('ps', bU)], w=[('Sf', l, 0)])
            self.pf(bU)
            self.sb_copy(l, 0)
            self.group_norm_out(st, j, u, (('grs', j), self.grs[0:L, j, :]), 0)

        self.S.tag = 'ssd_proj'
        s4, pv4 = self.panel(l, O_MZ, 512)

        def ev_z(j, b):
            self.op('act', lambda e: e.activation(out=self.zs[0:L, j, :], in_=self.ps[b][0:L, :], func=AF.Silu),
                    r=[('ps', b)], w=[('zs', j)])

        self.proj_tm(st, s4, pv4, 0, 512, ev_z)
        s5, pv5 = self.panel(l, O_XBC, 512)
        s6, pv6 = self.panel(l, O_XBC + 512, 256)
        for cc in range(6):
            b = self.pa()
            for kc in range(KC):
                lh = pv5[:, kc, cc * 128:(cc + 1) * 128] if cc < 4 else pv6[:, kc, (cc - 4) * 128:(cc - 3) * 128]
                self.mm(self.ps[b][:, 0:NT], lh, self.hT[:, kc, 0:NT], kc == 0, kc == KC - 1,
                        r=[*self.sk(s5 if cc < 4 else s6), ('hT',)], w=[('ps', b)])
            self.op('act', lambda e, b=b, cc=cc: e.activation(out=self.xbcT[:, cc, 3:3 + NT], in_=self.ps[b][:, 0:NT],
                                                             func=AF.Copy), r=[('ps', b)], w=[('xbcT', cc)])
            self.pf(b)
            self.op('dve', lambda e, cc=cc: e.tensor_copy(out=self.xbcT[:, cc, 0:3], in_=self.cst[:, l, cc, :]),
                    r=[('cst', l)], w=[('xbcT', cc)])
            self.op('dve', lambda e, cc=cc: e.tensor_scalar(
                out=self.acc[:, cc, 0:NT], in0=self.xbcT[:, cc, 0:NT], scalar1=self.cw[:, l, cc, 0:1],
                scalar2=self.cb[:, l, cc:cc + 1], op0=ALU.mult, op1=ALU.add),
                r=[('xbcT', cc), ('const',)], w=[('acc', cc)])
            for jj in range(1, 4):
                self.op('dve', lambda e, cc=cc, jj=jj: e.scalar_tensor_tensor(
                    out=self.acc[:, cc, 0:NT], in0=self.xbcT[:, cc, jj:jj + NT], scalar=self.cw[:, l, cc, jj:jj + 1],
                    in1=self.acc[:, cc, 0:NT], op0=ALU.mult, op1=ALU.add),
                    r=[('xbcT', cc), ('const',), ('acc', cc)], w=[('acc', cc)])
            if cc < 4:
                self.op('act', lambda e, cc=cc: e.activation(out=self.acc[:, cc, 0:NT], in_=self.acc[:, cc, 0:NT], func=AF.Silu),
                        r=[('acc', cc)], w=[('acc', cc)])
            else:
                self.op('act', lambda e, cc=cc: e.activation(out=self.BCT[:, cc - 4, 0:NT], in_=self.acc[:, cc, 0:NT],
                                                            func=AF.Silu), r=[('acc', cc)], w=[('BCT',)])
                if st.f32 and cc == 4:
                    self.op('act', lambda e, cc=cc: e.activation(out=self.B.BCT[:, 0, 0:NT], in_=self.acc[:, cc, 0:NT],
                                                                func=AF.Silu), r=[('acc', cc)], w=[('BCTb',)])
        if st.last:
            dst = self.dout['conv_' + st.kind][l, st.seq]
            self.op('pool', lambda e, dst=dst: [e.dma_start(
                out=dst[:, cc * 128:(cc + 1) * 128].rearrange("t p -> p t"), in_=self.xbcT[:, cc, NT:NT + 3]) for cc in range(6)],
                r=[('xbcT', cc) for cc in range(6)], w=[], dsem=('st', l, 3), ndma=6)
        self.op('dve', lambda e: e.tensor_copy(out=self.cst[:, l, :, :], in_=self.xbcT[:, :, NT:NT + 3]),
                r=[('xbcT', cc) for cc in range(6)], w=[('cst', l)])
        self.S.tag = 'ssd_scan'
        sd = self.sdec
        for j in range(NS):
            tok = slice(j * L, (j + 1) * L)
            bd = self.pa()
            self.mm(self.ps[bd][0:L, 0:8], self.tri[0:L, 0:L], self.at[0:L, j, :], True, True, r=[('at', j), ('const',)],
                    w=[('ps', bd)])
            self.mm(self.ps[bd][0:L, 8:16], self.upper[0:L, 0:L], self.at[0:L, j, :], True, True, r=[('at', j), ('const',)],
                    w=[('ps', bd)])
            self.mm(self.ps[bd][:, 16:24], self.ones[0:L, :], self.at[0:L, j, :], True, True, r=[('at', j), ('const',)],
                    w=[('ps', bd)])
            self.op('dve', lambda e, bd=bd: e.tensor_scalar(out=sd[0:L, 0:8], in0=self.ps[bd][0:L, 0:8], scalar1=-1.0,
                                                           scalar2=0.0, op0=ALU.mult, op1=ALU.add), r=[('ps', bd)], w=[('sd', 0)])
            self.op('act', lambda e, bd=bd: e.activation(out=sd[0:L, 8:24], in_=self.ps[bd][0:L, 0:16], func=AF.Exp),
                    r=[('ps', bd)], w=[('sd', 1)])
            for g in range(2):
                self.op('act', lambda e, bd=bd, g=g: e.activation(
                    out=sd[g * 64:(g + 1) * 64, 24:28], in_=self.ps[bd][g * 64:(g + 1) * 64, 16 + 4 * g:20 + 4 * g], func=AF.Exp),
                    r=[('ps', bd)], w=[('sd', 2)])
            self.pf(bd)
            bX = self.pa()
            Lt = L if L > 64 else 128
            for c4 in range(4):
                self.tp(self.ps[bX][0:Lt, c4 * 128:(c4 + 1) * 128], self.acc[:, c4, j * L:j * L + Lt], self.idf[:, :],
                        r=[('acc', c4), ('const',)], w=[('ps', bX)])
            px3 = self.ps[bX][0:L, :].rearrange("p (h d) -> p h d", h=8)
            self.op('dve', lambda e, j=j, px3=px3: e.tensor_tensor(
                out=self.xdt[0:L, :].rearrange("p (h d) -> p h d", h=8), in0=px3,
                in1=self.dtt[0:L, j, :].unsqueeze(2).to_broadcast([L, 8, 64]), op=ALU.mult),
                r=[('ps', bX), ('dtt', j)], w=[('xdt',)])
            self.op('dve', lambda e, px3=px3: e.tensor_tensor(
                out=self.xsD[0:L, :].rearrange("p (h d) -> p h d", h=8), in0=px3,
                in1=self.dD[0:L, l, :].unsqueeze(2).to_broadcast([L, 8, 64]), op=ALU.mult),
                r=[('ps', bX), ('const',)], w=[('xsD',)])
            self.pf(bX)
            self.op('dve', lambda e: e.tensor_tensor(
                out=self.xdd[0:L, :].rearrange("p (h d) -> p h d", h=8), in0=self.xdt[0:L, :].rearrange("p (h d) -> p h d", h=8),
                in1=sd[0:L, 16:24].unsqueeze(2).to_broadcast([L, 8, 64]), op=ALU.mult),
                r=[('xdt',), ('sd', 1)], w=[('xdd',)])
            bB = self.pa()
            pbB = self.ps[bB][:].bitcast(BF16)
            self.tp(pbB[0:L, 0:128], self.B.BCT[:, 0, tok], self.B.idb[:, :], r=[('BCT',), ('BCTb',), ('idb',)], w=[('ps', bB)])
            self.op('dve', lambda e, pbB=pbB: e.tensor_copy(out=self.Btm[0:L, :], in_=pbB[0:L, 0:128]),
                    r=[('ps', bB)], w=[('Btm',)])
            self.pf(bB)
            for half in range(2):
                bD = self.pa()
                for hq in range(4):
                    h = 4 * half + hq
                    self.mm(self.ps[bD][0:L, hq * 128:hq * 128 + L], self.at[0:L, j, h:h + 1].to_broadcast([L, L]),
                            self.tri[0:L, 0:L], True, False, r=[('at', j), ('const',)], w=[('ps', bD)])
                    self.mm(self.ps[bD][0:L, hq * 128:hq * 128 + L], self.idf[0:L, 0:L], self.neg[0:L, 0:L], False, True,
                            r=[('const',)], w=[('ps', bD)])
                for hq in range(4):
                    h = 4 * half + hq
                    self.op('act', lambda e, bD=bD, hq=hq, h=h: e.activation(
                        out=self.Dh[0:L, h, 0:L], in_=self.ps[bD][0:L, hq * 128:hq * 128 + L], func=AF.Exp,
                        bias=sd[0:L, h:h + 1]), r=[('ps', bD), ('sd', 0)], w=[('Dh', half)])
                self.pf(bD)
            for g in range(2):
                pr = slice(g * 64, g * 64 + 64)
                bG = self.pa()
                self.mm(self.ps[bG][0:L, 0:L], self.BCT[pr, 0, tok], self.BCT[pr, 1, tok], True, True,
                        r=[('BCT',)], w=[('ps', bG)])
                self.op('act', lambda e, bG=bG, g=g: e.activation(out=self.GT[0:L, g, 0:L], in_=self.ps[bG][0:L, 0:L], func=AF.Copy),
                        r=[('ps', bG)], w=[('GT',)])
                self.pf(bG)
            for g in range(2):
                self.op('dve', lambda e, g=g: e.tensor_tensor(
                    out=self.Mh[0:L, 4 * g:4 * g + 4, 0:L], in0=self.Dh[0:L, 4 * g:4 * g + 4, 0:L],
                    in1=self.GT[0:L, g, 0:L].unsqueeze(1).to_broadcast([L, 4, L]), op=ALU.mult),
                    r=[('Dh', g), ('GT',)], w=[('Mh', g)])
            bO = self.pa()
            for h in range(8):
                self.mm(self.ps[bO][0:L, h * 64:(h + 1) * 64], self.Mh[0:L, h, 0:L], self.xdt[0:L, h * 64:(h + 1) * 64],
                        True, True, r=[('Mh', h // 4), ('xdt',)], w=[('ps', bO)])
            self.flush_y()
            u = self.sc()
            if st.f32:
                self.op('act', lambda e, bO=bO, u=u: e.activation(out=scr[0:L, u, :], in_=self.ps[bO][0:L, :], func=AF.Copy),
                        r=[('ps', bO)], w=[('scr', u)])
            else:
                bI = self.pa()
                self.mm(self.ps[bI][0:L, :], self.BCT[:, 1, tok], Ssb[:, :], True, True,
                        r=[('BCT',), ('Sb', l, 1)], w=[('ps', bI)])
                self.op('dve', lambda e, bI=bI, u=u: e.tensor_tensor(
                    out=scr[0:L, u, :].rearrange("p (h d) -> p h d", h=8),
                    in0=self.ps[bI][0:L, :].rearrange("p (h d) -> p h d", h=8),
                    in1=sd[0:L, 8:16].unsqueeze(2).to_broadcast([L, 8, 64]), op=ALU.mult),
                    r=[('ps', bI), ('sd', 1)], w=[('scr', u)])
                self.pf(bI)
                self.op('dve', lambda e, bO=bO, u=u: e.tensor_tensor(out=scr[0:L, u, :], in0=scr[0:L, u, :],
                                                                    in1=self.ps[bO][0:L, :], op=ALU.add),
                        r=[('ps', bO), ('scr', u)], w=[('scr', u)])
            self.pf(bO)
            self.op('dve', lambda e, u=u: e.tensor_tensor(out=scr[0:L, u, :], in0=scr[0:L, u, :], in1=self.xsD[0:L, :],
                                                          op=ALU.add), r=[('scr', u), ('xsD',)], w=[('scr', u)])
            self.op('dve', lambda e, u=u, j=j: e.tensor_tensor(out=scr[0:L, u, :], in0=scr[0:L, u, :], in1=self.zs[0:L, j, :],
                                                               op=ALU.mult), r=[('scr', u), ('zs', j)], w=[('scr', u)])
            m = self.smi()
            t = self.sc()
            self.op('dve', lambda e, u=u, t=t, m=m: e.scalar_tensor_tensor(
                out=scr[0:L, t, :], in0=scr[0:L, u, :], scalar=1.0, in1=scr[0:L, u, :], op0=ALU.mult, op1=ALU.mult,
                accum_out=sm[0:L, m, 0:1]), r=[('scr', u)], w=[('scr', t), ('sm', m)])
            self.rstd_ops(L, m, 0, 1, 1, 1.0 / 512)
            self.op('dve', lambda e, u=u, m=m: e.scalar_tensor_tensor(
                out=self.yb[0:L, :], in0=scr[0:L, u, :], scalar=sm[0:L, m, 1:2], in1=self.snorm[0:L, l, :],
                op0=ALU.mult, op1=ALU.mult), r=[('scr', u), ('sm', m), ('const',)], w=[('yb',)])
            self.y_transpose(st, j, 1)
            bU = self.pa()
            for g in range(2):
                self.mm(self.ps[bU][g * 64:(g + 1) * 64, 0:256], self.Btm[0:L, g * 64:(g + 1) * 64],
                        self.xdd[0:L, g * 256:(g + 1) * 256], True, True, r=[('Btm',), ('xdd',)], w=[('ps', bU)])
            self.op('dve', lambda e: e.tensor_tensor(
                out=Ss[:, :].rearrange("p (h d) -> p h d", h=4), in0=Ss[:, :].rearrange("p (h d) -> p h d", h=4),
                in1=sd[:, 24:28].unsqueeze(2).to_broadcast([128, 4, 64]), op=ALU.mult),
                r=[('Sf', l, 1), ('sd', 2)], w=[('Sf', l, 1)])
            self.op('dve', lambda e, bU=bU: e.tensor_tensor(out=Ss[:, :], in0=Ss[:, :], in1=self.ps[bU][:, 0:256], op=ALU.add),
                    r=[('Sf', l, 1), ('ps', bU)], w=[('Sf', l, 1)])
            self.pf(bU)
            self.sb_copy(l, 1)

        self.S.tag = 'ret_proj'
        s7, pv7 = self.panel(l, O_RQ, 512)
        rt = self.rt
        rk_off = {128: 4, 16: 8, 120: 12}[L]
        gl_off = {128: 16, 16: 18, 120: 20}[L]

        def ev_rope(j, b):
            q = self.rt_cnt % 2
            self.rt_cnt += 1
            p0 = st.pos0 + j * L
            self.op('sp', lambda e: e.dma_start(out=rt[0:L, q, :], in_=self.din['rope'][p0:p0 + L, :]), r=[], w=[('rt', q)],
                    dsem=('rt', q))
            p4 = self.ps[b][0:L, :].rearrange("p (h t d) -> p h t d", h=8, t=2)
            x1, x2 = p4[:, :, 0, :], p4[:, :, 1, :]
            cos = rt[0:L, q, 0:256].rearrange("p (h d) -> p h d", h=8)
            sin = rt[0:L, q, 256:512].rearrange("p (h d) -> p h d", h=8)
            o4 = self.rqk[0:L, j, :].rearrange("p (h t d) -> p h t d", h=8, t=2)
            ta, tb = self.sc(), self.sc()
            va = scr[0:L, ta, 0:256].rearrange("p (h d) -> p h d", h=8)
            vb = scr[0:L, ta, 256:512].rearrange("p (h d) -> p h d", h=8)
            vc = scr[0:L, tb, 0:256].rearrange("p (h d) -> p h d", h=8)
            vd = scr[0:L, tb, 256:512].rearrange("p (h d) -> p h d", h=8)
            for (o, i0, i1) in ((va, x1, cos), (vb, x2, sin), (vc, x1, sin), (vd, x2, cos)):
                tk = ta if (o is va or o is vb) else tb
                self.op('dve', lambda e, o=o, i0=i0, i1=i1: e.tensor_tensor(out=o, in0=i0, in1=i1, op=ALU.mult),
                        r=[('ps', b), ('rt', q)], w=[('scr', tk)])
            self.op('dve', lambda e: e.tensor_tensor(out=o4[:, :, 0, :], in0=va, in1=vb, op=ALU.subtract),
                    r=[('scr', ta)], w=[('rqk', j)])
            self.op('dve', lambda e: e.tensor_tensor(out=o4[:, :, 1, :], in0=vc, in1=vd, op=ALU.add),
                    r=[('scr', tb)], w=[('rqk', j)])

        self.proj_tm(st, s7, pv7, 0, 512, ev_rope)
        for j in range(NS):
            bT = self.pa()
            pb = self.psb(bT)
            to = self.toff
            for c4 in range(4):
                self.tp(pb[:, c4 * to:c4 * to + L], self.rqk[0:L, j, c4 * 128:(c4 + 1) * 128], self.idb[0:L, 0:L],
                        r=[('rqk', j), ('idb',)], w=[('ps', bT)])
            self.op('dve', lambda e, j=j, pb=pb, to=to: e.tensor_copy(
                out=self.rqkT[:, :, j * L:(j + 1) * L], in_=pb[:, 0:4 * to].rearrange("p (k l) -> p k l", k=4)[:, :, 0:L]),
                r=[('ps', bT)], w=[('rqkT',)])
            self.pf(bT)
            self.op('dve', lambda e, j=j: e.tensor_tensor(
                out=self.rkd[0:L, j, :].rearrange("p (h d) -> p h d", h=4),
                in0=self.rqk[0:L, j, 256:512].rearrange("p (h d) -> p h d", h=4),
                in1=self.rtab[0:L, rk_off:rk_off + 4].unsqueeze(2).to_broadcast([L, 4, 64]), op=ALU.mult),
                r=[('rqk', j), ('const',)], w=[('rkd', j)])
        s8, pv8 = self.panel(l, O_RV, 512)

        def ev_rv(j, b):
            self.op('act', lambda e: e.activation(out=self.rv[0:L, j, :], in_=self.ps[b][0:L, :], func=AF.Copy),
                    r=[('ps', b)], w=[('rv', j)])
            if st.f32:
                self.op('act', lambda e: e.activation(out=self.B.rv[0:L, j, :], in_=self.ps[b][0:L, :], func=AF.Copy),
                        r=[('ps', b)], w=[('rvb', j)])

        self.proj_tm(st, s8, pv8, 0, 512, ev_rv)
        s9, pv9 = self.panel(l, O_RG, 512)
        self.S.tag2 = 0
        self.proj_tm(st, s9, pv9, 0, 512, ev_gate(self.rgs, 'rgs', self.rnorm))
        self.S.tag = "ret_scan"
        for j in range(NS):
            tok = slice(j * L, (j + 1) * L)
            bSs = [self.pa(), self.pa()]
            for h in range(4):
                c, hh = h // 2, h % 2
                pr = slice(hh * 64, hh * 64 + 64)
                self.mm(self.ps[bSs[hh]][0:L, c * 128:c * 128 + L], self.rqkT[pr, 2 + c, tok], self.rqkT[pr, c, tok], True, True,
                        r=[('rqkT',)], w=[('ps', bSs[hh])])
            for hh in range(2):
                self.op('dve', lambda e, hh=hh, bq=bSs[hh]: e.tensor_tensor(
                    out=self.P[0:L, :, 0:L].rearrange("p (c h) l -> p c h l", h=2)[:, :, hh, :],
                    in0=self.ps[bq][0:L, 0:256].rearrange("p (c l) -> p c l", c=2)[:, :, 0:L],
                    in1=self.dret[0:L, :, 0:L].rearrange("p (c h) l -> p c h l", h=2)[:, :, hh, :], op=ALU.mult),
                    r=[('ps', bSs[hh]), ('const',)], w=[('P',)])
                self.pf(bSs[hh])
            bO = self.pa()
            for h in range(4):
                self.mm(self.ps[bO][0:L, h * 128:(h + 1) * 128], self.P[0:L, h, 0:L], self.rv[0:L, j, h * 128:(h + 1) * 128],
                        True, True, r=[('P',), ('rv', j)], w=[('ps', bO)])
            u = self.sc()
            if st.f32:
                self.op('act', lambda e, bO=bO, u=u: e.activation(out=scr[0:L, u, :], in_=self.ps[bO][0:L, :], func=AF.Copy),
                        r=[('ps', bO)], w=[('scr', u)])
            else:
                bI = self.pa()
                for c in range(2):
                    self.mm(self.ps[bI][0:L, c * 256:(c + 1) * 256], self.rqkT[:, c, tok], Srb[:, c * 256:(c + 1) * 256],
                            True, True, r=[('rqkT',), ('Sb', l, 2)], w=[('ps', bI)])
                self.op('dve', lambda e, bI=bI, u=u: e.tensor_tensor(
                    out=scr[0:L, u, :].rearrange("p (h d) -> p h d", h=4),
                    in0=self.ps[bI][0:L, :].rearrange("p (h d) -> p h d", h=4),
                    in1=self.rtab[0:L, 0:4].unsqueeze(2).to_broadcast([L, 4, 128]), op=ALU.mult),
                    r=[('ps', bI), ('const',)], w=[('scr', u)])
                self.pf(bI)
                self.op('dve', lambda e, bO=bO, u=u: e.tensor_tensor(out=scr[0:L, u, :], in0=scr[0:L, u, :],
                                                                    in1=self.ps[bO][0:L, :], op=ALU.add),
                        r=[('ps', bO), ('scr', u)], w=[('scr', u)])
            self.pf(bO)
            bU = self.pa()
            for h in range(4):
                c, hh = h // 2, h % 2
                self.mm(self.ps[bU][hh * 64:hh * 64 + 64, c * 128:(c + 1) * 128], self.rkd[0:L, j, h * 64:(h + 1) * 64],
                        self.B.rv[0:L, j, h * 128:(h + 1) * 128], True, True, r=[('rkd', j), ('rv', j), ('rvb', j)], w=[('ps', bU)])
            for c in range(2):
                self.op('dve', lambda e, c=c, bU=bU: e.scalar_tensor_tensor(
                    out=Sr[:, c * 128:(c + 1) * 128], in0=Sr[:, c * 128:(c + 1) * 128],
                    scalar=self.rtab[:, gl_off + c:gl_off + c + 1], in1=self.ps[bU][:, c * 128:(c + 1) * 128],
                    op0=ALU.mult, op1=ALU.add), r=[('Sf', l, 2), ('const',), ('ps', bU)], w=[('Sf', l, 2)])
            self.pf(bU)
            self.sb_copy(l, 2)
            self.group_norm_out(st, j, u, (('rgs', j), self.rgs[0:L, j, :]), 2)

        self.flush_y()
        self.S.tag = 'merge'
        wbr = [self.wview(n, l) for n in ('wbg', 'wbs', 'wbr')]
        W = 128 if st.f32 else 256
        ndc = W // 128
        for pp in range(D // W):
            sA = self.ring([
                (lambda f: f[:, 0:KC * 2 * W].rearrange("p (k n) -> p k n", k=KC)[:, :, 0:W], wi[:, :, O_GA + pp * W:O_GA + (pp + 1) * W]),
                (lambda f: f[:, 0:KC * 2 * W].rearrange("p (k n) -> p k n", k=KC)[:, :, W:2 * W], wi[:, :, O_GB + pp * W:O_GB + (pp + 1) * W])],
                [kW], nslots=1)
            sB = self.ring([(lambda f: f[:, 0:KC * W].rearrange("p (k n) -> p k n", k=KC), wi[:, :, O_GC + pp * W:O_GC + (pp + 1) * W])],
                           [kW], nslots=1)
            sC = self.ring([(lambda f, xi=xi: f[:, xi * 4 * W:(xi + 1) * 4 * W].rearrange("p (k n) -> p k n", k=4),
                             wbr[xi][:, :, pp * W:(pp + 1) * W]) for xi in range(3)],
                           [('W', 'wbg', l), ('W', 'wbs', l), ('W', 'wbr', l)], nslots=1)
            pA = self.sv(sA)[:, 0:KC * 2 * W].rearrange("p (k n) -> p k n", k=KC)
            pB = self.sv(sB)[:, 0:KC * W].rearrange("p (k n) -> p k n", k=KC)
            pC = self.sv(sC)[:, 0:12 * W].rearrange("p (x k n) -> p x k n", x=3, k=4)
            for dc in range(ndc):
                c = ndc * pp + dc
                prs = []
                for xi in range(3):
                    b = self.pa()
                    for kc in range(KC):
                        lh = pA[:, kc, xi * W + dc * 128:xi * W + dc * 128 + 128] if xi < 2 else pB[:, kc, dc * 128:dc * 128 + 128]
                        self.mm(self.ps[b][:, 0:NT], lh, self.hT[:, kc, 0:NT], kc == 0, kc == KC - 1,
                                r=[*self.sk(sA if xi < 2 else sB), ('hT',)], w=[('ps', b)])
                    for kc in range(4):
                        self.mm(self.ps[b][:, 256:256 + NT], pC[:, xi, kc, dc * 128:dc * 128 + 128], self.yT[xi][:, kc, 0:NT],
                                kc == 0, kc == 3, r=[*self.sk(sC), ('yT', xi)], w=[('ps', b)])
                    t = self.sc()
                    self.op('act', lambda e, b=b, t=t: e.activation(out=scr[:, t, 0:NT], in_=self.ps[b][:, 0:NT], func=AF.Tanh,
                                                                   scale=0.5), r=[('ps', b)], w=[('scr', t)])
                    self.op('dve', lambda e, b=b, t=t: e.scalar_tensor_tensor(
                        out=scr[:, t, 256:256 + NT], in0=scr[:, t, 0:NT], scalar=1.0, in1=self.ps[b][:, 256:256 + NT],
                        op0=ALU.add, op1=ALU.mult), r=[('ps', b), ('scr', t)], w=[('scr', t)])
                    self.pf(b)
                    prs.append(t)
                t0, t1, t2 = prs
                self.op('dve', lambda e, t0=t0, t1=t1: e.tensor_tensor(out=scr[:, t0, 256:256 + NT], in0=scr[:, t0, 256:256 + NT],
                                                                       in1=scr[:, t1, 256:256 + NT], op=ALU.add),
                        r=[('scr', t0), ('scr', t1)], w=[('scr', t0)])
                self.op('dve', lambda e, t0=t0, t2=t2, c=c: e.tensor_tensor(out=self.mT[:, c, 0:NT], in0=scr[:, t0, 256:256 + NT],
                                                                            in1=scr[:, t2, 256:256 + NT], op=ALU.add),
                        r=[('scr', t0), ('scr', t2)], w=[('mT', c)])
        self.S.tag = 'wout'
        wo = self.wview('w_out', l)
        for half in range(2):
            sO = self.ring([(lambda f: f[:, 0:4096].rearrange("p (k n) -> p k n", k=KC), wo[:, :, half * 512:(half + 1) * 512])],
                           [('W', 'w_out', l)])
            pO = self.sv(sO).rearrange("p (k n) -> p k n", k=KC)
            for j in range(NS):
                b = self.pa()
                for kc in range(KC):
                    self.mm(self.ps[b][0:L, :], self.mT[:, kc, j * L:(j + 1) * L], pO[:, kc, :], kc == 0, kc == KC - 1,
                            r=[*self.sk(sO), ('mT', kc)], w=[('ps', b)])
                self.op('dve', lambda e, b=b, j=j, half=half: e.scalar_tensor_tensor(
                    out=self.x[0:L, j, half * 512:(half + 1) * 512], in0=self.ps[b][0:L, :], scalar=0.5,
                    in1=self.x[0:L, j, half * 512:(half + 1) * 512], op0=ALU.mult, op1=ALU.add),
                    r=[('ps', b), ('x', j)], w=[('x', j)])
                self.pf(b)

    def sb_copy(self, l, k):
        Sf, Sb = self.Sf[l][k], self.Sb[l][k]
        for q in range(2):
            pr = slice(q * 64, q * 64 + 64)
            if k == 1:
                self.op('dve', lambda e, pr=pr, q=q: e.tensor_copy(out=Sb[pr, q * 256:(q + 1) * 256], in_=Sf[pr, :]),
                        r=[('Sf', l, k)], w=[('Sb', l, k)])
            else:
                self.op('dve', lambda e, pr=pr, q=q: e.tensor_copy(
                    out=Sb[pr, :].rearrange("p (c h d) -> p c h d", c=2, h=2)[:, :, q, :],
                    in_=Sf[pr, :].rearrange("p (c d) -> p c d", c=2)), r=[('Sf', l, k)], w=[('Sb', l, k)])

    def state_init(self, st):
        for l in range(2):
            Sg, Ss, Sr = self.Sf[l]
            if st.kind == 'p':
                for k in range(3):
                    self.op('pool', lambda e, t=self.Sf[l][k]: e.memset(t[:, :], 0.0), r=[], w=[('Sf', l, k)])
                self.op('pool', lambda e, l=l: e.memset(self.cst[:, l, :, :], 0.0), r=[], w=[('cst', l)])
            else:
                b = st.seq
                self.op('pool', lambda e, l=l, b=b: e.dma_start(
                    out=self.Sf[l][0][:, :].rearrange("p (c d) -> p c d", c=2),
                    in_=self.din['sgla'][l, b].rearrange("(c q) d -> q c d", q=128)),
                    r=[], w=[('Sf', l, 0)], dsem=('si', l, 0))
                self.op('pool', lambda e, l=l, b=b: [e.dma_start(
                    out=self.Sf[l][1][g * 64:(g + 1) * 64, :].rearrange("p (h d) -> p h d", h=4),
                    in_=self.din['sssd'][l, b, g * 256:(g + 1) * 256, :].rearrange("(h n) d -> n h d", n=64)) for g in range(2)],
                    r=[], w=[('Sf', l, 1)], dsem=('si', l, 1), ndma=2)
                self.op('pool', lambda e, l=l, b=b: e.dma_start(
                    out=self.Sf[l][2][:, :].rearrange("p (c d) -> p c d", c=2),
                    in_=self.din['sret'][l, b].rearrange("(c q) d -> q c d", q=128)),
                    r=[], w=[('Sf', l, 2)], dsem=('si', l, 2))
                self.op('pool', lambda e, l=l, b=b: [e.dma_start(
                    out=self.cst[:, l, cc, :], in_=self.din['cconv'][l, b][:, cc * 128:(cc + 1) * 128].rearrange("t p -> p t"))
                    for cc in range(6)], r=[], w=[('cst', l)], dsem=('ci', l), ndma=6)
            for k in range(3):
                self.sb_copy(l, k)

    def state_out(self, st):
        sfx = st.kind
        b = st.seq
        for l in range(2):
            self.op('pool', lambda e, l=l: e.dma_start(
                out=self.dout['gla_' + sfx][l, b].rearrange("(c q) d -> q c d", q=128),
                in_=self.Sf[l][0][:, :].rearrange("p (c d) -> p c d", c=2)),
                r=[('Sf', l, 0)], w=[], dsem=('st', l, 0))
            self.op('pool', lambda e, l=l: [e.dma_start(
                out=self.dout['ssd_' + sfx][l, b, g * 256:(g + 1) * 256, :].rearrange("(h n) d -> n h d", n=64),
                in_=self.Sf[l][1][g * 64:(g + 1) * 64, :].rearrange("p (h d) -> p h d", h=4)) for g in range(2)],
                r=[('Sf', l, 1)], w=[], dsem=('st', l, 1), ndma=2)
            self.op('pool', lambda e, l=l: e.dma_start(
                out=self.dout['ret_' + sfx][l, b].rearrange("(c q) d -> q c d", q=128),
                in_=self.Sf[l][2][:, :].rearrange("p (c d) -> p c d", c=2)),
                r=[('Sf', l, 2)], w=[], dsem=('st', l, 2))

    def run_st(self, st):
        L, NS = st.L, st.NS
        self.f32 = st.f32
        src = self.din['xp'] if st.kind == 'p' else self.din['xs']
        dst = self.dout['yp'] if st.kind == 'p' else self.dout['ys']
        t0 = st.tok0
        for j in range(NS):
            self.op('pool', lambda e, j=j: e.dma_start(out=self.x[0:L, j, :], in_=src[st.seq, t0 + j * L:t0 + (j + 1) * L, :]),
                    r=[], w=[('x', j)], dsem=('xin', j))
        if st.first:
            self.state_init(st)
        ph = 0
        for l in range(2):
            for which in (1, 0, 2):
                ph += 1
                if ph > self.DBG:
                    continue
                if which == 0:
                    self.mixer(st, l)
                else:
                    self.ffn(st, l, which, 3 * l + (0 if which == 1 else 2))
        g = self.gt_cnt % 2
        self.gt_cnt += 1
        self.op('sp', lambda e: e.dma_start(out=self.gt[:, g, :], in_=self.din['gains'][6]), r=[], w=[('gt', g)], dsem=('gt', g))
        sm = self.sm
        for j in range(NS):
            m = self.smi()
            self.op('act', lambda e, j=j, m=m: e.activation(out=self.hb[0:L, j, :], in_=self.x[0:L, j, :], func=AF.Square,
                                                           accum_out=sm[0:L, m, 0:1]), r=[('x', j)], w=[('hb', j), ('sm', m)])
            self.rstd_ops(L, m, 0, 1, 1, 1.0 / D)
            self.op('dve', lambda e, j=j, m=m: e.scalar_tensor_tensor(
                out=self.x[0:L, j, :], in0=self.x[0:L, j, :], scalar=sm[0:L, m, 1:2], in1=self.gt[0:L, g, :],
                op0=ALU.mult, op1=ALU.mult), r=[('x', j), ('sm', m), ('gt', g)], w=[('x', j)])
            self.op('pool', lambda e, j=j: e.dma_start(out=dst[st.seq, t0 + j * L:t0 + (j + 1) * L, :], in_=self.x[0:L, j, :]),
                    r=[('x', j)], w=[], dsem=('yout', j))
        if st.last:
            self.state_out(st)
        self.f32 = False

    def build(self):
        self.declare()
        self.setup()
        nst = self.T // 256
        for seq in range(self.NPS):
            tiles = [(0, 16, 1, True), (16, 120, 2, False)] + [(i * 256, 128, 2, False) for i in range(1, nst)]
            for ti, (t0, L, NS, f32) in enumerate(tiles):
                st = ST()
                st.kind, st.seq, st.L, st.NS, st.NT, st.f32 = 'p', seq, L, NS, L * NS, f32
                st.tok0 = st.pos0 = t0
                st.first, st.last = (ti == 0), (ti == len(tiles) - 1)
                self.run_st(st)
        for seq in range(self.NSS):
            st = ST()
            st.kind, st.seq, st.L, st.NS, st.NT, st.f32 = 's', seq, self.TS, 1, self.TS, False
            st.tok0, st.pos0 = 0, 1024
            st.first = st.last = True
            self.run_st(st)
        self.emit()
        return self.nc

    def emit(self):
        nc, S = self.nc, self.S
        S.finalize()
        sems = {e: self.es.enter_context(nc.semaphore('s_' + e)) for e in ENGS}
        dsems = {}
        for i, k in enumerate(S.dcount.keys()):
            dsems[k] = self.es.enter_context(nc.semaphore('d%d' % i))
        block = self.es.enter_context(nc.Block())

        @block.tensor
        def _(e):
            S.emit('pe', e, sems, dsems)

        @block.scalar
        def _(e):
            S.emit('act', e, sems, dsems)

        @block.vector
        def _(e):
            S.emit('dve', e, sems, dsems)

        @block.gpsimd
        def _(e):
            S.emit('pool', e, sems, dsems)
            for k, cnt in S.dcount.items():
                e.wait_ge(dsems[k], 16 * cnt)

        @block.sync
        def _(e):
            S.emit('sp', e, sems, dsems)

        self.es.close()


for _n in DUAL:
    setattr(Prog, _n, _Dual(_n))


def host_consts():
    c = {}
    c['idf'] = np.eye(128, dtype=np.float32)
    i = np.arange(128)
    c['tri'] = (i[:, None] <= i[None, :]).astype(np.float32)
    c['upper'] = (i[:, None] > i[None, :]).astype(np.float32)
    c['ones'] = np.ones((128, 128), np.float32)
    c['neg'] = np.where(i[:, None] > i[None, :], -30000.0, 0.0).astype(np.float32)
    lg = np.log1p(-np.exp2(-5.0 - np.arange(4, dtype=np.float64)))
    dl = (i[None, :] - i[:, None]).astype(np.float64)
    dret = np.zeros((128, 4, 128), np.float64)
    for h in range(4):
        dret[:, h, :] = np.where(dl >= 0, np.exp(lg[h] * np.maximum(dl, 0)), 0.0)
    c['dret'] = dret.astype(np.float32)
    rtab = np.zeros((128, 24), np.float64)
    for h in range(4):
        rtab[:, h] = np.exp(lg[h] * (i + 1))
        rtab[:, 4 + h] = np.exp(lg[h] * (127 - i))
        rtab[:, 8 + h] = np.exp(lg[h] * np.maximum(15 - i, 0))
        rtab[:, 12 + h] = np.exp(lg[h] * np.maximum(119 - i, 0))
    for cc in range(2):
        for hh in range(2):
            rtab[hh * 64:(hh + 1) * 64, 16 + cc] = np.exp(lg[2 * cc + hh] * 128)
            rtab[hh * 64:(hh + 1) * 64, 18 + cc] = np.exp(lg[2 * cc + hh] * 16)
            rtab[hh * 64:(hh + 1) * 64, 20 + cc] = np.exp(lg[2 * cc + hh] * 120)
    c['rtab'] = rtab.astype(np.float32)
    half = 32
    freqs = (10000.0 ** (-np.arange(half, dtype=np.float32) / half)).astype(np.float32)
    ang = np.arange(4096, dtype=np.float32)[:, None] * freqs[None, :]
    cos, sin = np.cos(ang).astype(np.float32), np.sin(ang).astype(np.float32)
    rope = np.zeros((4096, 512), np.float32)
    for h in range(8):
        sc = 1.0 if h < 4 else 0.125
        rope[:, h * 32:(h + 1) * 32] = cos * sc
        rope[:, 256 + h * 32:256 + (h + 1) * 32] = sin * sc
    c['rope'] = rope
    return c


def rep(a, n=128):
    return np.ascontiguousarray(np.broadcast_to(a[None], (n,) + a.shape)).astype(np.float32)


def shared_inputs(inp):
    f = lambda a: np.ascontiguousarray(np.asarray(a, dtype=np.float32))
    sh = host_consts()
    for k in ('ffn1_w_in', 'ffn1_w_out', 'w_in', 'w_out', 'ffn2_w_in', 'ffn2_w_out'):
        sh[k] = f(inp[k])
    sh['w2'] = f(inp['gla_w_gate2'])
    sh['wbg'], sh['wbs'], sh['wbr'] = f(inp['w_branch_gla']), f(inp['w_branch_ssd']), f(inp['w_branch_ret'])
    gains = np.stack([inp['norm_ffn1'][0], inp['norm_mix'][0], inp['norm_ffn2'][0],
                      inp['norm_ffn1'][1], inp['norm_mix'][1], inp['norm_ffn2'][1], inp['norm_final']])
    sh['gains'] = np.ascontiguousarray(np.broadcast_to(f(gains)[:, None, :], (7, 128, D)))
    sh['glab'] = rep(f(inp['gla_b_gate']))
    sh['gnorm'], sh['snorm'], sh['rnorm'] = rep(f(inp['gla_norm'])), rep(f(inp['ssd_norm'])), rep(f(inp['ret_norm']))
    sh['dtb'], sh['alog'], sh['dD'] = rep(f(inp['ssd_dt_bias'])), rep(f(inp['ssd_a_log'])), rep(f(inp['ssd_d']))
    cwt = f(inp['ssd_conv_w'])
    sh['cw'] = np.ascontiguousarray(cwt.reshape(2, 4, 6, 128).transpose(3, 0, 2, 1))
    sh['cb'] = np.ascontiguousarray(f(inp['ssd_conv_b']).reshape(2, 6, 128).transpose(2, 0, 1))
    return sh


_PROG_CACHE = {}


def run(inp, n_cores, NPS, NSS):
    f = lambda a: np.ascontiguousarray(np.asarray(a, dtype=np.float32))
    T = inp['x_prompt'].shape[1]
    key = (NPS, T, NSS)
    prog = Prog(NPS, T, NSS)
    nc = prog.build()
    sh = shared_inputs(inp)
    in_maps = []
    for i in range(n_cores):
        m = dict(sh)
        m['xp'] = f(inp['x_prompt'][i * NPS:(i + 1) * NPS])
        m['xs'] = f(inp['x_sample'][i * NSS:(i + 1) * NSS])
        m['sgla'] = f(inp['state_gla'][:, i * NSS:(i + 1) * NSS]).reshape(2, NSS, 256, 128)
        m['sssd'] = f(inp['state_ssd'][:, i * NSS:(i + 1) * NSS]).reshape(2, NSS, 512, 64)
        m['cconv'] = f(inp['cache_conv'][:, i * NSS:(i + 1) * NSS])
        m['sret'] = f(inp['state_ret'][:, i * NSS:(i + 1) * NSS]).reshape(2, NSS, 256, 128)
        in_maps.append(m)
    res = run_bass_kernel_spmd(nc, in_maps, core_ids=list(range(n_cores)))
    R = res.results
    cat = lambda k, ax: np.concatenate([np.asarray(r[k]) for r in R], axis=ax)
    Bp, Bs = NPS * n_cores, NSS * n_cores
    out = (
        cat('yp', 0), cat('ys', 0),
        cat('gla_p', 1).reshape(2, Bp, 4, 64, 128), cat('ssd_p', 1).reshape(2, Bp, 8, 64, 64),
        cat('conv_p', 1), cat('ret_p', 1).reshape(2, Bp, 4, 64, 128),
        cat('gla_s', 1).reshape(2, Bs, 4, 64, 128), cat('ssd_s', 1).reshape(2, Bs, 8, 64, 64),
        cat('conv_s', 1), cat('ret_s', 1).reshape(2, Bs, 4, 64, 128),
    )
    return tuple(np.ascontiguousarray(o, dtype=np.float32) for o in out)


def kernel(**inputs):
    return run(inputs, 8, 2, 2)
```

```python
import numpy as np
from contextlib import ExitStack
import concourse.bass as bass
import concourse.mybir as mybir
from concourse.bass_utils import run_bass_kernel_spmd

F32 = mybir.dt.float32
BF16 = mybir.dt.bfloat16
AF = mybir.ActivationFunctionType
ALU = mybir.AluOpType
AX = mybir.AxisListType

D = 1024
KC = 8
DFF = 2816
FC = 22
INC = 7448
EPS = 1e-6
(O_GQ, O_GK, O_GV, O_GR, O_GLR, O_MZ, O_XBC, O_DT, O_RQ, O_RK, O_RV, O_RG, O_GA, O_GB, O_GC) = (
    0, 256, 512, 1024, 1536, 1552, 2064, 2832, 2840, 3096, 3352, 3864, 4376, 5400, 6424)
NSLOT = 5
NSCR = 6
NSM = 8
ENGS = ('pe', 'act', 'dve', 'pool', 'sp')


class Sched:
    def __init__(self):
        self.ops = {e: [] for e in ENGS}
        self.last_w = {}
        self.readers = {}
        self.dcount = {}
        self.tag = ''

    def op(self, eng, fn, r=(), w=(), dsem=None, ndma=1):
        deps = {}

        def add(k2, v):
            if deps.get(k2, 0) < v:
                deps[k2] = v

        for k in r:
            t = self.last_w.get(k)
            if t is not None:
                add(t[:2], t[2])
        for k in w:
            t = self.last_w.get(k)
            if t is not None:
                add(t[:2], t[2])
            for k2, v in self.readers.get(k, {}).items():
                add(k2, v)
        idx = len(self.ops[eng]) + 1
        if dsem is None:
            tok = ('E', eng, idx)
        else:
            self.dcount[dsem] = self.dcount.get(dsem, 0) + ndma
            tok = ('D', dsem, 16 * self.dcount[dsem])
        for k in w:
            self.last_w[k] = tok
            self.readers[k] = {}
        for k in r:
            d = self.readers.setdefault(k, {})
            if d.get(tok[:2], 0) < tok[2]:
                d[tok[:2]] = tok[2]
        self.ops[eng].append(dict(fn=fn, deps=deps, dsem=dsem, flag=False, waits=[], tag=self.tag))

    def finalize(self):
        for eng in ENGS:
            waited = {}
            for op in self.ops[eng]:
                for (kind, name), val in op['deps'].items():
                    if kind == 'E' and name == 'pe' and eng == 'pe':
                        continue
                    if waited.get((kind, name), 0) >= val:
                        continue
                    waited[(kind, name)] = val
                    op['waits'].append((kind, name, val))
                    if kind == 'E':
                        self.ops[name][val - 1]['flag'] = True
        self.semval = {}
        for eng in ENGS:
            c = 0
            for i, op in enumerate(self.ops[eng]):
                if op['flag'] and op['dsem'] is None:
                    c += 1
                    self.semval[(eng, i + 1)] = c

    def emit(self, eng, e, sems, dsems):
        for op in self.ops[eng]:
            for (kind, name, val) in op['waits']:
                if kind == 'E':
                    e.wait_ge(sems[name], self.semval[(name, val)])
                else:
                    e.wait_ge(dsems[name], val)
            ins = op['fn'](e)
            if op['dsem'] is not None:
                for x in (ins if isinstance(ins, (list, tuple)) else [ins]):
                    x.then_inc(dsems[op['dsem']], 16)
            elif op['flag']:
                ins.then_inc(sems[eng], 1)


class ST:
    pass


class NS_:
    pass


class _Dual:
    def __init__(self, n):
        self.n = n

    def __get__(self, obj, cls):
        if obj is None:
            return self
        return getattr(obj.F if obj.f32 else obj.B, self.n)


DUAL = ('hb', 'hT', 'hid', 'glrT', 'qdT', 'kdT', 'v', 'P', 'yb', 'BCT', 'xdt', 'Mh', 'rqk', 'rqkT', 'rv', 'yT', 'mT', 'w2t', 'idb')


class Prog:
    DBG = 99
    def __init__(self, NPS, T, NSS, TS=16):
        self.NPS, self.T, self.NSS, self.TS = NPS, T, NSS, TS
        self.nc = bass.Bass("TRN2", target_bir_lowering=False)
        self.S = Sched()
        self.es = ExitStack()
        self.din = {}
        self.dout = {}
        self.ps_live = [False] * 8
        self.ps_next = 0
        self.scr_next = 0
        self.sm_next = 0
        self.ring_cnt = 0
        self.gt_cnt = 0
        self.rt_cnt = 0
        self.f32 = False
        self.slot_f32 = {}
        self.deferred = []

    def input_specs(self):
        NPS, T, NSS, TS = self.NPS, self.T, self.NSS, self.TS
        sp = {
            'xp': ([NPS, T, D], F32), 'xs': ([NSS, TS, D], F32),
            'sgla': ([2, NSS, 256, 128], F32), 'sssd': ([2, NSS, 512, 64], F32),
            'cconv': ([2, NSS, 3, 768], F32), 'sret': ([2, NSS, 256, 128], F32),
            'ffn1_w_in': ([2, D, 2 * DFF], F32), 'ffn1_w_out': ([2, DFF, D], F32),
            'w_in': ([2, D, INC], F32), 'w2': ([2, 16, 256], F32),
            'wbg': ([2, 512, D], F32), 'wbs': ([2, 512, D], F32), 'wbr': ([2, 512, D], F32),
            'w_out': ([2, D, D], F32),
            'ffn2_w_in': ([2, D, 2 * DFF], F32), 'ffn2_w_out': ([2, DFF, D], F32),
            'gains': ([7, 128, D], F32), 'glab': ([128, 2, 256], F32),
            'gnorm': ([128, 2, 512], F32), 'snorm': ([128, 2, 512], F32), 'rnorm': ([128, 2, 512], F32),
            'dtb': ([128, 2, 8], F32), 'alog': ([128, 2, 8], F32), 'dD': ([128, 2, 8], F32),
            'cw': ([128, 2, 6, 4], F32), 'cb': ([128, 2, 6], F32),
            'idf': ([128, 128], F32), 'tri': ([128, 128], F32), 'upper': ([128, 128], F32),
            'ones': ([128, 128], F32), 'neg': ([128, 128], F32), 'dret': ([128, 4, 128], F32),
            'rtab': ([128, 24], F32),
            'rope': ([4096, 512], F32),
        }
        return sp

    def output_specs(self):
        NPS, T, NSS, TS = self.NPS, self.T, self.NSS, self.TS
        return {
            'yp': ([NPS, T, D], F32), 'ys': ([NSS, TS, D], F32),
            'gla_p': ([2, NPS, 256, 128], F32), 'ssd_p': ([2, NPS, 512, 64], F32),
            'conv_p': ([2, NPS, 3, 768], F32), 'ret_p': ([2, NPS, 256, 128], F32),
            'gla_s': ([2, NSS, 256, 128], F32), 'ssd_s': ([2, NSS, 512, 64], F32),
            'conv_s': ([2, NSS, 3, 768], F32), 'ret_s': ([2, NSS, 256, 128], F32),
        }

    def sb(self, name, shape, dt):
        return self.es.enter_context(self.nc.sbuf_tensor('t_' + name, shape, dt))

    def declare(self):
        nc = self.nc
        for k, (shp, dt) in self.input_specs().items():
            self.din[k] = nc.dram_tensor(k, shp, dt, kind="ExternalInput").ap()
        for k, (shp, dt) in self.output_specs().items():
            self.dout[k] = nc.dram_tensor(k, shp, dt, kind="ExternalOutput").ap()
        self.wb = {}
        for k in ('ffn1_w_in', 'ffn1_w_out', 'w_in', 'w2', 'wbg', 'wbs', 'wbr', 'w_out', 'ffn2_w_in', 'ffn2_w_out'):
            shp = self.input_specs()[k][0]
            self.wb[k] = nc.dram_tensor('b_' + k, shp, BF16, kind="Internal").ap()
        self.es.enter_context(nc.allow_low_precision("bf16 matmul operands, fp32 accumulate"))
        self.es.enter_context(nc.allow_non_contiguous_dma("small strided state/conv transfers"))
        sb = self.sb
        self.x = sb('x', [128, 2, D], F32)
        self.hb = sb('hb', [128, 2, D], BF16)
        self.hT = sb('hT', [128, KC, 256], BF16)
        self.gt = sb('gt', [128, 2, D], F32)
        self.hid = sb('hid', [128, FC, 256], BF16)
        self.wr = sb('wr', [128, NSLOT, 4096], BF16)
        self.scr = sb('scr', [128, NSCR, 512], F32)
        self.sm = sb('sm', [128, NSM, 16], F32)
        self.glrT = sb('glrT', [16, 256], BF16)
        self.la = sb('la', [128, 2, 256], F32)
        self.ebT = sb('ebT', [128, 2, 256], F32)
        self.enbT = sb('enbT', [128, 2, 256], F32)
        self.ebLb = sb('ebLb', [128, 2, 256], F32)
        self.qdT = sb('qdT', [128, 2, 256], BF16)
        self.kdT = sb('kdT', [128, 2, 256], BF16)
        self.ks = sb('ks', [128, 2, 256], BF16)
        self.v = sb('v', [128, 2, 512], BF16)
        self.grs = sb('grs', [128, 2, 512], BF16)
        self.P = sb('P', [128, 4, 128], BF16)
        self.yb = sb('yb', [128, 512], BF16)
        self.zs = sb('zs', [128, 2, 512], BF16)
        self.xbcT = sb('xbcT', [128, 6, 260], F32)
        self.acc = sb('acc', [128, 6, 256], F32)
        self.cst = sb('cst', [128, 2, 6, 3], F32)
        self.BCT = sb('BCT', [128, 2, 256], BF16)
        self.xdt = sb('xdt', [128, 512], BF16)
        self.xdd = sb('xdd', [128, 512], BF16)
        self.xsD = sb('xsD', [128, 512], F32)
        self.Btm = sb('Btm', [128, 128], BF16)
        self.dtt = sb('dtt', [128, 2, 8], F32)
        self.at = sb('at', [128, 2, 8], F32)
        self.sdec = sb('sdec', [128, 40], F32)
        self.Dh = sb('Dh', [128, 8, 128], F32)
        self.GT = sb('GT', [128, 2, 128], F32)
        self.Mh = sb('Mh', [128, 8, 128], BF16)
        self.rt = sb('rt', [128, 2, 512], F32)
        self.rqk = sb('rqk', [128, 2, 512], BF16)
        self.rqkT = sb('rqkT', [128, 4, 256], BF16)
        self.rkd = sb('rkd', [128, 2, 256], BF16)
        self.rv = sb('rv', [128, 2, 512], BF16)
        self.rgs = sb('rgs', [128, 2, 512], BF16)
        self.yT = [sb('yT%d' % i, [128, 4, 256], BF16) for i in range(3)]
        self.mT = sb('mT', [128, KC, 256], BF16)
        self.Sf = [[sb('Sf%d_%d' % (l, k), [128, 256], F32) for k in range(3)] for l in range(2)]
        self.Sb = [[sb('Sb%d_%d' % (l, k), [128, 512], BF16) for k in range(3)] for l in range(2)]
        self.idf = sb('idf', [128, 128], F32)
        self.idb = sb('idb', [128, 128], BF16)
        self.tri = sb('tri', [128, 128], F32)
        self.upper = sb('upper', [128, 128], F32)
        self.ones = sb('ones', [128, 128], F32)
        self.neg = sb('neg', [128, 128], F32)
        self.dret = sb('dret', [128, 4, 128], F32)
        self.rtab = sb('rtab', [128, 24], F32)
        self.glab = sb('glab', [128, 2, 256], F32)
        self.gnorm = sb('gnorm', [128, 2, 512], F32)
        self.snorm = sb('snorm', [128, 2, 512], F32)
        self.rnorm = sb('rnorm', [128, 2, 512], F32)
        self.dtb = sb('dtb', [128, 2, 8], F32)
        self.At = sb('At', [128, 2, 8], F32)
        self.dD = sb('dD', [128, 2, 8], F32)
        self.cw = sb('cw', [128, 2, 6, 4], F32)
        self.cb = sb('cb', [128, 2, 6], F32)
        self.w2t = sb('w2t', [16, 2, 256], BF16)
        self.ps = [self.es.enter_context(nc.psum_tensor('ps%d' % i, [128, 512], F32)) for i in range(8)]
        names = ('hb', 'hT', 'hid', 'glrT', 'qdT', 'kdT', 'v', 'P', 'yb', 'BCT', 'xdt', 'Mh', 'rqk', 'rqkT', 'rv', 'yT', 'mT',
                 'w2t', 'idb')
        self.B = NS_()
        for n in names:
            setattr(self.B, n, self.__dict__.pop(n))
        Fs = NS_()
        Fs.hb = sb('f_hb', [128, 1, D], F32)
        Fs.hT = sb('f_hT', [128, KC, 16], F32)
        Fs.hid = sb('f_hid', [128, FC, 16], F32)
        Fs.glrT = sb('f_glrT', [16, 16], F32)
        Fs.qdT = sb('f_qdT', [128, 2, 16], F32)
        Fs.kdT = sb('f_kdT', [128, 2, 16], F32)
        Fs.v = sb('f_v', [128, 1, 512], F32)
        Fs.P = sb('f_P', [128, 4, 16], F32)
        Fs.yb = sb('f_yb', [128, 512], F32)
        Fs.BCT = sb('f_BCT', [128, 2, 16], F32)
        Fs.xdt = sb('f_xdt', [128, 512], F32)
        Fs.Mh = sb('f_Mh', [128, 8, 16], F32)
        Fs.rqk = sb('f_rqk', [128, 1, 512], F32)
        Fs.rqkT = sb('f_rqkT', [128, 4, 16], F32)
        Fs.rv = sb('f_rv', [128, 1, 512], F32)
        Fs.yT = [sb('f_yT%d' % i, [128, 4, 16], F32) for i in range(3)]
        Fs.mT = sb('f_mT', [128, KC, 16], F32)
        Fs.w2t = sb('f_w2t', [16, 2, 256], F32)
        Fs.idb = self.idf
        self.F = Fs

    def pa(self):
        for _ in range(8):
            b = self.ps_next
            self.ps_next = (self.ps_next + 1) % 8
            if not self.ps_live[b]:
                self.ps_live[b] = True
                return b
        raise RuntimeError("PSUM exhausted")

    def pf(self, b):
        self.ps_live[b] = False

    def psb(self, b):
        return self.ps[b][:, :] if self.f32 else self.ps[b][:].bitcast(BF16)

    @property
    def toff(self):
        return 64 if self.f32 else 128

    def sc(self):
        i = self.scr_next
        self.scr_next = (i + 1) % NSCR
        return i

    def smi(self):
        i = self.sm_next
        self.sm_next = (i + 1) % NSM
        return i

    def op(self, eng, fn, **k):
        f = self.f32

        def fn2(e, fn=fn, f=f):
            old = self.f32
            self.f32 = f
            try:
                return fn(e)
            finally:
                self.f32 = old

        self.S.op(eng, fn2, **k)

    def mm(self, out, lhsT, rhs, start, stop, r, w):
        self.S.op('pe', lambda e: e.matmul(out, lhsT, rhs, start=start, stop=stop), r=r, w=w)

    def tp(self, out, in_, ident, r, w):
        self.S.op('pe', lambda e: e.transpose(out=out, in_=in_, identity=ident), r=r, w=w)

    def ring(self, dmas, wkeys, nslots=2):
        s = self.ring_cnt % NSLOT
        if self.f32 and nslots == 1:
            self.ring_cnt += 1
            self.slot_f32[s] = 1
            keys = [('wr', s)]
        elif self.f32:
            if s == NSLOT - 1:
                self.ring_cnt += 1
                s = 0
            self.ring_cnt += 2
            self.slot_f32[s] = 2
            keys = [('wr', s), ('wr', s + 1)]
        else:
            self.ring_cnt += 1
            self.slot_f32[s] = False
            keys = [('wr', s)]
        flat = self.sv(s)

        def fn(e, dmas=dmas, flat=flat):
            return [e.dma_start(out=d(flat), in_=src) for d, src in dmas]

        self.S.op('sp', fn, r=list(wkeys), w=keys, dsem=('wr', s), ndma=len(dmas))
        return s

    def sv(self, s):
        if self.slot_f32.get(s) == 2:
            return self.wr[:, s:s + 2, :].rearrange("p a b -> p (a b)").bitcast(F32)
        if self.slot_f32.get(s) == 1:
            return self.wr[:, s, :].bitcast(F32)
        return self.wr[:, s, :]

    def sk(self, s):
        return [('wr', s), ('wr', s + 1)] if self.slot_f32.get(s) == 2 else [('wr', s)]

    def wsrc(self, name):
        return self.din[name] if self.f32 else self.wb[name]

    def wview(self, name, l):
        return self.wsrc(name)[l].rearrange("(kc p) n -> p kc n", p=128)

    def setup(self):
        S, din = self.S, self.din
        order = []
        for l in range(2):
            for k in ('ffn1_w_in', 'ffn1_w_out', 'w_in', 'w2', 'wbg', 'wbs', 'wbr', 'w_out', 'ffn2_w_in', 'ffn2_w_out'):
                order.append((k, l))
        for (k, l) in order:
            src, dst = din[k][l], self.wb[k][l]
            rows = src.shape[0]
            step = 128 if rows > 128 else rows
            pieces = [(r0, min(r0 + step, rows)) for r0 in range(0, rows, step)]

            def fn(e, src=src, dst=dst, pieces=pieces):
                return [e.dma_start(out=dst[a:b, :], in_=src[a:b, :]) for a, b in pieces]

            S.op('pool', fn, r=[], w=[('W', k, l)], dsem=('cast', k, l), ndma=len(pieces))
        cl = [(self.idf, 'idf'), (self.tri, 'tri'), (self.upper, 'upper'), (self.ones, 'ones'), (self.neg, 'neg'),
              (self.dret, 'dret'), (self.rtab, 'rtab'), (self.glab, 'glab'), (self.gnorm, 'gnorm'),
              (self.snorm, 'snorm'), (self.rnorm, 'rnorm'), (self.dtb, 'dtb'), (self.At, 'alog'), (self.dD, 'dD'),
              (self.cw, 'cw'), (self.cb, 'cb')]

        def fnc(e):
            return [e.dma_start(out=t[:], in_=din[n]) for t, n in cl]

        S.op('sp', fnc, r=[], w=[('const',)], dsem=('const',), ndma=len(cl))
        S.op('sp', lambda e: e.dma_start(out=self.w2t[:], in_=self.wb['w2'].rearrange("l k n -> k l n")),
             r=[('W', 'w2', 0), ('W', 'w2', 1)], w=[('w2t',)], dsem=('w2t',))
        S.op('dve', lambda e: e.tensor_copy(out=self.idb[:], in_=self.idf[:]), r=[('const',)], w=[('idb',)])
        S.op('sp', lambda e: e.dma_start(out=self.F.w2t[:], in_=din['w2'].rearrange("l k n -> k l n")),
             r=[], w=[('w2t',)], dsem=('w2f',))
        for l in range(2):
            for k in range(3):
                S.op('pool', lambda e, t=self.Sb[l][k]: e.memset(t[:, :], 0.0), r=[], w=[('Sb', l, k)])
        S.op('pool', lambda e: e.memset(self.acc[:, :, :], 0.0), r=[], w=[('acc', c) for c in range(6)])
        S.op('act', lambda e: e.activation(out=self.At[:], in_=self.At[:], func=AF.Exp), r=[('const',)], w=[('At',)])
        S.op('dve', lambda e: e.tensor_scalar(out=self.At[:], in0=self.At[:], scalar1=-1.0, scalar2=0.0, op0=ALU.mult, op1=ALU.add),
             r=[('At',)], w=[('At',)])

    def norm_T(self, st, gidx):
        self.S.tag = 'norm'
        L, NS = st.L, st.NS
        g = self.gt_cnt % 2
        self.gt_cnt += 1
        self.op('sp', lambda e: e.dma_start(out=self.gt[:, g, :], in_=self.din['gains'][gidx]),
                r=[], w=[('gt', g)], dsem=('gt', g))
        for j in range(NS):
            m = self.smi()
            sm = self.sm
            self.op('act', lambda e, j=j, m=m: e.activation(out=self.hb[0:L, j, :], in_=self.x[0:L, j, :], func=AF.Square,
                                                           accum_out=sm[0:L, m, 0:1]),
                    r=[('x', j)], w=[('hb', j), ('sm', m)])
            self.op('act', lambda e, m=m: e.activation(out=sm[0:L, m, 1:2], in_=sm[0:L, m, 0:1], func=AF.Ln, bias=EPS,
                                                      scale=1.0 / D), r=[('sm', m)], w=[('sm', m)])
            self.op('act', lambda e, m=m: e.activation(out=sm[0:L, m, 2:3], in_=sm[0:L, m, 1:2], func=AF.Exp, scale=-0.5),
                    r=[('sm', m)], w=[('sm', m)])
            self.op('dve', lambda e, j=j, m=m: e.scalar_tensor_tensor(
                out=self.hb[0:L, j, :], in0=self.x[0:L, j, :], scalar=sm[0:L, m, 2:3], in1=self.gt[0:L, g, :],
                op0=ALU.mult, op1=ALU.mult), r=[('x', j), ('sm', m), ('gt', g)], w=[('hb', j)])
            b = self.pa()
            pb = self.psb(b)
            to = self.toff
            for kc in range(KC):
                self.tp(pb[:, kc * to:kc * to + L], self.hb[0:L, j, kc * 128:(kc + 1) * 128], self.idb[0:L, 0:L],
                        r=[('hb', j), ('idb',)], w=[('ps', b)])
            self.op('dve', lambda e, j=j, pb=pb, to=to: e.tensor_copy(
                out=self.hT[:, :, j * L:(j + 1) * L], in_=pb[:, 0:KC * to].rearrange("p (k l) -> p k l", k=KC)[:, :, 0:L]),
                r=[('ps', b)], w=[('hT',)])
            self.pf(b)

    def ffn(self, st, l, which, gidx):
        L, NS, NT = st.L, st.NS, st.NT
        self.norm_T(st, gidx)
        self.S.tag = 'ffn_in'
        wi = self.wview('ffn%d_w_in' % which, l)
        wo = self.wsrc('ffn%d_w_out' % which)[l].rearrange("(c p) n -> p c n", p=128)
        kin = ('W', 'ffn%d_w_in' % which, l)
        kout = ('W', 'ffn%d_w_out' % which, l)
        for i in range(FC // 2):
            s = self.ring([
                (lambda f: f[:, 0:4096].rearrange("p (k n) -> p k n", k=KC)[:, :, 0:256], wi[:, :, 256 * i:256 * i + 256]),
                (lambda f: f[:, 0:4096].rearrange("p (k n) -> p k n", k=KC)[:, :, 256:512],
                 wi[:, :, DFF + 256 * i:DFF + 256 * i + 256])], [kin])
            pv = self.sv(s).rearrange("p (k n) -> p k n", k=KC)
            for mm_ in range(2):
                m = 2 * i + mm_
                b = self.pa()
                for half in range(2):
                    for kc in range(KC):
                        self.mm(self.ps[b][:, half * 256:half * 256 + NT],
                                pv[:, kc, half * 256 + mm_ * 128:half * 256 + mm_ * 128 + 128], self.hT[:, kc, 0:NT],
                                kc == 0, kc == KC - 1, r=[*self.sk(s), ('hT',)], w=[('ps', b)])
                t = self.sc()
                self.op('act', lambda e, b=b, t=t: e.activation(out=self.scr[:, t, 0:NT], in_=self.ps[b][:, 0:NT],
                                                               func=AF.Silu), r=[('ps', b)], w=[('scr', t)])
                self.op('dve', lambda e, b=b, t=t, m=m: e.tensor_tensor(
                    out=self.hid[:, m, 0:NT], in0=self.scr[:, t, 0:NT], in1=self.ps[b][:, 256:256 + NT], op=ALU.mult),
                    r=[('scr', t), ('ps', b)], w=[('hid', m)])
                self.pf(b)
        self.S.tag = 'ffn_out'
        banks = [[self.pa() for _ in range(2)] for _ in range(NS)]
        for i in range((FC + 3) // 4):
            c0 = 4 * i
            n = min(4, FC - c0)
            s = self.ring([(lambda f, n=n: f[:, 0:n * 1024].rearrange("p (c n) -> p c n", c=n), wo[:, c0:c0 + n, :])], [kout])
            pv = self.sv(s).rearrange("p (c n) -> p c n", c=4)
            for ci in range(n):
                c = c0 + ci
                for j in range(NS):
                    for half in range(2):
                        b = banks[j][half]
                        self.mm(self.ps[b][0:L, :], self.hid[:, c, j * L:(j + 1) * L], pv[:, ci, half * 512:(half + 1) * 512],
                                c == 0, c == FC - 1, r=[*self.sk(s), ('hid', c)], w=[('ps', b)])
        for j in range(NS):
            for half in range(2):
                b = banks[j][half]
                self.op('dve', lambda e, b=b, j=j, half=half: e.scalar_tensor_tensor(
                    out=self.x[0:L, j, half * 512:(half + 1) * 512], in0=self.ps[b][0:L, :], scalar=0.5,
                    in1=self.x[0:L, j, half * 512:(half + 1) * 512], op0=ALU.mult, op1=ALU.add),
                    r=[('ps', b), ('x', j)], w=[('x', j)])
                self.pf(b)

    def proj_tm(self, st, s, pv, c0, n, evac):
        L, NS = st.L, st.NS
        for j in range(NS):
            b = self.pa()
            for kc in range(KC):
                self.mm(self.ps[b][0:L, 0:n], self.hT[:, kc, j * L:(j + 1) * L], pv[:, kc, c0:c0 + n],
                        kc == 0, kc == KC - 1, r=[*self.sk(s), ('hT',)], w=[('ps', b)])
            evac(j, b)
            self.pf(b)

    def panel(self, l, c0, n):
        wi = self.wview('w_in', l)
        s = self.ring([(lambda f, n=n: f[:, 0:KC * n].rearrange("p (k n) -> p k n", k=KC), wi[:, :, c0:c0 + n])],
                      [('W', 'w_in', l)])
        return s, self.sv(s)[:, 0:KC * n].rearrange("p (k n) -> p k n", k=KC)

    def rstd_ops(self, L, m, col_in, col_out, n, scale, ncols=1):
        sm = self.sm
        self.op('act', lambda e: e.activation(out=sm[0:L, m, col_out:col_out + ncols], in_=sm[0:L, m, col_in:col_in + ncols],
                                              func=AF.Ln, bias=EPS, scale=scale), r=[('sm', m)], w=[('sm', m)])
        self.op('act', lambda e: e.activation(out=sm[0:L, m, col_out:col_out + ncols], in_=sm[0:L, m, col_out:col_out + ncols],
                                              func=AF.Exp, scale=-0.5), r=[('sm', m)], w=[('sm', m)])

    def group_norm_out(self, st, j, u, gate, ydst):
        L = st.L
        self.flush_y()
        sm, scr = self.sm, self.scr
        m = self.smi()
        t = self.sc()
        u3 = scr[0:L, u, :].rearrange("p (h d) -> p h d", h=4)
        t3 = scr[0:L, t, :].rearrange("p (h d) -> p h d", h=4)
        self.op('dve', lambda e: e.tensor_reduce(out=sm[0:L, m, 0:4], in_=u3, axis=AX.X, op=ALU.add),
                r=[('scr', u)], w=[('sm', m)])
        self.op('dve', lambda e: e.tensor_tensor(out=scr[0:L, t, :], in0=scr[0:L, u, :], in1=scr[0:L, u, :], op=ALU.mult),
                r=[('scr', u)], w=[('scr', t)])
        self.op('dve', lambda e: e.tensor_reduce(out=sm[0:L, m, 4:8], in_=t3, axis=AX.X, op=ALU.add),
                r=[('scr', t), ('sm', m)], w=[('sm', m)])
        self.op('dve', lambda e: e.tensor_scalar(out=sm[0:L, m, 0:4], in0=sm[0:L, m, 0:4], scalar1=1.0 / 128, scalar2=0.0,
                                                 op0=ALU.mult, op1=ALU.add), r=[('sm', m)], w=[('sm', m)])
        self.op('dve', lambda e: e.tensor_tensor(out=sm[0:L, m, 8:12], in0=sm[0:L, m, 0:4], in1=sm[0:L, m, 0:4], op=ALU.mult),
                r=[('sm', m)], w=[('sm', m)])
        self.op('dve', lambda e: e.scalar_tensor_tensor(out=sm[0:L, m, 4:8], in0=sm[0:L, m, 4:8], scalar=1.0 / 128,
                                                        in1=sm[0:L, m, 8:12], op0=ALU.mult, op1=ALU.subtract),
                r=[('sm', m)], w=[('sm', m)])
        self.rstd_ops(L, m, 4, 12, 4, 1.0, ncols=4)
        self.op('dve', lambda e: e.tensor_tensor(out=u3, in0=u3, in1=sm[0:L, m, 0:4].unsqueeze(2).to_broadcast([L, 4, 128]),
                                                 op=ALU.subtract), r=[('scr', u), ('sm', m)], w=[('scr', u)])
        self.op('dve', lambda e: e.tensor_tensor(out=u3, in0=u3, in1=sm[0:L, m, 12:16].unsqueeze(2).to_broadcast([L, 4, 128]),
                                                 op=ALU.mult), r=[('scr', u), ('sm', m)], w=[('scr', u)])
        gk, gap = gate
        self.op('dve', lambda e: e.tensor_tensor(out=self.yb[0:L, :], in0=scr[0:L, u, :], in1=gap, op=ALU.mult),
                r=[('scr', u), gk], w=[('yb',)])
        self.y_transpose(st, j, ydst)

    def y_transpose(self, st, j, ydst):
        self.deferred.append((st, j, ydst))

    def flush_y(self):
        for (st, j, ydst) in self.deferred:
            self.y_transpose_now(st, j, ydst)
        self.deferred = []

    def y_transpose_now(self, st, j, ydst):
        L = st.L
        b = self.pa()
        pb = self.psb(b)
        to = self.toff
        for c4 in range(4):
            self.tp(pb[:, c4 * to:c4 * to + L], self.yb[0:L, c4 * 128:(c4 + 1) * 128], self.idb[0:L, 0:L],
                    r=[('yb',), ('idb',)], w=[('ps', b)])
        yt = self.yT[ydst]
        self.op('dve', lambda e: e.tensor_copy(out=yt[:, :, j * L:(j + 1) * L],
                                               in_=pb[:, 0:4 * to].rearrange("p (k l) -> p k l", k=4)[:, :, 0:L]),
                r=[('ps', b)], w=[('yT', ydst)])
        self.pf(b)

    def mixer(self, st, l):
        L, NS, NT = st.L, st.NS, st.NT
        S = self.S
        scr, sm = self.scr, self.sm
        self.norm_T(st, 3 * l + 1)
        wi = self.wview('w_in', l)
        kW = ('W', 'w_in', l)
        Sg, Ss, Sr = self.Sf[l]
        Sgb, Ssb, Srb = self.Sb[l]
        self.S.tag = 'gla_proj'
        s0 = self.ring([
            (lambda f: f[:, 0:KC * 32].rearrange("p (k n) -> p k n", k=KC)[:, :, 0:16], wi[:, :, O_GLR:O_GLR + 16]),
            (lambda f: f[:, 0:KC * 32].rearrange("p (k n) -> p k n", k=KC)[:, :, 16:24], wi[:, :, O_DT:O_DT + 8])], [kW])
        pv0 = self.sv(s0)[:, 0:KC * 32].rearrange("p (k n) -> p k n", k=KC)
        b = self.pa()
        for kc in range(KC):
            self.mm(self.ps[b][0:16, 0:NT], pv0[:, kc, 0:16], self.hT[:, kc, 0:NT], kc == 0, kc == KC - 1,
                    r=[*self.sk(s0), ('hT',)], w=[('ps', b)])
        self.op('dve', lambda e, b=b: e.tensor_copy(out=self.glrT[0:16, 0:NT], in_=self.ps[b][0:16, 0:NT]),
                r=[('ps', b)], w=[('glrT',)])
        self.pf(b)

        def ev_dt(j, b):
            self.op('dve', lambda e: e.tensor_tensor(out=self.dtt[0:L, j, :], in0=self.ps[b][0:L, 0:8],
                                                     in1=self.dtb[0:L, l, :], op=ALU.add),
                    r=[('ps', b), ('const',)], w=[('dtt', j)])
            self.op('act', lambda e: e.activation(out=self.dtt[0:L, j, :], in_=self.dtt[0:L, j, :], func=AF.Exp),
                    r=[('dtt', j)], w=[('dtt', j)])
            self.op('act', lambda e: e.activation(out=self.dtt[0:L, j, :], in_=self.dtt[0:L, j, :], func=AF.Ln, bias=1.0),
                    r=[('dtt', j)], w=[('dtt', j)])
            self.op('dve', lambda e: e.tensor_tensor(out=self.at[0:L, j, :], in0=self.dtt[0:L, j, :], in1=self.At[0:L, l, :],
                                                     op=ALU.mult), r=[('dtt', j), ('At',)], w=[('at', j)])

        self.proj_tm(st, s0, pv0, 16, 8, ev_dt)
        for j in range(NS):
            b = self.pa()
            self.mm(self.ps[b][0:L, 0:256], self.glrT[0:16, j * L:(j + 1) * L], self.w2t[0:16, l, :], True, True,
                    r=[('glrT',), ('w2t',)], w=[('ps', b)])
            self.op('dve', lambda e, j=j, b=b: e.tensor_tensor(out=self.la[0:L, j, :], in0=self.ps[b][0:L, 0:256],
                                                              in1=self.glab[0:L, l, :], op=ALU.add),
                    r=[('ps', b), ('const',)], w=[('la', j)])
            self.pf(b)
            self.op('act', lambda e, j=j: e.activation(out=self.la[0:L, j, :], in_=self.la[0:L, j, :], func=AF.Exp, scale=-1.0),
                    r=[('la', j)], w=[('la', j)])
            self.op('act', lambda e, j=j: e.activation(out=self.la[0:L, j, :], in_=self.la[0:L, j, :], func=AF.Ln, bias=1.0),
                    r=[('la', j)], w=[('la', j)])
            self.op('dve', lambda e, j=j: e.tensor_scalar(out=self.la[0:L, j, :], in0=self.la[0:L, j, :], scalar1=-1.0 / 16.0,
                                                         scalar2=0.0, op0=ALU.mult, op1=ALU.add), r=[('la', j)], w=[('la', j)])
            b = self.pa()
            for c in range(2):
                self.mm(self.ps[b][:, c * 128:c * 128 + L], self.la[0:L, j, c * 128:(c + 1) * 128], self.tri[0:L, 0:L],
                        True, True, r=[('la', j), ('const',)], w=[('ps', b)])
            b2 = self.pa()
            self.mm(self.ps[b2][0:L, 0:256], self.upper[0:L, 0:L], self.la[0:L, j, :], True, True,
                    r=[('la', j), ('const',)], w=[('ps', b2)])
            pin = self.ps[b][:, 0:256].rearrange("p (c l) -> p c l", c=2)[:, :, 0:L]
            self.op('act', lambda e, j=j, pin=pin: e.activation(out=self.ebT[:, :, j * L:(j + 1) * L], in_=pin, func=AF.Exp),
                    r=[('ps', b)], w=[('ebT',)])
            self.op('act', lambda e, j=j, pin=pin: e.activation(out=self.enbT[:, :, j * L:(j + 1) * L], in_=pin, func=AF.Exp,
                                                               scale=-1.0), r=[('ps', b)], w=[('enbT',)])
            self.op('act', lambda e, j=j, b2=b2: e.activation(out=self.ebLb[0:L, j, :], in_=self.ps[b2][0:L, 0:256], func=AF.Exp),
                    r=[('ps', b2)], w=[('ebLb', j)])
            self.pf(b)
            self.pf(b2)
        s1, pv1 = self.panel(l, 0, 512)
        for c in range(4):
            b = self.pa()
            for kc in range(KC):
                self.mm(self.ps[b][:, 0:NT], pv1[:, kc, c * 128:(c + 1) * 128], self.hT[:, kc, 0:NT], kc == 0, kc == KC - 1,
                        r=[*self.sk(s1), ('hT',)], w=[('ps', b)])
            if c < 2:
                self.op('dve', lambda e, b=b, c=c: e.scalar_tensor_tensor(
                    out=self.qdT[:, c, 0:NT], in0=self.ps[b][:, 0:NT], scalar=0.125, in1=self.ebT[:, c, 0:NT],
                    op0=ALU.mult, op1=ALU.mult), r=[('ps', b), ('ebT',)], w=[('qdT',)])
            else:
                self.op('dve', lambda e, b=b, c=c: e.tensor_tensor(
                    out=self.kdT[:, c - 2, 0:NT], in0=self.ps[b][:, 0:NT], in1=self.enbT[:, c - 2, 0:NT], op=ALU.mult),
                    r=[('ps', b), ('enbT',)], w=[('kdT',)])
            self.pf(b)

        def ev_k(j, b):
            self.op('dve', lambda e: e.tensor_tensor(out=self.ks[0:L, j, :], in0=self.ps[b][0:L, 0:256],
                                                     in1=self.ebLb[0:L, j, :], op=ALU.mult),
                    r=[('ps', b), ('ebLb', j)], w=[('ks', j)])

        self.proj_tm(st, s1, pv1, 256, 256, ev_k)
        s2, pv2 = self.panel(l, O_GV, 512)

        def ev_v(j, b):
            self.op('act', lambda e: e.activation(out=self.v[0:L, j, :], in_=self.ps[b][0:L, :], func=AF.Copy),
                    r=[('ps', b)], w=[('v', j)])
            if st.f32:
                self.op('act', lambda e: e.activation(out=self.B.v[0:L, j, :], in_=self.ps[b][0:L, :], func=AF.Copy),
                        r=[('ps', b)], w=[('vb', j)])

        self.proj_tm(st, s2, pv2, 0, 512, ev_v)

        def ev_gate(dst, dkey, normt):
            def ev(j, b):
                t = self.sc()
                self.op('act', lambda e: e.activation(out=scr[0:L, t, :], in_=self.ps[b][0:L, :], func=AF.Silu),
                        r=[('ps', b)], w=[('scr', t)])
                self.op('dve', lambda e: e.tensor_tensor(out=dst[0:L, j, :], in0=scr[0:L, t, :], in1=normt[0:L, l, :],
                                                          op=ALU.mult), r=[('scr', t), ('const',)], w=[(dkey, j)])
            return ev

        s3, pv3 = self.panel(l, O_GR, 512)
        self.proj_tm(st, s3, pv3, 0, 512, ev_gate(self.grs, 'grs', self.gnorm))
        self.S.tag = 'gla_scan'
        for j in range(NS):
            tok = slice(j * L, (j + 1) * L)
            bSs = [self.pa(), self.pa()]
            for h in range(4):
                c, hh = h // 2, h % 2
                pr = slice(hh * 64, hh * 64 + 64)
                self.mm(self.ps[bSs[hh]][0:L, c * 128:c * 128 + L], self.kdT[pr, c, tok], self.qdT[pr, c, tok], True, True,
                        r=[('kdT',), ('qdT',)], w=[('ps', bSs[hh])])
            for hh in range(2):
                self.op('dve', lambda e, hh=hh, bq=bSs[hh]: e.tensor_tensor(
                    out=self.P[0:L, :, 0:L].rearrange("p (c h) l -> p c h l", h=2)[:, :, hh, :],
                    in0=self.ps[bq][0:L, 0:256].rearrange("p (c l) -> p c l", c=2)[:, :, 0:L],
                    in1=self.tri[0:L, 0:L].unsqueeze(1).to_broadcast([L, 2, L]), op=ALU.mult),
                    r=[('ps', bSs[hh]), ('const',)], w=[('P',)])
                self.pf(bSs[hh])
            bO = self.pa()
            for c in range(2):
                if not st.f32:
                    self.mm(self.ps[bO][0:L, c * 256:(c + 1) * 256], self.qdT[:, c, tok], Sgb[:, c * 256:(c + 1) * 256],
                            True, False, r=[('qdT',), ('Sb', l, 0)], w=[('ps', bO)])
                for hh in range(2):
                    h = 2 * c + hh
                    self.mm(self.ps[bO][0:L, h * 128:(h + 1) * 128], self.P[0:L, h, 0:L], self.v[0:L, j, h * 128:(h + 1) * 128],
                            bool(st.f32), True if st.f32 else hh == 1, r=[('P',), ('v', j)], w=[('ps', bO)])
            u = self.sc()
            self.op('act', lambda e, bO=bO, u=u: e.activation(out=scr[0:L, u, :], in_=self.ps[bO][0:L, :], func=AF.Copy),
                    r=[('ps', bO)], w=[('scr', u)])
            self.pf(bO)
            bU = self.pa()
            for h in range(4):
                c, hh = h // 2, h % 2
                self.mm(self.ps[bU][hh * 64:hh * 64 + 64, c * 128:(c + 1) * 128], self.ks[0:L, j, h * 64:(h + 1) * 64],
                        self.B.v[0:L, j, h * 128:(h + 1) * 128], True, True, r=[('ks', j), ('v', j), ('vb', j)], w=[('ps', bU)])
            for c in range(2):
                self.op('dve', lambda e, c=c, bU=bU, j=j: e.scalar_tensor_tensor(
                    out=Sg[:, c * 128:(c + 1) * 128], in0=Sg[:, c * 128:(c + 1) * 128],
                    scalar=self.ebT[:, c, (j + 1) * L - 1:(j + 1) * L], in1=self.ps[bU][:, c * 128:(c + 1) * 128],
                    op0=ALU.mult, op1=ALU.add), r=[('Sf', l, 0), ('ebT',), ('ps', bU)], w=[('Sf', l, 0)])
            self.pf(bU)
            self.sb_copy(l, 0)
            self.group_norm_out(st, j, u, (('grs', j), self.grs[0:L, j, :]), 0)

        self.S.tag = 'ssd_proj'
        s4, pv4 = self.panel(l, O_MZ, 512)

        def ev_z(j, b):
            self.op('act', lambda e: e.activation(out=self.zs[0:L, j, :], in_=self.ps[b][0:L, :], func=AF.Silu),
                    r=[('ps', b)], w=[('zs', j)])

        self.proj_tm(st, s4, pv4, 0, 512, ev_z)
        s5, pv5 = self.panel(l, O_XBC, 512)
        s6, pv6 = self.panel(l, O_XBC + 512, 256)
        for cc in range(6):
            b = self.pa()
            for kc in range(KC):
                lh = pv5[:, kc, cc * 128:(cc + 1) * 128] if cc < 4 else pv6[:, kc, (cc - 4) * 128:(cc - 3) * 128]
                self.mm(self.ps[b][:, 0:NT], lh, self.hT[:, kc, 0:NT], kc == 0, kc == KC - 1,
                        r=[*self.sk(s5 if cc < 4 else s6), ('hT',)], w=[('ps', b)])
            self.op('act', lambda e, b=b, cc=cc: e.activation(out=self.xbcT[:, cc, 3:3 + NT], in_=self.ps[b][:, 0:NT],
                                                             func=AF.Copy), r=[('ps', b)], w=[('xbcT', cc)])
            self.pf(b)
            self.op('dve', lambda e, cc=cc: e.tensor_copy(out=self.xbcT[:, cc, 0:3], in_=self.cst[:, l, cc, :]),
                    r=[('cst', l)], w=[('xbcT', cc)])
            self.op('dve', lambda e, cc=cc: e.tensor_scalar(
                out=self.acc[:, cc, 0:NT], in0=self.xbcT[:, cc, 0:NT], scalar1=self.cw[:, l, cc, 0:1],
                scalar2=self.cb[:, l, cc:cc + 1], op0=ALU.mult, op1=ALU.add),
                r=[('xbcT', cc), ('const',)], w=[('acc', cc)])
            for jj in range(1, 4):
                self.op('dve', lambda e, cc=cc, jj=jj: e.scalar_tensor_tensor(
                    out=self.acc[:, cc, 0:NT], in0=self.xbcT[:, cc, jj:jj + NT], scalar=self.cw[:, l, cc, jj:jj + 1],
                    in1=self.acc[:, cc, 0:NT], op0=ALU.mult, op1=ALU.add),
                    r=[('xbcT', cc), ('const',), ('acc', cc)], w=[('acc', cc)])
            if cc < 4:
                self.op('act', lambda e, cc=cc: e.activation(out=self.acc[:, cc, 0:NT], in_=self.acc[:, cc, 0:NT], func=AF.Silu),
                        r=[('acc', cc)], w=[('acc', cc)])
            else:
                self.op('act', lambda e, cc=cc: e.activation(out=self.BCT[:, cc - 4, 0:NT], in_=self.acc[:, cc, 0:NT],
                                                            func=AF.Silu), r=[('acc', cc)], w=[('BCT',)])
                if st.f32 and cc == 4:
                    self.op('act', lambda e, cc=cc: e.activation(out=self.B.BCT[:, 0, 0:NT], in_=self.acc[:, cc, 0:NT],
                                                                func=AF.Silu), r=[('acc', cc)], w=[('BCTb',)])
        if st.last:
            dst = self.dout['conv_' + st.kind][l, st.seq]
            self.op('pool', lambda e, dst=dst: [e.dma_start(
                out=dst[:, cc * 128:(cc + 1) * 128].rearrange("t p -> p t"), in_=self.xbcT[:, cc, NT:NT + 3]) for cc in range(6)],
                r=[('xbcT', cc) for cc in range(6)], w=[], dsem=('st', l, 3), ndma=6)
        self.op('dve', lambda e: e.tensor_copy(out=self.cst[:, l, :, :], in_=self.xbcT[:, :, NT:NT + 3]),
                r=[('xbcT', cc) for cc in range(6)], w=[('cst', l)])
        self.S.tag = 'ssd_scan'
        sd = self.sdec
        for j in range(NS):
            tok = slice(j * L, (j + 1) * L)
            bd = self.pa()
            self.mm(self.ps[bd][0:L, 0:8], self.tri[0:L, 0:L], self.at[0:L, j, :], True, True, r=[('at', j), ('const',)],
                    w=[('ps', bd)])
            self.mm(self.ps[bd][0:L, 8:16], self.upper[0:L, 0:L], self.at[0:L, j, :], True, True, r=[('at', j), ('const',)],
                    w=[('ps', bd)])
            self.mm(self.ps[bd][:, 16:24], self.ones[0:L, :], self.at[0:L, j, :], True, True, r=[('at', j), ('const',)],
                    w=[('ps', bd)])
            self.op('dve', lambda e, bd=bd: e.tensor_scalar(out=sd[0:L, 0:8], in0=self.ps[bd][0:L, 0:8], scalar1=-1.0,
                                                           scalar2=0.0, op0=ALU.mult, op1=ALU.add), r=[('ps', bd)], w=[('sd', 0)])
            self.op('act', lambda e, bd=bd: e.activation(out=sd[0:L, 8:24], in_=self.ps[bd][0:L, 0:16], func=AF.Exp),
                    r=[('ps', bd)], w=[('sd', 1)])
            for g in range(2):
                self.op('act', lambda e, bd=bd, g=g: e.activation(
                    out=sd[g * 64:(g + 1) * 64, 24:28], in_=self.ps[bd][g * 64:(g + 1) * 64, 16 + 4 * g:20 + 4 * g], func=AF.Exp),
                    r=[('ps', bd)], w=[('sd', 2)])
            self.pf(bd)
            bX = self.pa()
            Lt = L if L > 64 else 128
            for c4 in range(4):
                self.tp(self.ps[bX][0:Lt, c4 * 128:(c4 + 1) * 128], self.acc[:, c4, j * L:j * L + Lt], self.idf[:, :],
                        r=[('acc', c4), ('const',)], w=[('ps', bX)])
            px3 = self.ps[bX][0:L, :].rearrange("p (h d) -> p h d", h=8)
            self.op('dve', lambda e, j=j, px3=px3: e.tensor_tensor(
                out=self.xdt[0:L, :].rearrange("p (h d) -> p h d", h=8), in0=px3,
                in1=self.dtt[0:L, j, :].unsqueeze(2).to_broadcast([L, 8, 64]), op=ALU.mult),
                r=[('ps', bX), ('dtt', j)], w=[('xdt',)])
            self.op('dve', lambda e, px3=px3: e.tensor_tensor(
                out=self.xsD[0:L, :].rearrange("p (h d) -> p h d", h=8), in0=px3,
                in1=self.dD[0:L, l, :].unsqueeze(2).to_broadcast([L, 8, 64]), op=ALU.mult),
                r=[('ps', bX), ('const',)], w=[('xsD',)])
            self.pf(bX)
            self.op('dve', lambda e: e.tensor_tensor(
                out=self.xdd[0:L, :].rearrange("p (h d) -> p h d", h=8), in0=self.xdt[0:L, :].rearrange("p (h d) -> p h d", h=8),
                in1=sd[0:L, 16:24].unsqueeze(2).to_broadcast([L, 8, 64]), op=ALU.mult),
                r=[('xdt',), ('sd', 1)], w=[('xdd',)])
            bB = self.pa()
            pbB = self.ps[bB][:].bitcast(BF16)
            self.tp(pbB[0:L, 0:128], self.B.BCT[:, 0, tok], self.B.idb[:, :], r=[('BCT',), ('BCTb',), ('idb',)], w=[('ps', bB)])
            self.op('dve', lambda e, pbB=pbB: e.tensor_copy(out=self.Btm[0:L, :], in_=pbB[0:L, 0:128]),
                    r=[('ps', bB)], w=[('Btm',)])
            self.pf(bB)
            for half in range(2):
                bD = self.pa()
                for hq in range(4):
                    h = 4 * half + hq
                    self.mm(self.ps[bD][0:L, hq * 128:hq * 128 + L], self.at[0:L, j, h:h + 1].to_broadcast([L, L]),
                            self.tri[0:L, 0:L], True, False, r=[('at', j), ('const',)], w=[('ps', bD)])
                    self.mm(self.ps[bD][0:L, hq * 128:hq * 128 + L], self.idf[0:L, 0:L], self.neg[0:L, 0:L], False, True,
                            r=[('const',)], w=[('ps', bD)])
                for hq in range(4):
                    h = 4 * half + hq
                    self.op('act', lambda e, bD=bD, hq=hq, h=h: e.activation(
                        out=self.Dh[0:L, h, 0:L], in_=self.ps[bD][0:L, hq * 128:hq * 128 + L], func=AF.Exp,
                        bias=sd[0:L, h:h + 1]), r=[('ps', bD), ('sd', 0)], w=[('Dh', half)])
                self.pf(bD)
            for g in range(2):
                pr = slice(g * 64, g * 64 + 64)
                bG = self.pa()
                self.mm(self.ps[bG][0:L, 0:L], self.BCT[pr, 0, tok], self.BCT[pr, 1, tok], True, True,
                        r=[('BCT',)], w=[('ps', bG)])
                self.op('act', lambda e, bG=bG, g=g: e.activation(out=self.GT[0:L, g, 0:L], in_=self.ps[bG][0:L, 0:L], func=AF.Copy),
                        r=[('ps', bG)], w=[('GT',)])
                self.pf(bG)
            for g in range(2):
                self.op('dve', lambda e, g=g: e.tensor_tensor(
                    out=self.Mh[0:L, 4 * g:4 * g + 4, 0:L], in0=self.Dh[0:L, 4 * g:4 * g + 4, 0:L],
                    in1=self.GT[0:L, g, 0:L].unsqueeze(1).to_broadcast([L, 4, L]), op=ALU.mult),
                    r=[('Dh', g), ('GT',)], w=[('Mh', g)])
            bO = self.pa()
            for h in range(8):
                self.mm(self.ps[bO][0:L, h * 64:(h + 1) * 64], self.Mh[0:L, h, 0:L], self.xdt[0:L, h * 64:(h + 1) * 64],
                        True, True, r=[('Mh', h // 4), ('xdt',)], w=[('ps', bO)])
            self.flush_y()
            u = self.sc()
            if st.f32:
                self.op('act', lambda e, bO=bO, u=u: e.activation(out=scr[0:L, u, :], in_=self.ps[bO][0:L, :], func=AF.Copy),
                        r=[('ps', bO)], w=[('scr', u)])
            else:
                bI = self.pa()
                self.mm(self.ps[bI][0:L, :], self.BCT[:, 1, tok], Ssb[:, :], True, True,
                        r=[('BCT',), ('Sb', l, 1)], w=[('ps', bI)])
                self.op('dve', lambda e, bI=bI, u=u: e.tensor_tensor(
                    out=scr[0:L, u, :].rearrange("p (h d) -> p h d", h=8),
                    in0=self.ps[bI][0:L, :].rearrange("p (h d) -> p h d", h=8),
                    in1=sd[0:L, 8:16].unsqueeze(2).to_broadcast([L, 8, 64]), op=ALU.mult),
                    r=[('ps', bI), ('sd', 1)], w=[('scr', u)])
                self.pf(bI)
                self.op('dve', lambda e, bO=bO, u=u: e.tensor_tensor(out=scr[0:L, u, :], in0=scr[0:L, u, :],
                                                                    in1=self.ps[bO][0:L, :], op=ALU.add),
                        r=[('ps', bO), ('scr', u)], w=[('scr', u)])
            self.pf(bO)
            self.op('dve', lambda e, u=u: e.tensor_tensor(out=scr[0:L, u, :], in0=scr[0:L, u, :], in1=self.xsD[0:L, :],
                                                          op=ALU.add), r=[('scr', u), ('xsD',)], w=[('scr', u)])
            self.op('dve', lambda e, u=u, j=j: e.tensor_tensor(out=scr[0:L, u, :], in0=scr[0:L, u, :], in1=self.zs[0:L, j, :],
                                                               op=ALU.mult), r=[('scr', u), ('zs', j)], w=[('scr', u)])
            m = self.smi()
            t = self.sc()
            self.op('dve', lambda e, u=u, t=t, m=m: e.scalar_tensor_tensor(
                out=scr[0:L, t, :], in0=scr[0:L, u, :], scalar=1.0, in1=scr[0:L, u, :], op0=ALU.mult, op1=ALU.mult,
                accum_out=sm[0:L, m, 0:1]), r=[('scr', u)], w=[('scr', t), ('sm', m)])
            self.rstd_ops(L, m, 0, 1, 1, 1.0 / 512)
            self.op('dve', lambda e, u=u, m=m: e.scalar_tensor_tensor(
                out=self.yb[0:L, :], in0=scr[0:L, u, :], scalar=sm[0:L, m, 1:2], in1=self.snorm[0:L, l, :],
                op0=ALU.mult, op1=ALU.mult), r=[('scr', u), ('sm', m), ('const',)], w=[('yb',)])
            self.y_transpose(st, j, 1)
            bU = self.pa()
            for g in range(2):
                self.mm(self.ps[bU][g * 64:(g + 1) * 64, 0:256], self.Btm[0:L, g * 64:(g + 1) * 64],
                        self.xdd[0:L, g * 256:(g + 1) * 256], True, True, r=[('Btm',), ('xdd',)], w=[('ps', bU)])
            self.op('dve', lambda e: e.tensor_tensor(
                out=Ss[:, :].rearrange("p (h d) -> p h d", h=4), in0=Ss[:, :].rearrange("p (h d) -> p h d", h=4),
                in1=sd[:, 24:28].unsqueeze(2).to_broadcast([128, 4, 64]), op=ALU.mult),
                r=[('Sf', l, 1), ('sd', 2)], w=[('Sf', l, 1)])
            self.op('dve', lambda e, bU=bU: e.tensor_tensor(out=Ss[:, :], in0=Ss[:, :], in1=self.ps[bU][:, 0:256], op=ALU.add),
                    r=[('Sf', l, 1), ('ps', bU)], w=[('Sf', l, 1)])
            self.pf(bU)
            self.sb_copy(l, 1)

        self.S.tag = 'ret_proj'
        s7, pv7 = self.panel(l, O_RQ, 512)
        rt = self.rt
        rk_off = {128: 4, 16: 8, 120: 12}[L]
        gl_off = {128: 16, 16: 18, 120: 20}[L]

        def ev_rope(j, b):
            q = self.rt_cnt % 2
            self.rt_cnt += 1
            p0 = st.pos0 + j * L
            self.op('sp', lambda e: e.dma_start(out=rt[0:L, q, :], in_=self.din['rope'][p0:p0 + L, :]), r=[], w=[('rt', q)],
                    dsem=('rt', q))
            p4 = self.ps[b][0:L, :].rearrange("p (h t d) -> p h t d", h=8, t=2)
            x1, x2 = p4[:, :, 0, :], p4[:, :, 1, :]
            cos = rt[0:L, q, 0:256].rearrange("p (h d) -> p h d", h=8)
            sin = rt[0:L, q, 256:512].rearrange("p (h d) -> p h d", h=8)
            o4 = self.rqk[0:L, j, :].rearrange("p (h t d) -> p h t d", h=8, t=2)
            ta, tb = self.sc(), self.sc()
            va = scr[0:L, ta, 0:256].rearrange("p (h d) -> p h d", h=8)
            vb = scr[0:L, ta, 256:512].rearrange("p (h d) -> p h d", h=8)
            vc = scr[0:L, tb, 0:256].rearrange("p (h d) -> p h d", h=8)
            vd = scr[0:L, tb, 256:512].rearrange("p (h d) -> p h d", h=8)
            for (o, i0, i1) in ((va, x1, cos), (vb, x2, sin), (vc, x1, sin), (vd, x2, cos)):
                tk = ta if (o is va or o is vb) else tb
                self.op('dve', lambda e, o=o, i0=i0, i1=i1: e.tensor_tensor(out=o, in0=i0, in1=i1, op=ALU.mult),
                        r=[('ps', b), ('rt', q)], w=[('scr', tk)])
            self.op('dve', lambda e: e.tensor_tensor(out=o4[:, :, 0, :], in0=va, in1=vb, op=ALU.subtract),
                    r=[('scr', ta)], w=[('rqk', j)])
            self.op('dve', lambda e: e.tensor_tensor(out=o4[:, :, 1, :], in0=vc, in1=vd, op=ALU.add),
                    r=[('scr', tb)], w=[('rqk', j)])

        self.proj_tm(st, s7, pv7, 0, 512, ev_rope)
        for j in range(NS):
            bT = self.pa()
            pb = self.psb(bT)
            to = self.toff
            for c4 in range(4):
                self.tp(pb[:, c4 * to:c4 * to + L], self.rqk[0:L, j, c4 * 128:(c4 + 1) * 128], self.idb[0:L, 0:L],
                        r=[('rqk', j), ('idb',)], w=[('ps', bT)])
            self.op('dve', lambda e, j=j, pb=pb, to=to: e.tensor_copy(
                out=self.rqkT[:, :, j * L:(j + 1) * L], in_=pb[:, 0:4 * to].rearrange("p (k l) -> p k l", k=4)[:, :, 0:L]),
                r=[('ps', bT)], w=[('rqkT',)])
            self.pf(bT)
            self.op('dve', lambda e, j=j: e.tensor_tensor(
                out=self.rkd[0:L, j, :].rearrange("p (h d) -> p h d", h=4),
                in0=self.rqk[0:L, j, 256:512].rearrange("p (h d) -> p h d", h=4),
                in1=self.rtab[0:L, rk_off:rk_off + 4].unsqueeze(2).to_broadcast([L, 4, 64]), op=ALU.mult),
                r=[('rqk', j), ('const',)], w=[('rkd', j)])
        s8, pv8 = self.panel(l, O_RV, 512)

        def ev_rv(j, b):
            self.op('act', lambda e: e.activation(out=self.rv[0:L, j, :], in_=self.ps[b][0:L, :], func=AF.Copy),
                    r=[('ps', b)], w=[('rv', j)])
            if st.f32:
                self.op('act', lambda e: e.activation(out=self.B.rv[0:L, j, :], in_=self.ps[b][0:L, :], func=AF.Copy),
                        r=[('ps', b)], w=[('rvb', j)])

        self.proj_tm(st, s8, pv8, 0, 512, ev_rv)
        s9, pv9 = self.panel(l, O_RG, 512)
        self.S.tag2 = 0
        self.proj_tm(st, s9, pv9, 0, 512, ev_gate(self.rgs, 'rgs', self.rnorm))
        self.S.tag = "ret_scan"
        for j in range(NS):
            tok = slice(j * L, (j + 1) * L)
            bSs = [self.pa(), self.pa()]
            for h in range(4):
                c, hh = h // 2, h % 2
                pr = slice(hh * 64, hh * 64 + 64)
                self.mm(self.ps[bSs[hh]][0:L, c * 128:c * 128 + L], self.rqkT[pr, 2 + c, tok], self.rqkT[pr, c, tok], True, True,
                        r=[('rqkT',)], w=[('ps', bSs[hh])])
            for hh in range(2):
                self.op('dve', lambda e, hh=hh, bq=bSs[hh]: e.tensor_tensor(
                    out=self.P[0:L, :, 0:L].rearrange("p (c h) l -> p c h l", h=2)[:, :, hh, :],
                    in0=self.ps[bq][0:L, 0:256].rearrange("p (c l) -> p c l", c=2)[:, :, 0:L],
                    in1=self.dret[0:L, :, 0:L].rearrange("p (c h) l -> p c h l", h=2)[:, :, hh, :], op=ALU.mult),
                    r=[('ps', bSs[hh]), ('const',)], w=[('P',)])
                self.pf(bSs[hh])
            bO = self.pa()
            for h in range(4):
                self.mm(self.ps[bO][0:L, h * 128:(h + 1) * 128], self.P[0:L, h, 0:L], self.rv[0:L, j, h * 128:(h + 1) * 128],
                        True, True, r=[('P',), ('rv', j)], w=[('ps', bO)])
            u = self.sc()
            if st.f32:
                self.op('act', lambda e, bO=bO, u=u: e.activation(out=scr[0:L, u, :], in_=self.ps[bO][0:L, :], func=AF.Copy),
                        r=[('ps', bO)], w=[('scr', u)])
            else:
                bI = self.pa()
                for c in range(2):
                    self.mm(self.ps[bI][0:L, c * 256:(c + 1) * 256], self.rqkT[:, c, tok], Srb[:, c * 256:(c + 1) * 256],
                            True, True, r=[('rqkT',), ('Sb', l, 2)], w=[('ps', bI)])
                self.op('dve', lambda e, bI=bI, u=u: e.tensor_tensor(
                    out=scr[0:L, u, :].rearrange("p (h d) -> p h d", h=4),
                    in0=self.ps[bI][0:L, :].rearrange("p (h d) -> p h d", h=4),
                    in1=self.rtab[0:L, 0:4].unsqueeze(2).to_broadcast([L, 4, 128]), op=ALU.mult),
                    r=[('ps', bI), ('const',)], w=[('scr', u)])
                self.pf(bI)
                self.op('dve', lambda e, bO=bO, u=u: e.tensor_tensor(out=scr[0:L, u, :], in0=scr[0:L, u, :],
                                                                    in1=self.ps[bO][0:L, :], op=ALU.add),
                        r=[('ps', bO), ('scr', u)], w=[('scr', u)])
            self.pf(bO)
            bU = self.pa()
            for h in range(4):
                c, hh = h // 2, h % 2
                self.mm(self.ps[bU][hh * 64:hh * 64 + 64, c * 128:(c + 1) * 128], self.rkd[0:L, j, h * 64:(h + 1) * 64],
                        self.B.rv[0:L, j, h * 128:(h + 1) * 128], True, True, r=[('rkd', j), ('rv', j), ('rvb', j)], w=[('ps', bU)])
            for c in range(2):
                self.op('dve', lambda e, c=c, bU=bU: e.scalar_tensor_tensor(
                    out=Sr[:, c * 128:(c + 1) * 128], in0=Sr[:, c * 128:(c + 1) * 128],
                    scalar=self.rtab[:, gl_off + c:gl_off + c + 1], in1=self.ps[bU][:, c * 128:(c + 1) * 128],
                    op0=ALU.mult, op1=ALU.add), r=[('Sf', l, 2), ('const',), ('ps', bU)], w=[('Sf', l, 2)])
            self.pf(bU)
            self.sb_copy(l, 2)
            self.group_norm_out(st, j, u, (('rgs', j), self.rgs[0:L, j, :]), 2)

        self.flush_y()
        self.S.tag = 'merge'
        wbr = [self.wview(n, l) for n in ('wbg', 'wbs', 'wbr')]
        W = 128 if st.f32 else 256
        ndc = W // 128
        for pp in range(D // W):
            sA = self.ring([
                (lambda f: f[:, 0:KC * 2 * W].rearrange("p (k n) -> p k n", k=KC)[:, :, 0:W], wi[:, :, O_GA + pp * W:O_GA + (pp + 1) * W]),
                (lambda f: f[:, 0:KC * 2 * W].rearrange("p (k n) -> p k n", k=KC)[:, :, W:2 * W], wi[:, :, O_GB + pp * W:O_GB + (pp + 1) * W])],
                [kW], nslots=1)
            sB = self.ring([(lambda f: f[:, 0:KC * W].rearrange("p (k n) -> p k n", k=KC), wi[:, :, O_GC + pp * W:O_GC + (pp + 1) * W])],
                           [kW], nslots=1)
            sC = self.ring([(lambda f, xi=xi: f[:, xi * 4 * W:(xi + 1) * 4 * W].rearrange("p (k n) -> p k n", k=4),
                             wbr[xi][:, :, pp * W:(pp + 1) * W]) for xi in range(3)],
                           [('W', 'wbg', l), ('W', 'wbs', l), ('W', 'wbr', l)], nslots=1)
            pA = self.sv(sA)[:, 0:KC * 2 * W].rearrange("p (k n) -> p k n", k=KC)
            pB = self.sv(sB)[:, 0:KC * W].rearrange("p (k n) -> p k n", k=KC)
            pC = self.sv(sC)[:, 0:12 * W].rearrange("p (x k n) -> p x k n", x=3, k=4)
            for dc in range(ndc):
                c = ndc * pp + dc
                prs = []
                for xi in range(3):
                    b = self.pa()
                    for kc in range(KC):
                        lh = pA[:, kc, xi * W + dc * 128:xi * W + dc * 128 + 128] if xi < 2 else pB[:, kc, dc * 128:dc * 128 + 128]
                        self.mm(self.ps[b][:, 0:NT], lh, self.hT[:, kc, 0:NT], kc == 0, kc == KC - 1,
                                r=[*self.sk(sA if xi < 2 else sB), ('hT',)], w=[('ps', b)])
                    for kc in range(4):
                        self.mm(self.ps[b][:, 256:256 + NT], pC[:, xi, kc, dc * 128:dc * 128 + 128], self.yT[xi][:, kc, 0:NT],
                                kc == 0, kc == 3, r=[*self.sk(sC), ('yT', xi)], w=[('ps', b)])
                    t = self.sc()
                    self.op('act', lambda e, b=b, t=t: e.activation(out=scr[:, t, 0:NT], in_=self.ps[b][:, 0:NT], func=AF.Tanh,
                                                                   scale=0.5), r=[('ps', b)], w=[('scr', t)])
                    self.op('dve', lambda e, b=b, t=t: e.scalar_tensor_tensor(
                        out=scr[:, t, 256:256 + NT], in0=scr[:, t, 0:NT], scalar=1.0, in1=self.ps[b][:, 256:256 + NT],
                        op0=ALU.add, op1=ALU.mult), r=[('ps', b), ('scr', t)], w=[('scr', t)])
                    self.pf(b)
                    prs.append(t)
                t0, t1, t2 = prs
                self.op('dve', lambda e, t0=t0, t1=t1: e.tensor_tensor(out=scr[:, t0, 256:256 + NT], in0=scr[:, t0, 256:256 + NT],
                                                                       in1=scr[:, t1, 256:256 + NT], op=ALU.add),
                        r=[('scr', t0), ('scr', t1)], w=[('scr', t0)])
                self.op('dve', lambda e, t0=t0, t2=t2, c=c: e.tensor_tensor(out=self.mT[:, c, 0:NT], in0=scr[:, t0, 256:256 + NT],
                                                                            in1=scr[:, t2, 256:256 + NT], op=ALU.add),
                        r=[('scr', t0), ('scr', t2)], w=[('mT', c)])
        self.S.tag = 'wout'
        wo = self.wview('w_out', l)
        for half in range(2):
            sO = self.ring([(lambda f: f[:, 0:4096].rearrange("p (k n) -> p k n", k=KC), wo[:, :, half * 512:(half + 1) * 512])],
                           [('W', 'w_out', l)])
            pO = self.sv(sO).rearrange("p (k n) -> p k n", k=KC)
            for j in range(NS):
                b = self.pa()
                for kc in range(KC):
                    self.mm(self.ps[b][0:L, :], self.mT[:, kc, j * L:(j + 1) * L], pO[:, kc, :], kc == 0, kc == KC - 1,
                            r=[*self.sk(sO), ('mT', kc)], w=[('ps', b)])
                self.op('dve', lambda e, b=b, j=j, half=half: e.scalar_tensor_tensor(
                    out=self.x[0:L, j, half * 512:(half + 1) * 512], in0=self.ps[b][0:L, :], scalar=0.5,
                    in1=self.x[0:L, j, half * 512:(half + 1) * 512], op0=ALU.mult, op1=ALU.add),
                    r=[('ps', b), ('x', j)], w=[('x', j)])
                self.pf(b)

    def sb_copy(self, l, k):
        Sf, Sb = self.Sf[l][k], self.Sb[l][k]
        for q in range(2):
            pr = slice(q * 64, q * 64 + 64)
            if k == 1:
                self.op('dve', lambda e, pr=pr, q=q: e.tensor_copy(out=Sb[pr, q * 256:(q + 1) * 256], in_=Sf[pr, :]),
                        r=[('Sf', l, k)], w=[('Sb', l, k)])
            else:
                self.op('dve', lambda e, pr=pr, q=q: e.tensor_copy(
                    out=Sb[pr, :].rearrange("p (c h d) -> p c h d", c=2, h=2)[:, :, q, :],
                    in_=Sf[pr, :].rearrange("p (c d) -> p c d", c=2)), r=[('Sf', l, k)], w=[('Sb', l, k)])

    def state_init(self, st):
        for l in range(2):
            Sg, Ss, Sr = self.Sf[l]
            if st.kind == 'p':
                for k in range(3):
                    self.op('pool', lambda e, t=self.Sf[l][k]: e.memset(t[:, :], 0.0), r=[], w=[('Sf', l, k)])
                self.op('pool', lambda e, l=l: e.memset(self.cst[:, l, :, :], 0.0), r=[], w=[('cst', l)])
            else:
                b = st.seq
                self.op('pool', lambda e, l=l, b=b: e.dma_start(
                    out=self.Sf[l][0][:, :].rearrange("p (c d) -> p c d", c=2),
                    in_=self.din['sgla'][l, b].rearrange("(c q) d -> q c d", q=128)),
                    r=[], w=[('Sf', l, 0)], dsem=('si', l, 0))
                self.op('pool', lambda e, l=l, b=b: [e.dma_start(
                    out=self.Sf[l][1][g * 64:(g + 1) * 64, :].rearrange("p (h d) -> p h d", h=4),
                    in_=self.din['sssd'][l, b, g * 256:(g + 1) * 256, :].rearrange("(h n) d -> n h d", n=64)) for g in range(2)],
                    r=[], w=[('Sf', l, 1)], dsem=('si', l, 1), ndma=2)
                self.op('pool', lambda e, l=l, b=b: e.dma_start(
                    out=self.Sf[l][2][:, :].rearrange("p (c d) -> p c d", c=2),
                    in_=self.din['sret'][l, b].rearrange("(c q) d -> q c d", q=128)),
                    r=[], w=[('Sf', l, 2)], dsem=('si', l, 2))
                self.op('pool', lambda e, l=l, b=b: [e.dma_start(
                    out=self.cst[:, l, cc, :], in_=self.din['cconv'][l, b][:, cc * 128:(cc + 1) * 128].rearrange("t p -> p t"))
                    for cc in range(6)], r=[], w=[('cst', l)], dsem=('ci', l), ndma=6)
            for k in range(3):
                self.sb_copy(l, k)

    def state_out(self, st):
        sfx = st.kind
        b = st.seq
        for l in range(2):
            self.op('pool', lambda e, l=l: e.dma_start(
                out=self.dout['gla_' + sfx][l, b].rearrange("(c q) d -> q c d", q=128),
                in_=self.Sf[l][0][:, :].rearrange("p (c d) -> p c d", c=2)),
                r=[('Sf', l, 0)], w=[], dsem=('st', l, 0))
            self.op('pool', lambda e, l=l: [e.dma_start(
                out=self.dout['ssd_' + sfx][l, b, g * 256:(g + 1) * 256, :].rearrange("(h n) d -> n h d", n=64),
                in_=self.Sf[l][1][g * 64:(g + 1) * 64, :].rearrange("p (h d) -> p h d", h=4)) for g in range(2)],
                r=[('Sf', l, 1)], w=[], dsem=('st', l, 1), ndma=2)
            self.op('pool', lambda e, l=l: e.dma_start(
                out=self.dout['ret_' + sfx][l, b].rearrange("(c q) d -> q c d", q=128),
                in_=self.Sf[l][2][:, :].rearrange("p (c d) -> p c d", c=2)),
                r=[('Sf', l, 2)], w=[], dsem=('st', l, 2))

    def run_st(self, st):
        L, NS = st.L, st.NS
        self.f32 = st.f32
        src = self.din['xp'] if st.kind == 'p' else self.din['xs']
        dst = self.dout['yp'] if st.kind == 'p' else self.dout['ys']
        t0 = st.tok0
        for j in range(NS):
            self.op('pool', lambda e, j=j: e.dma_start(out=self.x[0:L, j, :], in_=src[st.seq, t0 + j * L:t0 + (j + 1) * L, :]),
                    r=[], w=[('x', j)], dsem=('xin', j))
        if st.first:
            self.state_init(st)
        ph = 0
        for l in range(2):
            for which in (1, 0, 2):
                ph += 1
                if ph > self.DBG:
                    continue
                if which == 0:
                    self.mixer(st, l)
                else:
                    self.ffn(st, l, which, 3 * l + (0 if which == 1 else 2))
        g = self.gt_cnt % 2
        self.gt_cnt += 1
        self.op('sp', lambda e: e.dma_start(out=self.gt[:, g, :], in_=self.din['gains'][6]), r=[], w=[('gt', g)], dsem=('gt', g))
        sm = self.sm
        for j in range(NS):
            m = self.smi()
            self.op('act', lambda e, j=j, m=m: e.activation(out=self.hb[0:L, j, :], in_=self.x[0:L, j, :], func=AF.Square,
                                                           accum_out=sm[0:L, m, 0:1]), r=[('x', j)], w=[('hb', j), ('sm', m)])
            self.rstd_ops(L, m, 0, 1, 1, 1.0 / D)
            self.op('dve', lambda e, j=j, m=m: e.scalar_tensor_tensor(
                out=self.x[0:L, j, :], in0=self.x[0:L, j, :], scalar=sm[0:L, m, 1:2], in1=self.gt[0:L, g, :],
                op0=ALU.mult, op1=ALU.mult), r=[('x', j), ('sm', m), ('gt', g)], w=[('x', j)])
            self.op('pool', lambda e, j=j: e.dma_start(out=dst[st.seq, t0 + j * L:t0 + (j + 1) * L, :], in_=self.x[0:L, j, :]),
                    r=[('x', j)], w=[], dsem=('yout', j))
        if st.last:
            self.state_out(st)
        self.f32 = False

    def build(self):
        self.declare()
        self.setup()
        nst = self.T // 256
        for seq in range(self.NPS):
            tiles = [(0, 16, 1, True), (16, 120, 2, False)] + [(i * 256, 128, 2, False) for i in range(1, nst)]
            for ti, (t0, L, NS, f32) in enumerate(tiles):
                st = ST()
                st.kind, st.seq, st.L, st.NS, st.NT, st.f32 = 'p', seq, L, NS, L * NS, f32
                st.tok0 = st.pos0 = t0
                st.first, st.last = (ti == 0), (ti == len(tiles) - 1)
                self.run_st(st)
        for seq in range(self.NSS):
            st = ST()
            st.kind, st.seq, st.L, st.NS, st.NT, st.f32 = 's', seq, self.TS, 1, self.TS, False
            st.tok0, st.pos0 = 0, 1024
            st.first = st.last = True
            self.run_st(st)
        self.emit()
        return self.nc

    def emit(self):
        nc, S = self.nc, self.S
        S.finalize()
        sems = {e: self.es.enter_context(nc.semaphore('s_' + e)) for e in ENGS}
        dsems = {}
        for i, k in enumerate(S.dcount.keys()):
            dsems[k] = self.es.enter_context(nc.semaphore('d%d' % i))
        block = self.es.enter_context(nc.Block())

        @block.tensor
        def _(e):
            S.emit('pe', e, sems, dsems)

        @block.scalar
        def _(e):
            S.emit('act', e, sems, dsems)

        @block.vector
        def _(e):
            S.emit('dve', e, sems, dsems)

        @block.gpsimd
        def _(e):
            S.emit('pool', e, sems, dsems)
            for k, cnt in S.dcount.items():
                e.wait_ge(dsems[k], 16 * cnt)

        @block.sync
        def _(e):
            S.emit('sp', e, sems, dsems)

        self.es.close()


for _n in DUAL:
    setattr(Prog, _n, _Dual(_n))


def host_consts():
    c = {}
    c['idf'] = np.eye(128, dtype=np.float32)
    i = np.arange(128)
    c['tri'] = (i[:, None] <= i[None, :]).astype(np.float32)
    c['upper'] = (i[:, None] > i[None, :]).astype(np.float32)
    c['ones'] = np.ones((128, 128), np.float32)
    c['neg'] = np.where(i[:, None] > i[None, :], -30000.0, 0.0).astype(np.float32)
    lg = np.log1p(-np.exp2(-5.0 - np.arange(4, dtype=np.float64)))
    dl = (i[None, :] - i[:, None]).astype(np.float64)
    dret = np.zeros((128, 4, 128), np.float64)
    for h in range(4):
        dret[:, h, :] = np.where(dl >= 0, np.exp(lg[h] * np.maximum(dl, 0)), 0.0)
    c['dret'] = dret.astype(np.float32)
    rtab = np.zeros((128, 24), np.float64)
    for h in range(4):
        rtab[:, h] = np.exp(lg[h] * (i + 1))
        rtab[:, 4 + h] = np.exp(lg[h] * (127 - i))
        rtab[:, 8 + h] = np.exp(lg[h] * np.maximum(15 - i, 0))
        rtab[:, 12 + h] = np.exp(lg[h] * np.maximum(119 - i, 0))
    for cc in range(2):
        for hh in range(2):
            rtab[hh * 64:(hh + 1) * 64, 16 + cc] = np.exp(lg[2 * cc + hh] * 128)
            rtab[hh * 64:(hh + 1) * 64, 18 + cc] = np.exp(lg[2 * cc + hh] * 16)
            rtab[hh * 64:(hh + 1) * 64, 20 + cc] = np.exp(lg[2 * cc + hh] * 120)
    c['rtab'] = rtab.astype(np.float32)
    half = 32
    freqs = (10000.0 ** (-np.arange(half, dtype=np.float32) / half)).astype(np.float32)
    ang = np.arange(4096, dtype=np.float32)[:, None] * freqs[None, :]
    cos, sin = np.cos(ang).astype(np.float32), np.sin(ang).astype(np.float32)
    rope = np.zeros((4096, 512), np.float32)
    for h in range(8):
        sc = 1.0 if h < 4 else 0.125
        rope[:, h * 32:(h + 1) * 32] = cos * sc
        rope[:, 256 + h * 32:256 + (h + 1) * 32] = sin * sc
    c['rope'] = rope
    return c


def rep(a, n=128):
    return np.ascontiguousarray(np.broadcast_to(a[None], (n,) + a.shape)).astype(np.float32)


def shared_inputs(inp):
    f = lambda a: np.ascontiguousarray(np.asarray(a, dtype=np.float32))
    sh = host_consts()
    for k in ('ffn1_w_in', 'ffn1_w_out', 'w_in', 'w_out', 'ffn2_w_in', 'ffn2_w_out'):
        sh[k] = f(inp[k])
    sh['w2'] = f(inp['gla_w_gate2'])
    sh['wbg'], sh['wbs'], sh['wbr'] = f(inp['w_branch_gla']), f(inp['w_branch_ssd']), f(inp['w_branch_ret'])
    gains = np.stack([inp['norm_ffn1'][0], inp['norm_mix'][0], inp['norm_ffn2'][0],
                      inp['norm_ffn1'][1], inp['norm_mix'][1], inp['norm_ffn2'][1], inp['norm_final']])
    sh['gains'] = np.ascontiguousarray(np.broadcast_to(f(gains)[:, None, :], (7, 128, D)))
    sh['glab'] = rep(f(inp['gla_b_gate']))
    sh['gnorm'], sh['snorm'], sh['rnorm'] = rep(f(inp['gla_norm'])), rep(f(inp['ssd_norm'])), rep(f(inp['ret_norm']))
    sh['dtb'], sh['alog'], sh['dD'] = rep(f(inp['ssd_dt_bias'])), rep(f(inp['ssd_a_log'])), rep(f(inp['ssd_d']))
    cwt = f(inp['ssd_conv_w'])
    sh['cw'] = np.ascontiguousarray(cwt.reshape(2, 4, 6, 128).transpose(3, 0, 2, 1))
    sh['cb'] = np.ascontiguousarray(f(inp['ssd_conv_b']).reshape(2, 6, 128).transpose(2, 0, 1))
    return sh


_PROG_CACHE = {}


def run(inp, n_cores, NPS, NSS):
    f = lambda a: np.ascontiguousarray(np.asarray(a, dtype=np.float32))
    T = inp['x_prompt'].shape[1]
    key = (NPS, T, NSS)
    prog = Prog(NPS, T, NSS)
    nc = prog.build()
    sh = shared_inputs(inp)
    in_maps = []
    for i in range(n_cores):
        m = dict(sh)
        m['xp'] = f(inp['x_prompt'][i * NPS:(i + 1) * NPS])
        m['xs'] = f(inp['x_sample'][i * NSS:(i + 1) * NSS])
        m['sgla'] = f(inp['state_gla'][:, i * NSS:(i + 1) * NSS]).reshape(2, NSS, 256, 128)
        m['sssd'] = f(inp['state_ssd'][:, i * NSS:(i + 1) * NSS]).reshape(2, NSS, 512, 64)
        m['cconv'] = f(inp['cache_conv'][:, i * NSS:(i + 1) * NSS])
        m['sret'] = f(inp['state_ret'][:, i * NSS:(i + 1) * NSS]).reshape(2, NSS, 256, 128)
        in_maps.append(m)
    res = run_bass_kernel_spmd(nc, in_maps, core_ids=list(range(n_cores)))
    R = res.results
    cat = lambda k, ax: np.concatenate([np.asarray(r[k]) for r in R], axis=ax)
    Bp, Bs = NPS * n_cores, NSS * n_cores
    out = (
        cat('yp', 0), cat('ys', 0),
        cat('gla_p', 1).reshape(2, Bp, 4, 64, 128), cat('ssd_p', 1).reshape(2, Bp, 8, 64, 64),
        cat('conv_p', 1), cat('ret_p', 1).reshape(2, Bp, 4, 64, 128),
        cat('gla_s', 1).reshape(2, Bs, 4, 64, 128), cat('ssd_s', 1).reshape(2, Bs, 8, 64, 64),
        cat('conv_s', 1), cat('ret_s', 1).reshape(2, Bs, 4, 64, 128),
    )
    return tuple(np.ascontiguousarray(o, dtype=np.float32) for o in out)


def kernel(**inputs):
    return run(inputs, 8, 2, 2)
```
